# Optimizing a Trainium2 kernel written in Bass

```python
import math
import jax, jax.numpy as jnp
from jax import lax
import numpy as np

D_MODEL = 1024
BATCH = 8
SEQ = 4096
DEPTH = 2

GRID_W = 64
N_GROUPS = 4
MIX_WIDTH = D_MODEL
GROUP_W = MIX_WIDTH // N_GROUPS
CONV_K = 31
HY_ORDER = 2
HY_SHORT_K = 3
HY_BANDS = 16
HY_EMB_DIM = 1 + 2 * HY_BANDS
HY_FILTER_W = 64
HY_FAST_DECAY_PCT = 0.3
HY_SLOW_DECAY_PCT = 1.5
HY_DECAY_TARGET = 1e-2
HY_MOD_SHIFT = 0.05
S5_CH = 16
S5_GROUPS = GROUP_W // S5_CH
S5_STATE = 64
S5_DT_MIN = 1e-3
S5_DT_MAX = 1e-1
HEAD_DIM = 64
N_Q_HEADS = GROUP_W // HEAD_DIM
N_KV_HEADS = 2
Q_PER_KV = N_Q_HEADS // N_KV_HEADS
ROPE_AXIS_DIM = HEAD_DIM // 2
ROPE_THETA = 10000.0
Q_BLOCK = 128
N_EXPERTS = 16
EC_CAPACITY_FACTOR = 2
EXPERT_FF = D_MODEL
DN_ALPHA = (2 * DEPTH) ** 0.25
DN_BETA = (8 * DEPTH) ** -0.25
LN_EPS = 1e-5
RMS_EPS = 1e-6
CONV_IN = 2 * GROUP_W
HY_IN = (HY_ORDER + 1) * GROUP_W
S5_IN = GROUP_W
Q_IN = N_Q_HEADS * HEAD_DIM
KV_IN = N_KV_HEADS * HEAD_DIM
IN_WIDTH = CONV_IN + HY_IN + S5_IN + Q_IN + 2 * KV_IN
IN_SPLITS = (CONV_IN, CONV_IN + HY_IN, CONV_IN + HY_IN + S5_IN, CONV_IN + HY_IN + S5_IN + Q_IN, CONV_IN + HY_IN + S5_IN + Q_IN + KV_IN)

kernel_name = "hybrid_parallel_group_encoder"

F32 = jnp.float32


def _layernorm(x, g, b):
    xf = x.astype(F32)
    mu = jnp.mean(xf, axis=-1, keepdims=True)
    var = jnp.mean(jnp.square(xf - mu), axis=-1, keepdims=True)
    return ((xf - mu) * lax.rsqrt(var + LN_EPS) * g.astype(F32) + b.astype(F32)).astype(x.dtype)


def _rms(xf, g):
    return xf * lax.rsqrt(jnp.mean(jnp.square(xf), axis=-1, keepdims=True) + RMS_EPS) * g.astype(F32)


def _depthwise_conv(x, w, b):
    c = x.shape[-1]
    y = lax.conv_general_dilated(x, w[:, None, :].astype(x.dtype), window_strides=(1,), padding="SAME",
                                 dimension_numbers=("NWC", "WIO", "NWC"), feature_group_count=c)
    return y + b.astype(x.dtype)


def _conformer_conv(u, dw_w, dw_b, ln_g, ln_b):
    a, gate = jnp.split(u, 2, axis=-1)
    h = a * jax.nn.sigmoid(gate)
    h = _depthwise_conv(h, dw_w, dw_b)
    h = _layernorm(h, ln_g, ln_b)
    return jax.nn.silu(h)


def _hyena_pos_features(L):
    t = jnp.linspace(0.0, 1.0, L, dtype=F32)[:, None]
    w = (2.0 * math.pi / L) * jnp.arange(L, dtype=F32)
    f = jnp.linspace(1e-4, HY_BANDS - 1, HY_BANDS, dtype=F32)
    ang = w[:, None] * f[None, :]
    z = jnp.concatenate([t, jnp.cos(ang), -jnp.sin(ang)], axis=-1)
    max_decay = math.log(HY_DECAY_TARGET) / HY_FAST_DECAY_PCT
    min_decay = math.log(HY_DECAY_TARGET) / HY_SLOW_DECAY_PCT
    deltas = jnp.linspace(min_decay, max_decay, GROUP_W, dtype=F32)
    window = jnp.exp(-t * jnp.abs(deltas)[None, :]) + HY_MOD_SHIFT
    return z, window


def _hyena_filter_spectra(z, window, w1, b1, freq, w2, b2, w3):
    L = z.shape[0]
    fr = freq.astype(F32)
    hid = jnp.sin(fr * (z @ w1.astype(F32) + b1.astype(F32)))
    hid = jnp.sin(fr * (hid @ w2.astype(F32) + b2.astype(F32)))
    h = (hid @ w3.astype(F32)).reshape(L, HY_ORDER, 2, GROUP_W) * window[:, None, None, :]
    fwd, bwd = h[:, :, 0], h[:, :, 1]
    k2 = jnp.concatenate([fwd, jnp.zeros((1, HY_ORDER, GROUP_W), F32), bwd[:0:-1]], axis=0)
    return jnp.fft.rfft(k2, axis=0)


def _hyena(u, short_w, short_b, spectra, bias):
    L = u.shape[1]
    uc = _depthwise_conv(u, short_w, short_b)
    v, x1, x2 = jnp.split(uc, HY_ORDER + 1, axis=-1)
    z = v.astype(F32)
    for o, gate in enumerate((x1, x2)):
        zf = jnp.fft.rfft(z, n=2 * L, axis=1)
        conv = jnp.fft.irfft(zf * spectra[None, :, o, :], n=2 * L, axis=1)[:, :L]
        z = gate.astype(F32) * (conv + z * bias[o].astype(F32))
    return z.astype(u.dtype)


def _complex_affine_combine(e1, e2):
    a1r, a1i, b1r, b1i = e1
    a2r, a2i, b2r, b2i = e2
    return (a2r * a1r - a2i * a1i, a2r * a1i + a2i * a1r,
            a2r * b1r - a2i * b1i + b2r, a2r * b1i + a2i * b1r + b2i)


def _s5_direction(uf, lam_re, lam_im, log_step, b_re, b_im, c_re, c_im, reverse):
    lam_re = jnp.minimum(lam_re.astype(F32), -1e-4)
    lam_im = lam_im.astype(F32)
    b_re, b_im, c_re, c_im = (a.astype(F32) for a in (b_re, b_im, c_re, c_im))
    step = jnp.exp(log_step.astype(F32))[:, None]
    mag = jnp.exp(lam_re * step)
    a_re = mag * jnp.cos(lam_im * step)
    a_im = mag * jnp.sin(lam_im * step)
    den = lam_re * lam_re + lam_im * lam_im
    n_re = a_re - 1.0
    z_re = (n_re * lam_re + a_im * lam_im) / den
    z_im = (a_im * lam_re - n_re * lam_im) / den
    bb_re = z_re[..., None] * b_re - z_im[..., None] * b_im
    bb_im = z_re[..., None] * b_im + z_im[..., None] * b_re
    bu_re = jnp.einsum('blgh,gph->blgp', uf, bb_re)
    bu_im = jnp.einsum('blgh,gph->blgp', uf, bb_im)
    shape = bu_re.shape
    elems = (jnp.broadcast_to(a_re, shape), jnp.broadcast_to(a_im, shape), bu_re, bu_im)
    _, _, s_re, s_im = lax.associative_scan(_complex_affine_combine, elems, reverse=reverse, axis=1)
    return jnp.einsum('blgp,ghp->blgh', s_re, c_re) - jnp.einsum('blgp,ghp->blgh', s_im, c_im)


def _s5(u, lam_re, lam_im, log_step, b_re, b_im, c_re, c_im, d, w_glu, b_glu):
    bsz, L, _ = u.shape
    uf = u.astype(F32).reshape(bsz, L, S5_GROUPS, S5_CH)
    y = d.astype(F32).reshape(S5_GROUPS, S5_CH) * uf
    for direction in range(2):
        y = y + _s5_direction(uf, lam_re[direction], lam_im[direction], log_step[direction],
                              b_re[direction], b_im[direction], c_re[direction], c_im[direction],
                              reverse=(direction == 1))
    g = jax.nn.gelu(y.reshape(bsz, L, GROUP_W)).astype(u.dtype)
    a, gate = jnp.split(g @ w_glu + b_glu, 2, axis=-1)
    return a * jax.nn.sigmoid(gate)


def _axial_rope_tables(L):
    rows = L // GRID_W
    row = jnp.repeat(jnp.arange(rows, dtype=F32), GRID_W)
    col = jnp.tile(jnp.arange(GRID_W, dtype=F32), rows)
    inv = ROPE_THETA ** (-jnp.arange(0, ROPE_AXIS_DIM, 2, dtype=F32) / ROPE_AXIS_DIM)
    ang_r = (row[:, None] * inv)[:, None, :]
    ang_c = (col[:, None] * inv)[:, None, :]
    return (jnp.cos(ang_r), jnp.sin(ang_r), jnp.cos(ang_c), jnp.sin(ang_c))


def _rotate(x, cos, sin):
    x1, x2 = jnp.split(x, 2, axis=-1)
    return jnp.concatenate([x1 * cos - x2 * sin, x1 * sin + x2 * cos], axis=-1)


def _apply_axial_rope(x, tabs):
    cr, sr, cc, sc = tabs
    return jnp.concatenate([_rotate(x[..., :ROPE_AXIS_DIM], cr, sr),
                            _rotate(x[..., ROPE_AXIS_DIM:], cc, sc)], axis=-1)


def _gqa(uq, uk, uv, q_g, k_g, tabs):
    bsz, L, _ = uq.shape
    q = uq.reshape(bsz, L, N_Q_HEADS, HEAD_DIM).astype(F32)
    k = uk.reshape(bsz, L, N_KV_HEADS, HEAD_DIM).astype(F32)
    v = uv.reshape(bsz, L, N_KV_HEADS, HEAD_DIM)
    q = _apply_axial_rope(_rms(q, q_g), tabs) * (HEAD_DIM ** -0.5)
    k = _apply_axial_rope(_rms(k, k_g), tabs)
    nb = L // Q_BLOCK
    qb = q.reshape(bsz, nb, Q_BLOCK, N_KV_HEADS, Q_PER_KV, HEAD_DIM).transpose(1, 0, 2, 3, 4, 5)

    def attend(q_blk):
        s = jnp.einsum('bqkgd,bskd->bkgqs', q_blk, k)
        p = jax.nn.softmax(s, axis=-1).astype(v.dtype)
        return jnp.einsum('bkgqs,bskd->bqkgd', p, v)

    o = lax.map(attend, qb)
    return o.transpose(1, 0, 2, 3, 4, 5).reshape(bsz, L, Q_IN)


def _expert_choice_moe(h, router_w, router_b, w_gate, w_up, w_down):
    bsz, L, D = h.shape
    cap = EC_CAPACITY_FACTOR * L // N_EXPERTS
    aff = jax.nn.softmax((h @ router_w + router_b).astype(F32), axis=-1)
    gates, idx = lax.top_k(aff.transpose(0, 2, 1), cap)
    xs = jax.vmap(lambda hb, ib: hb[ib])(h, idx)
    hid = jax.nn.silu(jnp.einsum('becd,edf->becf', xs, w_gate)) * jnp.einsum('becd,edf->becf', xs, w_up)
    y = jnp.einsum('becf,efd->becd', hid, w_down) * gates[..., None].astype(h.dtype)
    return jax.vmap(lambda yb, ib: jnp.zeros((L, D), yb.dtype).at[ib.reshape(-1)].add(yb.reshape(-1, D)))(y, idx)


def setup_inputs(seed: int = 0) -> dict:
    key = jax.random.key(seed)
    ks = iter(jax.random.split(key, 48))

    def nrm(shape, scale):
        return scale * jax.random.normal(next(ks), shape, F32)

    def gain(shape):
        return 1.0 + nrm(shape, 0.02)

    x = nrm((BATCH, SEQ, D_MODEL), 1.0)
    ln_in_g = gain((D_MODEL,))
    ln_in_b = nrm((D_MODEL,), 0.02)
    w_in = nrm((DEPTH, D_MODEL, IN_WIDTH), D_MODEL ** -0.5)
    w_in = w_in.at[:, :, IN_WIDTH - KV_IN:].multiply(DN_BETA)
    conv_dw_w = nrm((DEPTH, CONV_K, GROUP_W), CONV_K ** -0.5)
    conv_dw_b = nrm((DEPTH, GROUP_W), 0.02)
    conv_ln_g = gain((DEPTH, GROUP_W))
    conv_ln_b = nrm((DEPTH, GROUP_W), 0.02)
    hy_short_w = nrm((DEPTH, HY_SHORT_K, HY_IN), HY_SHORT_K ** -0.5)
    hy_short_b = nrm((DEPTH, HY_IN), 0.02)
    hy_f_w1 = nrm((DEPTH, HY_EMB_DIM, HY_FILTER_W), HY_EMB_DIM ** -0.5)
    hy_f_b1 = nrm((DEPTH, HY_FILTER_W), 0.1)
    hy_f_freq = gain((DEPTH, HY_FILTER_W))
    hy_f_w2 = nrm((DEPTH, HY_FILTER_W, HY_FILTER_W), HY_FILTER_W ** -0.5)
    hy_f_b2 = nrm((DEPTH, HY_FILTER_W), 0.1)
    hy_f_w3 = nrm((DEPTH, HY_FILTER_W, HY_ORDER * 2 * GROUP_W), 0.1 * HY_FILTER_W ** -0.5)
    hy_bias = nrm((DEPTH, HY_ORDER, GROUP_W), 1.0)
    s5_lam_re = -0.5 + nrm((DEPTH, 2, S5_GROUPS, S5_STATE), 0.01)
    s5_lam_im = math.pi * jnp.arange(S5_STATE, dtype=F32) + nrm((DEPTH, 2, S5_GROUPS, S5_STATE), 0.01)
    s5_log_step = jax.random.uniform(next(ks), (DEPTH, 2, S5_GROUPS), F32, math.log(S5_DT_MIN), math.log(S5_DT_MAX))
    s5_b_re = nrm((DEPTH, 2, S5_GROUPS, S5_STATE, S5_CH), (2 * S5_CH) ** -0.5)
    s5_b_im = nrm((DEPTH, 2, S5_GROUPS, S5_STATE, S5_CH), (2 * S5_CH) ** -0.5)
    s5_c_re = nrm((DEPTH, 2, S5_GROUPS, S5_CH, S5_STATE), 0.5)
    s5_c_im = nrm((DEPTH, 2, S5_GROUPS, S5_CH, S5_STATE), 0.5)
    s5_d = nrm((DEPTH, GROUP_W), 1.0)
    s5_w_glu = nrm((DEPTH, GROUP_W, 2 * GROUP_W), GROUP_W ** -0.5)
    s5_b_glu = nrm((DEPTH, 2 * GROUP_W), 0.02)
    q_norm_g = gain((DEPTH, HEAD_DIM))
    k_norm_g = gain((DEPTH, HEAD_DIM))
    mix_norm_g = gain((DEPTH, MIX_WIDTH))
    w_out = nrm((DEPTH, MIX_WIDTH, D_MODEL), DN_BETA * MIX_WIDTH ** -0.5)
    ln1_g = gain((DEPTH, D_MODEL))
    ln1_b = nrm((DEPTH, D_MODEL), 0.02)
    router_w = nrm((DEPTH, D_MODEL, N_EXPERTS), D_MODEL ** -0.5)
    router_b = nrm((DEPTH, N_EXPERTS), 0.01)
    exp_w_gate = nrm((DEPTH, N_EXPERTS, D_MODEL, EXPERT_FF), D_MODEL ** -0.5)
    exp_w_up = nrm((DEPTH, N_EXPERTS, D_MODEL, EXPERT_FF), D_MODEL ** -0.5)
    exp_w_down = nrm((DEPTH, N_EXPERTS, EXPERT_FF, D_MODEL), DN_BETA * EXPERT_FF ** -0.5)
    ln2_g = gain((DEPTH, D_MODEL))
    ln2_b = nrm((DEPTH, D_MODEL), 0.02)
    return {"x": x, "ln_in_g": ln_in_g, "ln_in_b": ln_in_b, "w_in": w_in,
            "conv_dw_w": conv_dw_w, "conv_dw_b": conv_dw_b, "conv_ln_g": conv_ln_g, "conv_ln_b": conv_ln_b,
            "hy_short_w": hy_short_w, "hy_short_b": hy_short_b, "hy_f_w1": hy_f_w1, "hy_f_b1": hy_f_b1,
            "hy_f_freq": hy_f_freq, "hy_f_w2": hy_f_w2, "hy_f_b2": hy_f_b2, "hy_f_w3": hy_f_w3, "hy_bias": hy_bias,
            "s5_lam_re": s5_lam_re, "s5_lam_im": s5_lam_im, "s5_log_step": s5_log_step,
            "s5_b_re": s5_b_re, "s5_b_im": s5_b_im, "s5_c_re": s5_c_re, "s5_c_im": s5_c_im,
            "s5_d": s5_d, "s5_w_glu": s5_w_glu, "s5_b_glu": s5_b_glu,
            "q_norm_g": q_norm_g, "k_norm_g": k_norm_g, "mix_norm_g": mix_norm_g, "w_out": w_out,
            "ln1_g": ln1_g, "ln1_b": ln1_b, "router_w": router_w, "router_b": router_b,
            "exp_w_gate": exp_w_gate, "exp_w_up": exp_w_up, "exp_w_down": exp_w_down,
            "ln2_g": ln2_g, "ln2_b": ln2_b}


def reference(x, ln_in_g, ln_in_b, w_in, conv_dw_w, conv_dw_b, conv_ln_g, conv_ln_b,
              hy_short_w, hy_short_b, hy_f_w1, hy_f_b1, hy_f_freq, hy_f_w2, hy_f_b2, hy_f_w3, hy_bias,
              s5_lam_re, s5_lam_im, s5_log_step, s5_b_re, s5_b_im, s5_c_re, s5_c_im, s5_d, s5_w_glu, s5_b_glu,
              q_norm_g, k_norm_g, mix_norm_g, w_out, ln1_g, ln1_b, router_w, router_b,
              exp_w_gate, exp_w_up, exp_w_down, ln2_g, ln2_b):
    bsz, L, _ = x.shape
    rope_tabs = _axial_rope_tables(L)
    hy_z, hy_window = _hyena_pos_features(L)
    h = _layernorm(x, ln_in_g, ln_in_b)
    for l in range(DEPTH):
        proj = h @ w_in[l]
        u_conv, u_hy, u_s5, u_q, u_k, u_v = jnp.split(proj, IN_SPLITS, axis=-1)
        out_a = _conformer_conv(u_conv, conv_dw_w[l], conv_dw_b[l], conv_ln_g[l], conv_ln_b[l])
        spectra = _hyena_filter_spectra(hy_z, hy_window, hy_f_w1[l], hy_f_b1[l], hy_f_freq[l],
                                        hy_f_w2[l], hy_f_b2[l], hy_f_w3[l])
        out_b = _hyena(u_hy, hy_short_w[l], hy_short_b[l], spectra, hy_bias[l])
        out_c = _s5(u_s5, s5_lam_re[l], s5_lam_im[l], s5_log_step[l], s5_b_re[l], s5_b_im[l],
                    s5_c_re[l], s5_c_im[l], s5_d[l], s5_w_glu[l], s5_b_glu[l])
        out_d = _gqa(u_q, u_k, u_v, q_norm_g[l], k_norm_g[l], rope_tabs)
        groups = jnp.stack([out_a.astype(F32), out_b.astype(F32), out_c.astype(F32), out_d.astype(F32)], axis=2)
        merged = _rms(groups, mix_norm_g[l].reshape(N_GROUPS, GROUP_W)).reshape(bsz, L, MIX_WIDTH).astype(h.dtype)
        h = _layernorm(DN_ALPHA * h + merged @ w_out[l], ln1_g[l], ln1_b[l])
        moe = _expert_choice_moe(h, router_w[l], router_b[l], exp_w_gate[l], exp_w_up[l], exp_w_down[l])
        h = _layernorm(DN_ALPHA * h + moe, ln2_g[l], ln2_b[l])
    return h
```

```python
import math
import os
from contextlib import ExitStack

import numpy as np
import ml_dtypes

import concourse.bass as bass
import concourse.mybir as mybir
from concourse.bass_utils import run_bass_kernel_spmd

F32 = mybir.dt.float32
BF16 = mybir.dt.bfloat16
I32 = mybir.dt.int32
U32 = mybir.dt.uint32
AF = mybir.ActivationFunctionType
ALU = mybir.AluOpType
AX = mybir.AxisListType

L = 4096
D = 1024
NT = L // 128
DEPTH = 2
DN_ALPHA = (2 * DEPTH) ** 0.25
SEM_EPOCH = 12000
DMA_EPOCH = 1500


class Buf:
    def __init__(self, name, ap=None):
        self.name = name
        self.ap = ap
        self.last_w = None
        self.readers = {}
        self.dsem = None
        self.dcnt = 0
        self.ddone = 0
        self.psum = False

    def __getitem__(self, idx):
        return self.ap[idx]


class Prog:
    def __init__(self, nc, stack):
        self.nc = nc
        self.stack = stack
        self.eng = {"pe": nc.tensor, "dve": nc.vector, "act": nc.scalar,
                    "pool": nc.gpsimd, "sp": nc.sync}
        self.sem = {}
        self.cnt = {}
        self.nsem = 0
        self.sem_owner = {}
        for e in self.eng:
            self._new_eng_sem(e)
        self.seen = {e: {} for e in self.eng}
        self.all_dma_bufs = []
        self.final = {}
        self.old_dma = []
        self.ninstr = {e: 0 for e in self.eng}
        self.free_dsems = []

    def _alloc_sem(self, name):
        self.nsem += 1
        return self.stack.enter_context(self.nc.semaphore(f"{name}_{self.nsem}"))

    def _new_eng_sem(self, e):
        self.sem[e] = self._alloc_sem("e" + e)
        self.cnt[e] = 0
        self.sem_owner[id(self.sem[e])] = e

    def _wait(self, e, ev):
        if ev is None:
            return
        s, v = ev
        k = id(s)
        if self.seen[e].get(k, 0) >= v:
            return
        self.seen[e][k] = v
        self.eng[e].wait_ge(s, v)

    def _cur(self, ev):
        if ev is None:
            return None
        s, v, owner = ev
        if owner is not None:
            if owner.dsem is s:
                return (s, owner.dcnt)
            return (s, self.final.get(id(s), v))
        return (s, v)

    def _deps(self, e, reads, writes, waw=True):
        for b in reads:
            self._wait(e, self._cur(b.last_w))
            if b.psum:
                for r in list(b.readers.values()):
                    if self.sem_owner.get(id(r[0])) != e:
                        self._wait(e, self._cur(r))
        for b in writes:
            lst = list(b.readers.values())
            if waw:
                lst.append(b.last_w)
            for r in lst:
                if r is not None and self.sem_owner.get(id(r[0])) == e:
                    continue
                self._wait(e, self._cur(r))

    def _record(self, ev, reads, writes):
        for b in writes:
            b.last_w = ev
            b.readers = {}
        for b in reads:
            if b not in writes:
                b.readers[id(ev[0])] = ev

    def op(self, e, fn, reads=(), writes=()):
        if self.cnt[e] >= SEM_EPOCH:
            self._new_eng_sem(e)
        self._deps(e, reads, writes)
        ins = fn(self.eng[e])
        self.cnt[e] += 1
        ins.then_inc(self.sem[e], 1)
        self._record((self.sem[e], self.cnt[e], None), reads, writes)
        self.ninstr[e] += 1
        return ins

    def dma(self, q, fn, reads, writes, waw=True):
        self._deps(q, reads, writes, waw=waw)
        owner = writes[0]
        if owner.dsem is None or owner.ddone >= DMA_EPOCH:
            if owner.dsem is not None:
                self.final[id(owner.dsem)] = owner.dcnt
                self.old_dma.append((owner.dsem, owner.dcnt))
            if self.free_dsems:
                owner.dsem, owner.dcnt = self.free_dsems.pop()
            else:
                owner.dsem = self._alloc_sem("d")
                owner.dcnt = 0
            owner.ddone = 0
            self.all_dma_bufs.append(owner)
        ins = fn(self.eng[q])
        owner.dcnt += 16
        owner.ddone += 1
        ins.then_inc(owner.dsem, 16)
        self._record((owner.dsem, owner.dcnt, owner), reads, writes)
        self.ninstr[q] += 1
        return ins

    def release(self, buf):
        if buf.dsem is not None:
            self.final[id(buf.dsem)] = buf.dcnt
            if buf.dcnt < 30000:
                self.free_dsems.append((buf.dsem, buf.dcnt))
            else:
                self.old_dma.append((buf.dsem, buf.dcnt))
            if buf in self.all_dma_bufs:
                self.all_dma_bufs = [b for b in self.all_dma_bufs if b is not buf]
            buf.dsem = None

    def barrier(self, engines=None):
        evs = []
        for e in self.eng:
            if self.cnt[e] > 0:
                evs.append((self.sem[e], self.cnt[e]))
        for b in self.all_dma_bufs:
            if b.dsem is not None and b.dcnt > 0:
                evs.append((b.dsem, b.dcnt))
        evs.extend(self.old_dma)
        evs.extend(self.free_dsems)
        for e in (engines or self.eng):
            for ev in evs:
                self._wait(e, ev)


class KB:
    def __init__(self, nc, st, P):
        self.nc = nc
        self.st = st
        self.P = P
        self.uid = 0
        self.psum = [Buf(f"ps{i}", st.enter_context(nc.psum_tensor(f"ps{i}", [128, 512], F32))) for i in range(8)]
        for b in self.psum:
            b.psum = True
        self.rr = 0

    def sb(self, stack, name, shape, dt=F32):
        self.uid += 1
        b = Buf(name, stack.enter_context(self.nc.sbuf_tensor(f"{name}_{self.uid}", shape, dt)))
        stack.callback(lambda: self.P.release(b))
        return b

    def dram_in(self, name, shape, dt=F32):
        return Buf(name, self.nc.dram_tensor(name, shape, dt, kind="ExternalInput"))

    def dram(self, name, shape, dt=F32, kind="Internal"):
        return Buf(name, self.nc.dram_tensor(name, shape, dt, kind=kind))

    def alt(self):
        self.rr += 1
        return "dve" if self.rr % 2 else "act"


def ld(P, dst, dst_ap, src, src_ap, q="sp"):
    P.dma(q, lambda e: e.dma_start(out=dst_ap, in_=src_ap), [src], [dst], waw=False)


def copy_on(P, eng, out_b, out_ap, in_b, in_ap):
    if eng == "pool_via_dve":
        eng = "dve"
    if eng == "act":
        P.op("act", lambda e: e.copy(out=out_ap, in_=in_ap), [in_b], [out_b])
    else:
        P.op(eng, lambda e: e.tensor_copy(out=out_ap, in_=in_ap), [in_b], [out_b])


def rstd_from(P, kb, var_b, var_ap, eps, tmp_b, tmp_ap, out_b, out_ap):
    P.op("dve", lambda e: e.tensor_scalar(out=tmp_ap, in0=var_ap, scalar1=float(eps), scalar2=None, op0=ALU.add), [var_b], [tmp_b])
    P.op("act", lambda e: e.activation(out=tmp_ap, in_=tmp_ap, func=AF.Sqrt), [tmp_b], [tmp_b])
    P.op("dve", lambda e: e.reciprocal(out=out_ap, in_=tmp_ap), [tmp_b], [out_b])


def pipeline(stages, n):
    ns = len(stages)
    for step in range(n + ns - 1):
        for k, fn in enumerate(stages):
            it = step - k
            if 0 <= it < n:
                fn(it)


def ln_a(P, X, scr):
    st6, mv, tmp, rstd = scr
    for c in range(2):
        P.op("dve", lambda e, c=c: e.bn_stats(out=st6[:, c * 6:(c + 1) * 6], in_=X[:, c * 512:(c + 1) * 512]), [X], [st6])
    P.op("dve", lambda e: e.bn_aggr(out=mv[:, :], in_=st6[:, :]), [st6], [mv])
    P.op("dve", lambda e: e.tensor_scalar(out=tmp[:, :], in0=mv[:, 1:2], scalar1=1e-5, scalar2=None, op0=ALU.add), [mv], [tmp])
    P.op("act", lambda e: e.activation(out=tmp[:, :], in_=tmp[:, :], func=AF.Sqrt), [tmp], [tmp])


def ln_b(P, X, G, Bt, scr):
    st6, mv, tmp, rstd = scr
    P.op("dve", lambda e: e.reciprocal(out=rstd[:, :], in_=tmp[:, :]), [tmp], [rstd])
    P.op("dve", lambda e: e.tensor_scalar(out=X[:, :], in0=X[:, :], scalar1=mv[:, 0:1], scalar2=rstd[:, 0:1],
                                          op0=ALU.subtract, op1=ALU.mult), [X, mv, rstd], [X])
    P.op("pool", lambda e: e.tensor_tensor(out=X[:, :], in0=X[:, :], in1=G[:, :], op=ALU.mult), [X, G], [X])
    P.op("pool", lambda e: e.tensor_tensor(out=X[:, :], in0=X[:, :], in1=Bt[:, :], op=ALU.add), [X, Bt], [X])


def ln_scr(kb, ph, n):
    return [(kb.sb(ph, f"st6{i}", [128, 12]), kb.sb(ph, f"mv{i}", [128, 2]), kb.sb(ph, f"tmp{i}", [128, 1]), kb.sb(ph, f"rstd{i}", [128, 1])) for i in range(n)]


def ln_tok_tile(P, kb, X, G, Bt, scr):
    st6, mv, tmp, rstd = scr
    for c in range(2):
        P.op("dve", lambda e, c=c: e.bn_stats(out=st6[:, c * 6:(c + 1) * 6], in_=X[:, c * 512:(c + 1) * 512]), [X], [st6])
    P.op("dve", lambda e: e.bn_aggr(out=mv[:, :], in_=st6[:, :]), [st6], [mv])
    rstd_from(P, kb, mv, mv[:, 1:2], 1e-5, tmp, tmp[:, :], rstd, rstd[:, :])
    P.op("dve", lambda e: e.tensor_scalar(out=X[:, :], in0=X[:, :], scalar1=mv[:, 0:1], scalar2=rstd[:, 0:1],
                                          op0=ALU.subtract, op1=ALU.mult), [X, mv, rstd], [X])
    P.op("pool", lambda e: e.tensor_tensor(out=X[:, :], in0=X[:, :], in1=G[:, :], op=ALU.mult), [X, G], [X])
    P.op("dve", lambda e: e.tensor_tensor(out=X[:, :], in0=X[:, :], in1=Bt[:, :], op=ALU.add), [X, Bt], [X])


def transpose_to_hT(P, kb, X, ident, stage, hT_d, ti):
    for half in range(2):
        ps = kb.psum[(2 * ti + half) % 4]
        for j in range(4):
            k = half * 4 + j
            P.op("pe", lambda e, k=k, j=j, ps=ps: e.transpose(out=ps[:, j * 128:(j + 1) * 128], in_=X[:, k * 128:(k + 1) * 128], identity=ident[:, :]),
                 [X, ident], [ps])
        copy_on(P, kb.alt(), stage, stage[:, half * 4:(half + 1) * 4, :], ps, ps[:, :].rearrange("p (j t) -> p j t", j=4))
    P.dma("sp", lambda e: e.dma_start(out=hT_d.ap.ap().rearrange("(k p) t -> p k t", p=128)[:, :, ti * 128:(ti + 1) * 128], in_=stage[:, :, :]),
          [stage], [hT_d], waw=False)


def load_cols(P, kb, ph, name, src, src_ap_1d, n):
    T = kb.sb(ph, name, [128, n])
    P.dma("sp", lambda e: e.dma_start(out=T[:, :], in_=src_ap_1d.rearrange("(j p) -> p j", p=128), allow_slow_non_contiguous=True), [src], [T], waw=False)
    return T


def group_rms(C, ph_scr, Ys, gain, gcol0, row0, tc):
    P, kb = C["P"], C["kb"]
    SQ, R, OB = ph_scr
    ps = kb.psum[6]
    for c, (yb, yap) in enumerate(Ys):
        P.op("act", lambda e, c=c, yap=yap: e.activation(out=SQ[:, c, :], in_=yap, func=AF.Square), [yb], [SQ])
    for c in range(2):
        P.op("pe", lambda e, c=c: e.matmul(ps[:, :], lhsT=C["ones256"][:, :], rhs=SQ[:, c, :], start=(c == 0), stop=(c == 1)), [C["ones256"], SQ], [ps])
    P.op("dve", lambda e: e.tensor_scalar(out=R[:, :], in0=ps[:, :], scalar1=1e-6, scalar2=None, op0=ALU.add), [ps], [R])
    P.op("act", lambda e: e.activation(out=R[:, :], in_=R[:, :], func=AF.Sqrt), [R], [R])
    P.op("dve", lambda e: e.reciprocal(out=R[:, :], in_=R[:, :]), [R], [R])
    for c, (yb, yap) in enumerate(Ys):
        P.op("dve", lambda e, c=c, yap=yap: e.scalar_tensor_tensor(out=OB[:, c, :], in0=yap, scalar=gain[:, gcol0 + c:gcol0 + c + 1], in1=R[:, :], op0=ALU.mult, op1=ALU.mult), [yb, gain, R], [OB])
    P.dma("sp", lambda e: e.dma_start(out=C["mergedT"].ap.ap()[row0:row0 + 256, :].rearrange("(c p) t -> p c t", p=128)[:, :, tc * 512:(tc + 1) * 512], in_=OB[:, :, :]), [OB], [C["mergedT"]], waw=False)


def rms_scratch(kb, ph):
    return (kb.sb(ph, "rSQ", [128, 2, 512]), kb.sb(ph, "rR", [128, 512]), kb.sb(ph, "rOB", [128, 2, 512], BF16))


def phase_conv(C, l):
    P, kb, EXT, projT = C["P"], C["kb"], C["EXT"], C["projT"]
    with ExitStack() as ph:
        gain = load_cols(P, kb, ph, "mixg", EXT["mix_norm_g"], EXT["mix_norm_g"].ap[l, :], 8)
        lng = load_cols(P, kb, ph, "clng", EXT["conv_ln_g"], EXT["conv_ln_g"].ap[l, :], 2)
        lnb = load_cols(P, kb, ph, "clnb", EXT["conv_ln_b"], EXT["conv_ln_b"].ap[l, :], 2)
        dwb = load_cols(P, kb, ph, "cdwb", EXT["conv_dw_b"], EXT["conv_dw_b"].ap[l, :], 2)
        dww = kb.sb(ph, "cdww", [128, 2, 31])
        for c in range(2):
            P.dma("sp", lambda e, c=c: e.dma_start(out=dww[:, c, :], in_=EXT["conv_dw_w"].ap[l, :, c * 128:(c + 1) * 128].rearrange("k p -> p k"), allow_slow_non_contiguous=True), [EXT["conv_dw_w"]], [dww], waw=False)
        A = kb.sb(ph, "cA", [128, L]); Gt = kb.sb(ph, "cG", [128, L]); HP = kb.sb(ph, "cHP", [128, L + 30])
        CV = [kb.sb(ph, f"cCV{c}", [128, L]) for c in range(2)]
        P.op("pool", lambda e: e.memset(HP[:, :], 0.0), [], [HP])
        for c in range(2):
            ld(P, A, A[:, :], projT, projT.ap[c * 128:(c + 1) * 128, :])
            ld(P, Gt, Gt[:, :], projT, projT.ap[256 + c * 128:256 + (c + 1) * 128, :])
            P.op("act", lambda e: e.activation(out=Gt[:, :], in_=Gt[:, :], func=AF.Sigmoid), [Gt], [Gt])
            P.op("dve", lambda e: e.tensor_tensor(out=HP[:, 15:15 + L], in0=A[:, :], in1=Gt[:, :], op=ALU.mult), [A, Gt], [HP])
            cv = CV[c]
            P.op("dve", lambda e, c=c, cv=cv: e.tensor_scalar(out=cv[:, :], in0=HP[:, 0:L], scalar1=dww[:, c, 0:1], scalar2=dwb[:, c:c + 1], op0=ALU.mult, op1=ALU.add), [HP, dww, dwb], [cv])
            for k in range(1, 31):
                P.op("dve", lambda e, c=c, cv=cv, k=k: e.scalar_tensor_tensor(out=cv[:, :], in0=HP[:, k:k + L], scalar=dww[:, c, k:k + 1], in1=cv[:, :], op0=ALU.mult, op1=ALU.add), [HP, dww, cv], [cv])
        ones = C["ones256"]
        SQs = [kb.sb(ph, f"cSQ{i}", [128, 2, 512]) for i in range(2)]
        MEANs = [kb.sb(ph, f"cMEAN{i}", [128, 512]) for i in range(3)]; VARs = [kb.sb(ph, f"cVAR{i}", [128, 512]) for i in range(3)]
        Tts = [kb.sb(ph, f"cT{i}", [128, 2, 512]) for i in range(2)]; SLs = [kb.sb(ph, f"cSL{i}", [128, 2, 512]) for i in range(2)]
        scrs = [rms_scratch(kb, ph) for _ in range(2)]

        def c0(tc):
            sl = slice(tc * 512, (tc + 1) * 512)
            pm, pe_ = kb.psum[0 + 2 * (tc % 2)], kb.psum[1 + 2 * (tc % 2)]
            SQ = SQs[tc % 2]
            for c in range(2):
                P.op("act", lambda e, c=c: e.activation(out=SQ[:, c, :], in_=CV[c][:, sl], func=AF.Square), [CV[c]], [SQ])
            for c in range(2):
                P.op("pe", lambda e, c=c: e.matmul(pm[:, :], lhsT=ones[:, :], rhs=CV[c][:, sl], start=(c == 0), stop=(c == 1)), [ones, CV[c]], [pm])
            for c in range(2):
                P.op("pe", lambda e, c=c: e.matmul(pe_[:, :], lhsT=ones[:, :], rhs=SQ[:, c, :], start=(c == 0), stop=(c == 1)), [ones, SQ], [pe_])

        def c1(tc):
            pm, pe_ = kb.psum[0 + 2 * (tc % 2)], kb.psum[1 + 2 * (tc % 2)]
            MEAN, VAR = MEANs[tc % 3], VARs[tc % 3]
            P.op("act", lambda e: e.copy(out=MEAN[:, :], in_=pm[:, :]), [pm], [MEAN])
            P.op("act", lambda e: e.activation(out=VAR[:, :], in_=pm[:, :], func=AF.Square), [pm], [VAR])
            P.op("dve", lambda e: e.tensor_tensor(out=VAR[:, :], in0=pe_[:, :], in1=VAR[:, :], op=ALU.subtract), [pe_, VAR], [VAR])
            P.op("dve", lambda e: e.tensor_scalar(out=VAR[:, :], in0=VAR[:, :], scalar1=1e-5, scalar2=None, op0=ALU.add), [VAR], [VAR])
            P.op("act", lambda e: e.activation(out=VAR[:, :], in_=VAR[:, :], func=AF.Sqrt), [VAR], [VAR])

        def c2(tc):
            sl = slice(tc * 512, (tc + 1) * 512)
            MEAN, VAR, Tt, SL = MEANs[tc % 3], VARs[tc % 3], Tts[tc % 2], SLs[tc % 2]
            P.op("dve", lambda e: e.reciprocal(out=VAR[:, :], in_=VAR[:, :]), [VAR], [VAR])
            for c in range(2):
                P.op("pool", lambda e, c=c: e.tensor_tensor(out=Tt[:, c, :], in0=CV[c][:, sl], in1=MEAN[:, :], op=ALU.subtract), [CV[c], MEAN], [Tt])
                P.op("pool", lambda e, c=c: e.tensor_tensor(out=Tt[:, c, :], in0=Tt[:, c, :], in1=VAR[:, :], op=ALU.mult), [Tt, VAR], [Tt])
                P.op("pool", lambda e, c=c: e.tensor_scalar(out=Tt[:, c, :], in0=Tt[:, c, :], scalar1=lng[:, c:c + 1], scalar2=lnb[:, c:c + 1], op0=ALU.mult, op1=ALU.add), [Tt, lng, lnb], [Tt])
                P.op("act", lambda e, c=c: e.activation(out=SL[:, c, :], in_=Tt[:, c, :], func=AF.Silu), [Tt], [SL])

        def c3(tc):
            SL = SLs[tc % 2]
            group_rms(C, scrs[tc % 2], [(SL, SL[:, 0, :]), (SL, SL[:, 1, :])], gain, 0, 0, tc)

        pipeline([c0, c1, c2, c3], 8)
        P.barrier()


def dbg_dump(C, name, buf, ap, rows, cols, r0=0):
    if C.get("dbgname") != name:
        return
    P, kb = C["P"], C["kb"]
    P.barrier()
    with ExitStack() as ph:
        T = kb.sb(ph, "dbgtmp", [128, cols])
        P.op("dve", lambda e: e.tensor_copy(out=T[0:rows, :], in_=ap), [buf], [T])
        P.dma("sp", lambda e: e.dma_start(out=C["dbg_d"].ap[r0:r0 + rows, 0:cols], in_=T[0:rows, :]), [T], [C["dbg_d"]], waw=False)
        P.barrier()


TWO_PI_HI = 6.28125
TWO_PI_LO = 2.0 * math.pi - 6.28125


def sin_scratch(kb, ph, shape, tag):
    return (kb.sb(ph, "sk_i" + tag, shape, I32), kb.sb(ph, "sk_f" + tag, shape), kb.sb(ph, "sk_r" + tag, shape))


def sin_of(P, scr, Xb, Xap, Ob, Oap, phase=0.0, scale=None, scale_b=None):
    KI, KF, Rr = scr
    nd = len(KI.ap.shape)
    full = tuple(slice(None) for _ in range(nd))
    if scale is None:
        P.op("dve", lambda e: e.tensor_scalar(out=Rr[full], in0=Xap, scalar1=float(phase), scalar2=None, op0=ALU.add), [Xb], [Rr])
    else:
        P.op("dve", lambda e: e.tensor_scalar(out=Rr[full], in0=Xap, scalar1=scale, scalar2=float(phase), op0=ALU.mult, op1=ALU.add), [Xb, scale_b], [Rr])
    P.op("dve", lambda e: e.tensor_scalar(out=KI[full], in0=Rr[full], scalar1=float(1.0 / (2.0 * math.pi)), scalar2=None, op0=ALU.mult), [Rr], [KI])
    P.op("dve", lambda e: e.tensor_copy(out=KF[full], in_=KI[full]), [KI], [KF])
    P.op("dve", lambda e: e.scalar_tensor_tensor(out=Rr[full], in0=KF[full], scalar=-TWO_PI_HI, in1=Rr[full], op0=ALU.mult, op1=ALU.add), [KF, Rr], [Rr])
    P.op("dve", lambda e: e.scalar_tensor_tensor(out=Rr[full], in0=KF[full], scalar=-TWO_PI_LO, in1=Rr[full], op0=ALU.mult, op1=ALU.add), [KF, Rr], [Rr])
    P.op("act", lambda e: e.activation(out=Oap, in_=Rr[full], func=AF.Sin), [Rr], [Ob])


def tt(P, eng, ob, oap, ab, aap, bb, bap, op):
    P.op(eng, lambda e: e.tensor_tensor(out=oap, in0=aap, in1=bap, op=op), [ab, bb], [ob])


def phase_s5(C, l):
    P, kb, EXT, projT = C["P"], C["kb"], C["EXT"], C["projT"]
    ident = C["ident"]
    with ExitStack() as ph:
        gain = load_cols(P, kb, ph, "mixg", EXT["mix_norm_g"], EXT["mix_norm_g"].ap[l, :], 8)
        dvec = load_cols(P, kb, ph, "s5d", EXT["s5_d"], EXT["s5_d"].ap[l, :], 2)
        bglu = load_cols(P, kb, ph, "s5bg", EXT["s5_b_glu"], EXT["s5_b_glu"].ap[l, :], 4)
        IOTA = kb.sb(ph, "s5iota", [128, 512])
        ld(P, IOTA, IOTA[:, :], EXT["iota512"], EXT["iota512"].ap.ap())
        U = kb.sb(ph, "s5U", [128, 2, L], BF16)
        YACC = kb.sb(ph, "s5Y", [128, 2, L])
        for cc in range(2):
            P.dma("pool", lambda e, cc=cc: e.dma_start(out=U[:, cc, :], in_=projT.ap[1280 + cc * 128:1280 + (cc + 1) * 128, :]), [projT], [U], waw=False)
            ld(P, YACC, YACC[:, cc, :], projT, projT.ap[1280 + cc * 128:1280 + (cc + 1) * 128, :])
            P.op("dve", lambda e, cc=cc: e.tensor_scalar(out=YACC[:, cc, :], in0=YACC[:, cc, :], scalar1=dvec[:, cc:cc + 1], scalar2=None, op0=ALU.mult), [YACC, dvec], [YACC])
        COSt = kb.sb(ph, "s5cos", [128, 8, 512]); SINt = kb.sb(ph, "s5sin", [128, 8, 512])
        LB = [kb.sb(ph, f"s5LB{i}", [128, 8, 128], BF16) for i in range(2)]
        LC = [kb.sb(ph, f"s5LC{i}", [128, 8, 128], BF16) for i in range(2)]
        MAG = kb.sb(ph, "s5mag", [128, 8]); CT = kb.sb(ph, "s5cT", [128, 8]); ST = kb.sb(ph, "s5sT", [128, 8])
        for d in range(2):
            with ExitStack() as pp:
                LRE = load_cols(P, kb, pp, "lre", EXT["s5_lam_re"], EXT["s5_lam_re"].ap[l, d, :], 8)
                LIM = load_cols(P, kb, pp, "lim", EXT["s5_lam_im"], EXT["s5_lam_im"].ap[l, d, :], 8)
                STEP = kb.sb(pp, "step", [128, 8])
                for g in range(16):
                    j, gl = g // 2, g % 2
                    P.dma("sp", lambda e, g=g, j=j, gl=gl: e.dma_start(out=STEP[gl * 64:(gl + 1) * 64, j:j + 1], in_=EXT["s5_log_step"].ap[l, d, g:g + 1].rearrange("(a b) -> a b", b=1).partition_broadcast(64)), [EXT["s5_log_step"]], [STEP], waw=False)
                P.op("act", lambda e: e.activation(out=STEP[:, :], in_=STEP[:, :], func=AF.Exp), [STEP], [STEP])
                P.op("dve", lambda e: e.tensor_scalar(out=LRE[:, :], in0=LRE[:, :], scalar1=-1e-4, scalar2=None, op0=ALU.min), [LRE], [LRE])
                TH = kb.sb(pp, "th", [128, 8]); X8 = kb.sb(pp, "x8", [128, 8]); CO = kb.sb(pp, "co", [128, 8]); SI = kb.sb(pp, "si", [128, 8])
                ARE = kb.sb(pp, "are", [128, 8]); AIM = kb.sb(pp, "aim", [128, 8]); DEN = kb.sb(pp, "den", [128, 8]); T8 = kb.sb(pp, "t8", [128, 8])
                ZRE = kb.sb(pp, "zre", [128, 8]); ZIM = kb.sb(pp, "zim", [128, 8]); NZIM = kb.sb(pp, "nzim", [128, 8])
                tt(P, "dve", X8, X8[:, :], LRE, LRE[:, :], STEP, STEP[:, :], ALU.mult)
                P.op("act", lambda e: e.activation(out=MAG[:, :], in_=X8[:, :], func=AF.Exp), [X8], [MAG])
                tt(P, "dve", TH, TH[:, :], LIM, LIM[:, :], STEP, STEP[:, :], ALU.mult)
                ss8 = sin_scratch(kb, pp, [128, 8], "a")
                ssb = sin_scratch(kb, pp, [128, 512], "b")
                sin_of(P, ss8, TH, TH[:, :], SI, SI[:, :])
                sin_of(P, ss8, TH, TH[:, :], CO, CO[:, :], phase=math.pi / 2)
                tt(P, "dve", ARE, ARE[:, :], MAG, MAG[:, :], CO, CO[:, :], ALU.mult)
                tt(P, "dve", AIM, AIM[:, :], MAG, MAG[:, :], SI, SI[:, :], ALU.mult)
                tt(P, "dve", DEN, DEN[:, :], LRE, LRE[:, :], LRE, LRE[:, :], ALU.mult)
                tt(P, "dve", T8, T8[:, :], LIM, LIM[:, :], LIM, LIM[:, :], ALU.mult)
                tt(P, "dve", DEN, DEN[:, :], DEN, DEN[:, :], T8, T8[:, :], ALU.add)
                P.op("dve", lambda e: e.reciprocal(out=DEN[:, :], in_=DEN[:, :]), [DEN], [DEN])
                NRE = kb.sb(pp, "nre", [128, 8])
                P.op("dve", lambda e: e.tensor_scalar(out=NRE[:, :], in0=ARE[:, :], scalar1=-1.0, scalar2=None, op0=ALU.add), [ARE], [NRE])
                tt(P, "dve", ZRE, ZRE[:, :], NRE, NRE[:, :], LRE, LRE[:, :], ALU.mult)
                tt(P, "dve", T8, T8[:, :], AIM, AIM[:, :], LIM, LIM[:, :], ALU.mult)
                tt(P, "dve", ZRE, ZRE[:, :], ZRE, ZRE[:, :], T8, T8[:, :], ALU.add)
                tt(P, "dve", ZRE, ZRE[:, :], ZRE, ZRE[:, :], DEN, DEN[:, :], ALU.mult)
                tt(P, "dve", ZIM, ZIM[:, :], AIM, AIM[:, :], LRE, LRE[:, :], ALU.mult)
                tt(P, "dve", T8, T8[:, :], NRE, NRE[:, :], LIM, LIM[:, :], ALU.mult)
                tt(P, "dve", ZIM, ZIM[:, :], ZIM, ZIM[:, :], T8, T8[:, :], ALU.subtract)
                tt(P, "dve", ZIM, ZIM[:, :], ZIM, ZIM[:, :], DEN, DEN[:, :], ALU.mult)
                P.op("dve", lambda e: e.tensor_scalar(out=NZIM[:, :], in0=ZIM[:, :], scalar1=-1.0, scalar2=None, op0=ALU.mult), [ZIM], [NZIM])
                P.op("dve", lambda e: e.tensor_scalar(out=X8[:, :], in0=TH[:, :], scalar1=512.0, scalar2=None, op0=ALU.mult), [TH], [X8])
                sin_of(P, ss8, X8, X8[:, :], ST, ST[:, :])
                sin_of(P, ss8, X8, X8[:, :], CT, CT[:, :], phase=math.pi / 2)
                for j in range(8):
                    sin_of(P, ssb, IOTA, IOTA[:, :], SINt, SINt[:, j, :], scale=TH[:, j:j + 1], scale_b=TH)
                    sin_of(P, ssb, IOTA, IOTA[:, :], COSt, COSt[:, j, :], phase=math.pi / 2, scale=TH[:, j:j + 1], scale_b=TH)
                BRE = kb.sb(pp, "bre", [128, 8, 128]); BIM = kb.sb(pp, "bim", [128, 8, 128])
                CRE = kb.sb(pp, "cre", [128, 8, 128]); CIM = kb.sb(pp, "cim", [128, 8, 128])
                BB = [kb.sb(pp, f"bb{i}", [128, 8, 128]) for i in range(2)]
                TB = kb.sb(pp, "tb", [128, 128])
                for t_ in (BRE, BIM, CRE, CIM):
                    P.op("pool", lambda e, t_=t_: e.memset(t_[:, :, :], 0.0), [], [t_])
                for g in range(16):
                    j, gl = g // 2, g % 2
                    c0 = 32 * (j % 4) + 16 * gl
                    for (dst, nm) in ((BRE, "s5_b_re"), (BIM, "s5_b_im")):
                        P.dma("sp", lambda e, dst=dst, nm=nm, g=g, j=j, gl=gl, c0=c0: e.dma_start(out=dst[gl * 64:(gl + 1) * 64, j, c0:c0 + 16], in_=EXT[nm].ap[l, d, g, :, :]), [EXT[nm]], [dst])
                    for (dst, nm) in ((CRE, "s5_c_re"), (CIM, "s5_c_im")):
                        P.dma("sp", lambda e, dst=dst, nm=nm, g=g, j=j, gl=gl, c0=c0: e.dma_start(out=dst[c0:c0 + 16, j, gl * 64:(gl + 1) * 64], in_=EXT[nm].ap[l, d, g, :, :]), [EXT[nm]], [dst])
                for j in range(8):
                    P.op("dve", lambda e, j=j: e.tensor_scalar(out=TB[:, :], in0=BRE[:, j, :], scalar1=ZRE[:, j:j + 1], scalar2=None, op0=ALU.mult), [BRE, ZRE], [TB])
                    P.op("dve", lambda e, j=j: e.scalar_tensor_tensor(out=BB[0][:, j, :], in0=BIM[:, j, :], scalar=NZIM[:, j:j + 1], in1=TB[:, :], op0=ALU.mult, op1=ALU.add), [BIM, NZIM, TB], [BB[0]])
                    P.op("dve", lambda e, j=j: e.tensor_scalar(out=TB[:, :], in0=BIM[:, j, :], scalar1=ZRE[:, j:j + 1], scalar2=None, op0=ALU.mult), [BIM, ZRE], [TB])
                    P.op("dve", lambda e, j=j: e.scalar_tensor_tensor(out=BB[1][:, j, :], in0=BRE[:, j, :], scalar=ZIM[:, j:j + 1], in1=TB[:, :], op0=ALU.mult, op1=ALU.add), [BRE, ZIM, TB], [BB[1]])
                n = 0
                for (src, dst, scale) in ((BB[0], LB[0], 1.0), (BB[1], LB[1], 1.0), (CRE, LC[0], 1.0), (CIM, LC[1], -1.0)):
                    for j in range(8):
                        ps = kb.psum[n % 4]; n += 1
                        P.op("pe", lambda e, src=src, j=j, ps=ps: e.transpose(out=ps[:, 0:128], in_=src[:, j, :], identity=ident[:, :]), [src, ident], [ps])
                        P.op("act", lambda e, dst=dst, j=j, ps=ps, scale=scale: e.mul(out=dst[:, j, :], in_=ps[:, 0:128], mul=scale), [ps], [dst])
                P.barrier()
            with ExitStack() as mp:
                R3 = 3

                def w(nm, n_=R3, dt=F32):
                    return [kb.sb(mp, f"{nm}{i}", [128, 512], dt) for i in range(n_)]
                BR, BI, WR, WI, SR, SI_ = w("BR"), w("BI"), w("WR"), w("WI"), w("SR"), w("SI")
                SRb, SIb = w("SRb", R3, BF16), w("SIb", R3, BF16)
                TA = w("TA", 4); TB_ = w("TB", 4)
                INI = [kb.sb(mp, f"ini{i}", [128, 4]) for i in range(8)]
                rev = (d == 1)
                order = list(range(8)) if not rev else list(range(7, -1, -1))
                its = [(ci, c, j) for ci, c in enumerate(order) for j in range(8)]
                fl = (lambda t_: t_[:, ::-1]) if rev else (lambda t_: t_[:, :])

                def tabs(j):
                    return (COSt[:, j, ::-1] if rev else COSt[:, j, :]), (SINt[:, j, ::-1] if rev else SINt[:, j, :])

                def st0(it):
                    ci, c, j = its[it]
                    sl = slice(c * 512, (c + 1) * 512)
                    r = it % R3
                    pa, pb = kb.psum[2 * (it % 2)], kb.psum[2 * (it % 2) + 1]
                    P.op("pe", lambda e: e.matmul(pa[:, :], lhsT=LB[0][:, j, :], rhs=U[:, j // 4, sl], start=True, stop=True), [LB[0], U], [pa])
                    P.op("pe", lambda e: e.matmul(pb[:, :], lhsT=LB[1][:, j, :], rhs=U[:, j // 4, sl], start=True, stop=True), [LB[1], U], [pb])
                    copy_on(P, "act", BR[r], BR[r][:, :], pa, pa[:, :])
                    copy_on(P, "act", BI[r], BI[r][:, :], pb, pb[:, :])

                def st1(it):
                    ci, c, j = its[it]
                    r = it % R3
                    cosap, sinap = tabs(j)
                    br, bi, wr, wi = BR[r], BI[r], WR[r], WI[r]
                    t1, t2, t3, t4 = TA[0], TA[1], TA[2], TA[3]
                    tt(P, "pool", t1, t1[:, :], COSt, cosap, br, br[:, :], ALU.mult)
                    tt(P, "pool", t2, t2[:, :], SINt, sinap, bi, bi[:, :], ALU.mult)
                    tt(P, "pool", wr, wr[:, :], t1, t1[:, :], t2, t2[:, :], ALU.add)
                    tt(P, "dve", t3, t3[:, :], COSt, cosap, bi, bi[:, :], ALU.mult)
                    tt(P, "dve", t4, t4[:, :], SINt, sinap, br, br[:, :], ALU.mult)
                    tt(P, "dve", wi, wi[:, :], t3, t3[:, :], t4, t4[:, :], ALU.subtract)

                def st2(it):
                    ci, c, j = its[it]
                    r = it % R3
                    wr, wi, sr, si = WR[r], WI[r], SR[r], SI_[r]
                    ini = INI[j]
                    if ci == 0:
                        ire, iim = 0.0, 0.0
                    else:
                        P.op("dve", lambda e: e.tensor_scalar(out=ini[:, 2:3], in0=ini[:, 0:1], scalar1=CT[:, j:j + 1], scalar2=None, op0=ALU.mult), [ini, CT], [ini])
                        P.op("dve", lambda e: e.scalar_tensor_tensor(out=ini[:, 2:3], in0=ini[:, 1:2], scalar=ST[:, j:j + 1], in1=ini[:, 2:3], op0=ALU.mult, op1=ALU.subtract), [ini, ST], [ini])
                        P.op("dve", lambda e: e.tensor_scalar(out=ini[:, 2:3], in0=ini[:, 2:3], scalar1=-1.0, scalar2=None, op0=ALU.mult), [ini], [ini])
                        P.op("dve", lambda e: e.tensor_scalar(out=ini[:, 3:4], in0=ini[:, 0:1], scalar1=ST[:, j:j + 1], scalar2=None, op0=ALU.mult), [ini, ST], [ini])
                        P.op("dve", lambda e: e.scalar_tensor_tensor(out=ini[:, 3:4], in0=ini[:, 1:2], scalar=CT[:, j:j + 1], in1=ini[:, 3:4], op0=ALU.mult, op1=ALU.add), [ini, CT], [ini])
                        ire, iim = ini[:, 2:3], ini[:, 3:4]
                    magb = MAG[:, j:j + 1].to_broadcast([128, 512])
                    P.op("dve", lambda e: e.tensor_tensor_scan(out=fl(sr), data0=magb, data1=fl(wr), initial=ire, op0=ALU.mult, op1=ALU.add), [MAG, wr, ini], [sr])
                    P.op("dve", lambda e: e.tensor_tensor_scan(out=fl(si), data0=magb, data1=fl(wi), initial=iim, op0=ALU.mult, op1=ALU.add), [MAG, wi, ini], [si])
                    last = slice(0, 1) if rev else slice(511, 512)
                    P.op("pool", lambda e: e.tensor_copy(out=ini[:, 0:1], in_=sr[:, last]), [sr], [ini])
                    P.op("pool", lambda e: e.tensor_copy(out=ini[:, 1:2], in_=si[:, last]), [si], [ini])

                def st3(it):
                    ci, c, j = its[it]
                    r = it % R3
                    cosap, sinap = tabs(j)
                    sr, si = SR[r], SI_[r]
                    t1, t2, t3, t4 = TB_[0], TB_[1], TB_[2], TB_[3]
                    tt(P, "pool", t1, t1[:, :], COSt, cosap, sr, sr[:, :], ALU.mult)
                    tt(P, "pool", t2, t2[:, :], SINt, sinap, si, si[:, :], ALU.mult)
                    tt(P, "pool", SRb[r], SRb[r][:, :], t1, t1[:, :], t2, t2[:, :], ALU.subtract)
                    tt(P, "dve", t3, t3[:, :], SINt, sinap, sr, sr[:, :], ALU.mult)
                    tt(P, "dve", t4, t4[:, :], COSt, cosap, si, si[:, :], ALU.mult)
                    tt(P, "dve", SIb[r], SIb[r][:, :], t3, t3[:, :], t4, t4[:, :], ALU.add)

                def st4(it):
                    ci, c, j = its[it]
                    sl = slice(c * 512, (c + 1) * 512)
                    r = it % R3
                    cc, jj = j // 4, j % 4
                    py = kb.psum[4 + (ci * 2 + cc) % 4]
                    P.op("pe", lambda e: e.matmul(py[:, :], lhsT=LC[0][:, j, :], rhs=SRb[r][:, :], start=(jj == 0), stop=False), [LC[0], SRb[r]], [py])
                    P.op("pe", lambda e: e.matmul(py[:, :], lhsT=LC[1][:, j, :], rhs=SIb[r][:, :], start=False, stop=(jj == 3)), [LC[1], SIb[r]], [py])
                    if jj == 3:
                        P.op("dve", lambda e: e.tensor_tensor(out=YACC[:, cc, sl], in0=YACC[:, cc, sl], in1=py[:, :], op=ALU.add), [YACC, py], [YACC])

                stages = [st0, st1, st2, st3, st4]
                nit = len(its)
                for step in range(nit + len(stages) - 1):
                    for k, fn in enumerate(stages):
                        it = step - k
                        if 0 <= it < nit:
                            fn(it)
                P.barrier()
        dbg_dump(C, "S5Y0", YACC, YACC[:, 0, :], 128, L)
        with ExitStack() as gp:
            Wg = kb.sb(gp, "s5Wg", [128, 2, 512], BF16)
            for k in range(2):
                P.dma("pool", lambda e, k=k: e.dma_start(out=Wg[:, k, :], in_=EXT["s5_w_glu"].ap[l, k * 128:(k + 1) * 128, :]), [EXT["s5_w_glu"]], [Wg], waw=False)
            SQ = kb.sb(gp, "gSQ", [128, 2, 512]); GY = kb.sb(gp, "gGY", [128, 2, 512], BF16)
            AO = kb.sb(gp, "gAO", [128, 2, 512]); GO = kb.sb(gp, "gGO", [128, 2, 512])
            scr = rms_scratch(kb, gp)
            c0 = math.sqrt(2.0 / math.pi)
            for tc in range(8):
                sl = slice(tc * 512, (tc + 1) * 512)
                P.op("act", lambda e: e.activation(out=SQ[:, :, :], in_=YACC[:, :, sl], func=AF.Square), [YACC], [SQ])
                P.op("dve", lambda e: e.tensor_scalar(out=SQ[:, :, :], in0=SQ[:, :, :], scalar1=2.0 * c0 * 0.044715, scalar2=2.0 * c0, op0=ALU.mult, op1=ALU.add), [SQ], [SQ])
                P.op("pool", lambda e: e.tensor_tensor(out=SQ[:, :, :], in0=SQ[:, :, :], in1=YACC[:, :, sl], op=ALU.mult), [SQ, YACC], [SQ])
                P.op("act", lambda e: e.activation(out=SQ[:, :, :], in_=SQ[:, :, :], func=AF.Sigmoid), [SQ], [SQ])
                P.op("dve", lambda e: e.tensor_tensor(out=GY[:, :, :], in0=SQ[:, :, :], in1=YACC[:, :, sl], op=ALU.mult), [SQ, YACC], [GY])
                for oc in range(4):
                    ps = kb.psum[oc]
                    for k in range(2):
                        P.op("pe", lambda e, oc=oc, k=k, ps=ps: e.matmul(ps[:, :], lhsT=Wg[:, k, oc * 128:(oc + 1) * 128], rhs=GY[:, k, :], start=(k == 0), stop=(k == 1)), [Wg, GY], [ps])
                    if oc < 2:
                        P.op("act", lambda e, oc=oc, ps=ps: e.activation(out=AO[:, oc, :], in_=ps[:, :], func=AF.Identity, bias=bglu[:, oc:oc + 1]), [ps, bglu], [AO])
                    else:
                        P.op("act", lambda e, oc=oc, ps=ps: e.activation(out=GO[:, oc - 2, :], in_=ps[:, :], func=AF.Sigmoid, bias=bglu[:, oc:oc + 1]), [ps, bglu], [GO])
                P.op("dve", lambda e: e.tensor_tensor(out=AO[:, :, :], in0=AO[:, :, :], in1=GO[:, :, :], op=ALU.mult), [AO, GO], [AO])
                group_rms(C, scr, [(AO, AO[:, 0, :]), (AO, AO[:, 1, :])], gain, 4, 512, tc)
            P.barrier()


def sin_reduce(P, scr, Ob, Oap):
    KI, KF, Rr = scr
    full = tuple(slice(None) for _ in range(len(KI.ap.shape)))
    P.op("dve", lambda e: e.tensor_scalar(out=KI[full], in0=Rr[full], scalar1=float(1.0 / (2.0 * math.pi)), scalar2=None, op0=ALU.mult), [Rr], [KI])
    P.op("dve", lambda e: e.tensor_copy(out=KF[full], in_=KI[full]), [KI], [KF])
    P.op("dve", lambda e: e.scalar_tensor_tensor(out=Rr[full], in0=KF[full], scalar=-TWO_PI_HI, in1=Rr[full], op0=ALU.mult, op1=ALU.add), [KF, Rr], [Rr])
    P.op("dve", lambda e: e.scalar_tensor_tensor(out=Rr[full], in0=KF[full], scalar=-TWO_PI_LO, in1=Rr[full], op0=ALU.mult, op1=ALU.add), [KF, Rr], [Rr])
    P.op("act", lambda e: e.activation(out=Oap, in_=Rr[full], func=AF.Sin), [Rr], [Ob])


def phase_hyena(C, l):
    P, kb, EXT, projT = C["P"], C["kb"], C["EXT"], C["projT"]
    ident, hy_tok, z1_tok, k_d = C["ident"], C["hy_tok"], C["z1_tok"], C["k_d"]
    with ExitStack() as ph:
        gain = load_cols(P, kb, ph, "mixg", EXT["mix_norm_g"], EXT["mix_norm_g"].ap[l, :], 8)
        CH = kb.sb(ph, "hCH", [128, 32]); SH = kb.sb(ph, "hSH", [128, 32])
        ld(P, CH, CH[:, :], EXT["hy_ch"], EXT["hy_ch"].ap.ap()); ld(P, SH, SH[:, :], EXT["hy_sh"], EXT["hy_sh"].ap.ap())
        BIAS = kb.sb(ph, "hBIAS", [128, 2, 256])
        for o in range(2):
            ld(P, BIAS, BIAS[:, o, :], EXT["hy_bias"], EXT["hy_bias"].ap[l, o:o + 1, :].partition_broadcast(128))
        Zop = kb.sb(ph, "hZop", [128, NT, 512], BF16)
        SGN = kb.sb(ph, "hSGN", [128, 1]); Jm = kb.sb(ph, "hJ", [128, 128])
        ld(P, SGN, SGN[:, :], EXT["hy_sgn"], EXT["hy_sgn"].ap.ap()); ld(P, Jm, Jm[:, :], EXT["hy_J"], EXT["hy_J"].ap.ap())
        csn = [0]

        with ExitStack() as p1:
            sw = kb.sb(p1, "hsw", [128, 6, 3]); sbias = load_cols(P, kb, p1, "hsb", EXT["hy_short_b"], EXT["hy_short_b"].ap[l, :], 6)
            for ch in range(6):
                P.dma("sp", lambda e, ch=ch: e.dma_start(out=sw[:, ch, :], in_=EXT["hy_short_w"].ap[l, :, ch * 128:(ch + 1) * 128].rearrange("k p -> p k"), allow_slow_non_contiguous=True), [EXT["hy_short_w"]], [sw], waw=False)
            XP = [kb.sb(p1, f"hXP{i}", [128, L + 2]) for i in range(2)]
            Y = [kb.sb(p1, f"hY{i}", [128, L]) for i in range(2)]
            STG = [kb.sb(p1, f"hSTG{i}", [128, 4, 128]) for i in range(2)]
            for xp in XP:
                P.op("pool", lambda e, xp=xp: e.memset(xp[:, 0:1], 0.0), [], [xp])
                P.op("pool", lambda e, xp=xp: e.memset(xp[:, L + 1:L + 2], 0.0), [], [xp])
            n = 0
            for ch in range(6):
                xp, y = XP[ch % 2], Y[ch % 2]
                ld(P, xp, xp[:, 1:L + 1], projT, projT.ap[512 + ch * 128:512 + (ch + 1) * 128, :])
                P.op("dve", lambda e, ch=ch, xp=xp, y=y: e.tensor_scalar(out=y[:, :], in0=xp[:, 0:L], scalar1=sw[:, ch, 0:1], scalar2=sbias[:, ch:ch + 1], op0=ALU.mult, op1=ALU.add), [xp, sw, sbias], [y])
                for k in (1, 2):
                    P.op("dve", lambda e, ch=ch, xp=xp, y=y, k=k: e.scalar_tensor_tensor(out=y[:, :], in0=xp[:, k:k + L], scalar=sw[:, ch, k:k + 1], in1=y[:, :], op0=ALU.mult, op1=ALU.add), [xp, sw, y], [y])
                for g4 in range(8):
                    ps = kb.psum[n % 4]; stg = STG[n % 2]; n += 1
                    for j in range(4):
                        nt = g4 * 4 + j
                        P.op("pe", lambda e, ps=ps, j=j, nt=nt, y=y: e.transpose(out=ps[:, j * 128:(j + 1) * 128], in_=y[:, nt * 128:(nt + 1) * 128], identity=ident[:, :]), [y, ident], [ps])
                    copy_on(P, "act", stg, stg[:, :, :], ps, ps[:, :].rearrange("p (j c) -> p j c", j=4))
                    if ch < 2:
                        P.op("pool", lambda e, stg=stg, g4=g4, ch=ch: e.tensor_copy(out=Zop[:, g4 * 4:(g4 + 1) * 4, ch * 128:(ch + 1) * 128], in_=stg[:, :, :]), [stg], [Zop])
                        P.op("pool", lambda e, stg=stg, g4=g4, ch=ch: e.tensor_scalar(out=Zop[:, g4 * 4:(g4 + 1) * 4, 256 + ch * 128:256 + (ch + 1) * 128], in0=stg[:, :, :], scalar1=SGN[:, 0:1], scalar2=None, op0=ALU.mult), [stg, SGN], [Zop])
                    P.dma("sp", lambda e, stg=stg, g4=g4, ch=ch: e.dma_start(out=hy_tok.ap[g4 * 512:(g4 + 1) * 512, ch * 128:(ch + 1) * 128].rearrange("(j p) c -> p j c", p=128), in_=stg[:, :, :]), [stg], [hy_tok], waw=False)
            P.barrier()
        with ExitStack() as p2:
            Fop = kb.sb(p2, "hF", [128, NT, 1024], BF16)
            with ExitStack() as p2a:
                ZT = kb.sb(p2a, "hZT", [33, L]); W1 = kb.sb(p2a, "hW1", [33, 64]); W2 = kb.sb(p2a, "hW2", [64, 64]); W3 = kb.sb(p2a, "hW3", [64, 1024])
                ld(P, ZT, ZT[:, :], EXT["hy_zT"], EXT["hy_zT"].ap.ap())
                ld(P, W1, W1[:, :], EXT["hy_f_w1"], EXT["hy_f_w1"].ap[l, :, :]); ld(P, W2, W2[:, :], EXT["hy_f_w2"], EXT["hy_f_w2"].ap[l, :, :])
                ld(P, W3, W3[:, :], EXT["hy_f_w3"], EXT["hy_f_w3"].ap[l, :, :])
                prm = kb.sb(p2a, "hprm", [64, 3])
                for i, nm in enumerate(("hy_f_b1", "hy_f_b2", "hy_f_freq")):
                    P.dma("sp", lambda e, i=i, nm=nm: e.dma_start(out=prm[:, i:i + 1], in_=EXT[nm].ap[l, :].rearrange("(p o) -> p o", o=1)), [EXT[nm]], [prm], waw=False)
                H1T = kb.sb(p2a, "hH1T", [64, L]); H2T = kb.sb(p2a, "hH2T", [64, L])
                scr = sin_scratch(kb, p2a, [64, 512], "h")
                for (Wm, kdim, src, dst, bcol) in ((W1, 33, ZT, H1T, 0), (W2, 64, H1T, H2T, 1)):
                    for tc in range(8):
                        sl = slice(tc * 512, (tc + 1) * 512)
                        ps = kb.psum[tc % 2]
                        P.op("pe", lambda e, Wm=Wm, kdim=kdim, src=src, ps=ps, sl=sl: e.matmul(ps[0:64, :], lhsT=Wm[0:kdim, 0:64], rhs=src[0:kdim, sl], start=True, stop=True), [Wm, src], [ps])
                        P.op("dve", lambda e, ps=ps, bcol=bcol: e.tensor_scalar(out=scr[2][:, :], in0=ps[0:64, :], scalar1=prm[:, bcol:bcol + 1], scalar2=prm[:, 2:3], op0=ALU.add, op1=ALU.mult), [ps, prm], [scr[2]])
                        sin_reduce(P, scr, dst, dst[:, sl])
                WIN = [kb.sb(p2a, f"hWIN{i}", [128, 256]) for i in range(2)]
                BW = [kb.sb(p2a, f"hBW{i}", [128, 256]) for i in range(2)]
                SD = [kb.sb(p2a, f"hSD{i}", [128, 512]) for i in range(2)]
                for mt in range(NT):
                    win = WIN[mt % 2]
                    ld(P, win, win[:, :], EXT["hy_win"], EXT["hy_win"].ap[mt * 128:(mt + 1) * 128, :])
                    for o in range(2):
                        ps = kb.psum[2 + (2 * mt + o) % 4]; bw = BW[o]; sd = SD[o]
                        P.op("pe", lambda e, ps=ps, mt=mt, o=o: e.matmul(ps[:, :], lhsT=H2T[:, mt * 128:(mt + 1) * 128], rhs=W3[:, o * 512:(o + 1) * 512], start=True, stop=True), [H2T, W3], [ps])
                        copy_on(P, "act", bw, bw[:, :], ps, ps[:, 256:512])
                        if mt == 0:
                            P.op("pool", lambda e, bw=bw: e.memset(bw[0:1, :], 0.0), [], [bw])
                        P.op("dve", lambda e, ps=ps, bw=bw, sd=sd: e.tensor_tensor(out=sd[:, 0:256], in0=ps[:, 0:256], in1=bw[:, :], op=ALU.add), [ps, bw], [sd])
                        P.op("dve", lambda e, ps=ps, bw=bw, sd=sd: e.tensor_tensor(out=sd[:, 256:512], in0=bw[:, :], in1=ps[:, 0:256], op=ALU.subtract), [ps, bw], [sd])
                        P.op("pool", lambda e, sd=sd, mt=mt, o=o, win=win: e.tensor_tensor(out=Fop[:, mt, o * 256:(o + 1) * 256], in0=sd[:, 0:256], in1=win[:, :], op=ALU.mult), [sd, win], [Fop])
                        P.op("pool", lambda e, sd=sd, mt=mt, o=o, win=win: e.tensor_tensor(out=Fop[:, mt, 512 + o * 256:512 + (o + 1) * 256], in0=sd[:, 256:512], in1=win[:, :], op=ALU.mult), [sd, win], [Fop])
                P.barrier()
            T1s = [kb.sb(p2, f"hfT1{i}", [128, 512]) for i in range(2)]; T2s = [kb.sb(p2, f"hfT2{i}", [128, 512]) for i in range(2)]; KO = [kb.sb(p2, f"hKO{i}", [128, 2, 512]) for i in range(2)]
            CTF = [kb.sb(p2, f"hCTF{i}", [128, 32, 128], BF16) for i in range(6)]
            CS = [(CTF[0], CTF[1]), (CTF[2], CTF[3]), (CTF[4], CTF[5])]
            for kc in range(32):
                Ct, St = CS[csn[0] % 3]; csn[0] += 1
                ld(P, Ct, Ct[:, :, :], EXT["dftC0"], EXT["dftC0"].ap[kc, :, :].rearrange("p (a b) -> p a b", b=128))
                ld(P, St, St[:, :, :], EXT["dftS0"], EXT["dftS0"].ap[kc, :, :].rearrange("p (a b) -> p a b", b=128))
                pc0, ps0 = kb.psum[2 * (kc % 4)], kb.psum[2 * (kc % 4) + 1]
                for nc_ in range(NT):
                    P.op("pe", lambda e, nc_=nc_: e.matmul(pc0[:, :], lhsT=Ct[:, nc_, :], rhs=Fop[:, nc_, 0:512], start=(nc_ == 0), stop=(nc_ == NT - 1)), [Ct, Fop], [pc0])
                    P.op("pe", lambda e, nc_=nc_: e.matmul(ps0[:, :], lhsT=St[:, nc_, :], rhs=Fop[:, nc_, 512:1024], start=(nc_ == 0), stop=(nc_ == NT - 1)), [St, Fop], [ps0])
                ko = KO[kc % 2]
                P.op("act", lambda e: e.copy(out=ko[:, 0, :], in_=pc0[:, :]), [pc0], [ko])
                P.op("dve", lambda e: e.tensor_copy(out=ko[:, 1, :], in_=ps0[:, :]), [ps0], [ko])
                P.dma("sp", lambda e, ko=ko, kc=kc: e.dma_start(out=k_d.ap[kc, :, :, :], in_=ko[:, :, :]), [ko], [k_d], waw=False)
            P.barrier()
        CT6 = [kb.sb(ph, f"hCT{i}", [128, 32, 128], BF16) for i in range(6)]
        YW = kb.sb(ph, "hYW", [128, NT, 512], BF16)
        OUTF = kb.sb(ph, "hOUTF", [128, 2, L])
        ctn = [0]

        def stream_c(idx):
            Ct = CT6[ctn[0] % 6]; ctn[0] += 1
            ld(P, Ct, Ct[:, :, :], EXT["dftC"], EXT["dftC"].ap[idx, :, :].rearrange("p (a b) -> p a b", b=128))
            return Ct

        KS = [kb.sb(ph, f"hKS{i}", [128, 2, 256]) for i in range(6)]
        W = [[kb.sb(ph, f"hw{a}{i}", [128, 256]) for i in range(2)] for a in range(5)]
        XS = [kb.sb(ph, f"hXS{i}", [128, 256]) for i in range(4)]
        X1 = [kb.sb(ph, f"hX1{i}", [128, 256]) for i in range(4)]
        ZG = [kb.sb(ph, f"hZG{i}", [128, 2, 256]) for i in range(6)]
        OT = [kb.sb(ph, f"hOT{i}", [128, 256]) for i in range(2)]
        pairn = [0]
        HS = os.environ.get("HY_STOP", "")
        for o in range(2):
            if HS == "pre" or (HS == "fwd0" and o == 1):
                break
            base = pairn[0]; pairn[0] += 32
            st_f = {}

            def f0(b, o=o, base=base, st_f=st_f):
                pr_ = (base + b) % 2
                blocks = (b, 31 - b)
                outs = (kb.psum[2 * pr_], kb.psum[2 * pr_ + 1])
                kss = []
                for bi, kc in enumerate(blocks):
                    Ct = stream_c(kc)
                    ks = KS[(2 * b + bi) % 6]; kss.append(ks)
                    ld(P, ks, ks[:, :, :], k_d, k_d.ap[kc, :, :, o * 256:(o + 1) * 256])
                    po = outs[bi]
                    for nc_ in range(NT):
                        P.op("pe", lambda e, po=po, Ct=Ct, nc_=nc_: e.matmul(po[:, :], lhsT=Ct[:, nc_, :], rhs=Zop[:, nc_, :], start=(nc_ == 0), stop=(nc_ == NT - 1)), [Ct, Zop], [po])
                st_f[b] = (pr_, blocks, outs, kss)

            def f1(b, st_f=st_f):
                pr_, blocks, outs, kss = st_f.pop(b)
                jps = (kb.psum[4 + 2 * pr_], kb.psum[5 + 2 * pr_])
                xs = [XS[(2 * b + bi) % 4] for bi in range(2)]
                for bi in range(2):
                    copy_on(P, "act", xs[bi], xs[bi][:, :], outs[bi], outs[bi][:, 256:512])
                for bi in range(2):
                    P.op("pe", lambda e, bi=bi: e.matmul(jps[bi][:, 0:256], lhsT=Jm[:, :], rhs=xs[1 - bi][:, :], start=True, stop=True), [Jm, xs[1 - bi]], [jps[bi]])
                for bi, kc in enumerate(blocks):
                    pa, pb, ks = outs[bi], jps[bi], kss[bi]
                    t1, t2, t3, t4, t5 = W[0][bi], W[1][bi], W[2][bi], W[3][bi], W[4][bi]
                    P.op("dve", lambda e: e.tensor_tensor(out=t1[:, :], in0=pa[:, 0:256], in1=ks[:, 0, :], op=ALU.mult), [pa, ks], [t1])
                    P.op("dve", lambda e: e.tensor_tensor(out=t2[:, :], in0=pb[:, 0:256], in1=ks[:, 1, :], op=ALU.mult), [pb, ks], [t2])
                    P.op("pool", lambda e: e.tensor_tensor(out=YW[:, kc, 0:256], in0=t1[:, :], in1=t2[:, :], op=ALU.add), [t1, t2], [YW])
                    P.op("dve", lambda e: e.tensor_tensor(out=t3[:, :], in0=pa[:, 0:256], in1=ks[:, 1, :], op=ALU.mult), [pa, ks], [t3])
                    P.op("dve", lambda e: e.tensor_tensor(out=t4[:, :], in0=pb[:, 0:256], in1=ks[:, 0, :], op=ALU.mult), [pb, ks], [t4])
                    P.op("pool", lambda e: e.tensor_tensor(out=t5[:, :], in0=t4[:, :], in1=t3[:, :], op=ALU.subtract), [t3, t4], [t5])
                    P.op("pool", lambda e: e.tensor_scalar(out=YW[:, kc, 256:512], in0=t5[:, :], scalar1=SGN[:, 0:1], scalar2=None, op0=ALU.mult), [t5, SGN], [YW])

            pipeline([f0, f1], 16)
            if HS == "fwd0":
                break
            base = pairn[0]; pairn[0] += 32
            st_i = {}

            def i0(b, o=o, base=base, st_i=st_i):
                pr_ = (base + b) % 2
                blocks = (b, 31 - b)
                outs = (kb.psum[2 * pr_], kb.psum[2 * pr_ + 1])
                zgs = []
                for bi, nc_ in enumerate(blocks):
                    Ct = stream_c(nc_)
                    zg = ZG[(2 * b + bi) % 6]; zgs.append(zg)
                    if o == 0:
                        ld(P, zg, zg[:, 0, :], hy_tok, hy_tok.ap[nc_ * 128:(nc_ + 1) * 128, 0:256])
                        ld(P, zg, zg[:, 1, :], hy_tok, hy_tok.ap[nc_ * 128:(nc_ + 1) * 128, 256:512])
                    else:
                        ld(P, zg, zg[:, 0, :], z1_tok, z1_tok.ap[nc_ * 128:(nc_ + 1) * 128, :])
                        ld(P, zg, zg[:, 1, :], hy_tok, hy_tok.ap[nc_ * 128:(nc_ + 1) * 128, 512:768])
                    po = outs[bi]
                    for kc in range(32):
                        P.op("pe", lambda e, po=po, Ct=Ct, kc=kc: e.matmul(po[:, :], lhsT=Ct[:, kc, :], rhs=YW[:, kc, :], start=(kc == 0), stop=(kc == 31)), [Ct, YW], [po])
                st_i[b] = (pr_, blocks, outs, zgs)

            def i1(b, o=o, st_i=st_i):
                pr_, blocks, outs, zgs = st_i.pop(b)
                jps = (kb.psum[4 + 2 * pr_], kb.psum[5 + 2 * pr_])
                xs = [XS[(2 * b + bi) % 4] for bi in range(2)]
                x1s = [X1[(2 * b + bi) % 4] for bi in range(2)]
                for bi in range(2):
                    copy_on(P, "act", xs[bi], xs[bi][:, :], outs[bi], outs[bi][:, 256:512])
                    copy_on(P, "act", x1s[bi], x1s[bi][:, :], outs[bi], outs[bi][:, 0:256])
                for bi in range(2):
                    P.op("pe", lambda e, bi=bi: e.matmul(jps[bi][:, 0:256], lhsT=Jm[:, :], rhs=xs[1 - bi][:, :], start=True, stop=True), [Jm, xs[1 - bi]], [jps[bi]])
                for bi, nc_ in enumerate(blocks):
                    zg, pj, x1 = zgs[bi], jps[bi], x1s[bi]
                    t1, ot = W[0][bi], OT[bi]
                    P.op("pool", lambda e: e.tensor_tensor(out=t1[:, :], in0=zg[:, 0, :], in1=BIAS[:, o, :], op=ALU.mult), [zg, BIAS], [t1])
                    P.op("dve", lambda e: e.scalar_tensor_tensor(out=t1[:, :], in0=pj[:, 0:256], scalar=2.0 / (2 * L), in1=t1[:, :], op0=ALU.mult, op1=ALU.add), [pj, t1], [t1])
                    P.op("dve", lambda e: e.scalar_tensor_tensor(out=t1[:, :], in0=x1[:, :], scalar=2.0 / (2 * L), in1=t1[:, :], op0=ALU.mult, op1=ALU.add), [x1, t1], [t1])
                    P.op("dve", lambda e: e.tensor_tensor(out=ot[:, :], in0=t1[:, :], in1=zg[:, 1, :], op=ALU.mult), [t1, zg], [ot])
                    if o == 0:
                        P.op("pool", lambda e: e.tensor_copy(out=Zop[:, nc_, 0:256], in_=ot[:, :]), [ot], [Zop])
                        P.op("pool", lambda e: e.tensor_scalar(out=Zop[:, nc_, 256:512], in0=ot[:, :], scalar1=SGN[:, 0:1], scalar2=None, op0=ALU.mult), [ot, SGN], [Zop])
                        P.dma("sp", lambda e: e.dma_start(out=z1_tok.ap[nc_ * 128:(nc_ + 1) * 128, :], in_=ot[:, :]), [ot], [z1_tok], waw=False)
                    else:
                        pt = jps[bi]
                        for c in range(2):
                            P.op("pe", lambda e, c=c: e.transpose(out=pt[:, 256 + c * 128:256 + (c + 1) * 128], in_=ot[:, c * 128:(c + 1) * 128], identity=ident[:, :]), [ot, ident], [pt])
                        copy_on(P, "act", OUTF, OUTF[:, :, nc_ * 128:(nc_ + 1) * 128], pt, pt[:, 256:512].rearrange("p (c t) -> p c t", c=2))

            pipeline([i0, i1], 16)
        dbg_dump(C, "HYO0", OUTF, OUTF[:, 0, :], 128, L)
        scr = rms_scratch(kb, ph)
        for tc in range(8):
            sl = slice(tc * 512, (tc + 1) * 512)
            group_rms(C, scr, [(OUTF, OUTF[:, 0, sl]), (OUTF, OUTF[:, 1, sl])], gain, 2, 256, tc)
        P.barrier()


def phase_moe(C, l, stop_after):
    P, kb, EXT = C["P"], C["kb"], C["EXT"]
    ident, h_tok, h1_tok, moe_acc, mergedT = C["ident"], C["h_tok"], C["h1_tok"], C["moe_acc"], C["mergedT"]
    with ExitStack() as ph:
        AFFT = kb.sb(ph, "mAFFT", [16, L])
        IDXT = kb.sb(ph, "mIDXT", [128, 4, 16], I32); GATET = kb.sb(ph, "mGATET", [128, 4, 16])
        with ExitStack() as pd:
            Wo = kb.sb(pd, "dWo", [128, 8, D], BF16)
            for k in range(8):
                P.dma("pool", lambda e, k=k: e.dma_start(out=Wo[:, k, :], in_=EXT["w_out"].ap[l, k * 128:(k + 1) * 128, :]), [EXT["w_out"]], [Wo], waw=False)
            G1 = kb.sb(pd, "dG1", [128, D]); B1 = kb.sb(pd, "dB1", [128, D])
            ld(P, G1, G1[:, :], EXT["ln1_g"], EXT["ln1_g"].ap[l:l + 1, :].partition_broadcast(128))
            ld(P, B1, B1[:, :], EXT["ln1_b"], EXT["ln1_b"].ap[l:l + 1, :].partition_broadcast(128))
            RW = kb.sb(pd, "dRW", [128, 8, 16]); RB = kb.sb(pd, "dRB", [128, 16])
            ld(P, RW, RW[:, :, :], EXT["router_w"], EXT["router_w"].ap[l, :, :].rearrange("(k p) e -> p k e", p=128))
            ld(P, RB, RB[:, :], EXT["router_b"], EXT["router_b"].ap[l:l + 1, :].partition_broadcast(128))
            MT = [kb.sb(pd, f"dMT{i}", [128, 8, 512], BF16) for i in range(3)]
            Hh = [kb.sb(pd, f"dH{i}", [128, D]) for i in range(3)]
            Xx = [kb.sb(pd, f"dX{i}", [128, D]) for i in range(6)]
            HT1 = [kb.sb(pd, f"dHT{i}", [128, 8, 128]) for i in range(3)]
            scr = ln_scr(kb, pd, 4)
            LG = [kb.sb(pd, f"dLG{i}", [128, 16]) for i in range(4)]
            SM = [kb.sb(pd, f"dSM{i}", [128, 4]) for i in range(4)]

            def d0(ti):
                tc, t4 = ti // 4, ti % 4
                mt = MT[tc % 3]
                if t4 == 0:
                    ld(P, mt, mt[:, :, :], mergedT, mergedT.ap.ap().rearrange("(k p) t -> p k t", p=128)[:, :, tc * 512:(tc + 1) * 512])
                H, X = Hh[ti % 3], Xx[ti % 6]
                ld(P, H, H[:, :], h_tok, h_tok.ap[ti * 128:(ti + 1) * 128, :])
                for half in range(2):
                    ps = kb.psum[(2 * ti + half) % 4]
                    for k in range(8):
                        P.op("pe", lambda e, ps=ps, k=k, half=half: e.matmul(ps[:, :], lhsT=mt[:, k, t4 * 128:(t4 + 1) * 128], rhs=Wo[:, k, half * 512:(half + 1) * 512], start=(k == 0), stop=(k == 7)), [mt, Wo], [ps])
                    P.op("dve", lambda e, ps=ps, half=half: e.scalar_tensor_tensor(out=X[:, half * 512:(half + 1) * 512], in0=H[:, half * 512:(half + 1) * 512], scalar=float(DN_ALPHA), in1=ps[:, :], op0=ALU.mult, op1=ALU.add), [H, ps], [X])

            def d1(ti):
                ln_a(P, Xx[ti % 6], scr[ti % 4])

            def d2(ti):
                ln_b(P, Xx[ti % 6], G1, B1, scr[ti % 4])

            def d3(ti):
                X, ht = Xx[ti % 6], HT1[ti % 3]
                P.dma("sp", lambda e: e.dma_start(out=h1_tok.ap[ti * 128:(ti + 1) * 128, :], in_=X[:, :]), [X], [h1_tok], waw=False)
                for half in range(2):
                    ps = kb.psum[4 + half]
                    for j in range(4):
                        k = half * 4 + j
                        P.op("pe", lambda e, ps=ps, k=k, j=j: e.transpose(out=ps[:, j * 128:(j + 1) * 128], in_=X[:, k * 128:(k + 1) * 128], identity=ident[:, :]), [X, ident], [ps])
                    copy_on(P, "act" if half else "pool_via_dve", ht, ht[:, half * 4:(half + 1) * 4, :], ps, ps[:, :].rearrange("p (j t) -> p j t", j=4))

            def d4(ti):
                ht, lg, sm = HT1[ti % 3], LG[ti % 4], SM[ti % 4]
                pr = kb.psum[6]
                for k in range(8):
                    P.op("pe", lambda e, k=k: e.matmul(pr[:, 0:16], lhsT=ht[:, k, :], rhs=RW[:, k, :], start=(k == 0), stop=(k == 7)), [ht, RW], [pr])
                P.op("dve", lambda e: e.tensor_tensor(out=lg[:, :], in0=pr[:, 0:16], in1=RB[:, :], op=ALU.add), [pr, RB], [lg])
                P.op("dve", lambda e: e.reduce_max(out=sm[:, 0:1], in_=lg[:, :], axis=AX.X), [lg], [sm])
                P.op("dve", lambda e: e.tensor_scalar(out=sm[:, 1:2], in0=sm[:, 0:1], scalar1=-1.0, scalar2=None, op0=ALU.mult), [sm], [sm])
                P.op("act", lambda e: e.activation(out=lg[:, :], in_=lg[:, :], func=AF.Exp, bias=sm[:, 1:2]), [lg, sm], [lg])

            def d5(ti):
                lg, sm = LG[ti % 4], SM[ti % 4]
                P.op("dve", lambda e: e.reduce_sum(out=sm[:, 2:3], in_=lg[:, :], axis=AX.X), [lg], [sm])
                P.op("dve", lambda e: e.reciprocal(out=sm[:, 3:4], in_=sm[:, 2:3]), [sm], [sm])
                P.op("dve", lambda e: e.tensor_scalar(out=lg[:, :], in0=lg[:, :], scalar1=sm[:, 3:4], scalar2=None, op0=ALU.mult), [lg, sm], [lg])
                pt = kb.psum[7]
                P.op("pe", lambda e: e.transpose(out=pt[0:16, 0:128], in_=lg[:, :], identity=ident[:, :]), [lg, ident], [pt])
                copy_on(P, "act", AFFT, AFFT[:, ti * 128:(ti + 1) * 128], pt, pt[0:16, 0:128])

            pipeline([d0, d1, d2, d3, d4, d5], NT)
            P.barrier()
        dbg_dump(C, "AFFT", AFFT, AFFT[:, :], 16, L)
        if stop_after == f"D{l}":
            return
        Wg = [kb.sb(ph, f"fWg{i}", [128, 8, D], BF16) for i in range(2)]
        Wu = [kb.sb(ph, f"fWu{i}", [128, 8, D], BF16) for i in range(2)]
        Wd = [kb.sb(ph, f"fWd{i}", [128, 8, D], BF16) for i in range(2)]
        ZERO = kb.sb(ph, "fZ", [128, D])
        WSTG = [kb.sb(ph, f"fWS{i}", [128, D]) for i in range(6)]
        wsn = [0]
        stg_of = {}
        WNAMES = ("exp_w_gate", "exp_w_up", "exp_w_down")

        def load_part(e_, k):
            for m in range(3):
                stg = WSTG[wsn[0] % 6]; wsn[0] += 1
                stg_of[(e_, k, m)] = stg
                P.dma("sp", lambda e, stg=stg, m=m: e.dma_start(out=stg[:, :], in_=EXT[WNAMES[m]].ap[l, e_, k * 128:(k + 1) * 128, :]), [EXT[WNAMES[m]]], [stg], waw=False)

        def cast_part(e_, k, engs=("act", "dve", "pool")):
            i2 = e_ % 2
            for m, Wt in enumerate((Wg[i2], Wu[i2], Wd[i2])):
                stg = stg_of.pop((e_, k, m))
                copy_on(P, engs[m], Wt, Wt[:, k, :], stg, stg[:, :])

        def load_w(e_, engs):
            for k in range(8):
                load_part(e_, k)
                if k >= 1:
                    cast_part(e_, k - 1, engs)
            cast_part(e_, 7, engs)

        P.op("pool", lambda e: e.memset(ZERO[:, :], 0.0), [], [ZERO])
        for ti in range(NT):
            P.dma("sp", lambda e, ti=ti: e.dma_start(out=moe_acc.ap[ti * 128:(ti + 1) * 128, :], in_=ZERO[:, :]), [ZERO], [moe_acc], waw=False)
        load_w(0, ("act", "pool", "pool"))
        with ExitStack() as pe_:
            WORK = kb.sb(pe_, "eWORK", [16, L]); MX = kb.sb(pe_, "eMX", [16, 8]); IDX = kb.sb(pe_, "eIDX", [16, 512], U32)
            GATES = kb.sb(pe_, "eGATES", [16, 512]); IDXF = kb.sb(pe_, "eIDXF", [16, 512])
            P.op("dve", lambda e: e.tensor_copy(out=WORK[:, :], in_=AFFT[:, :]), [AFFT], [WORK])
            for r in range(64):
                P.op("dve", lambda e, r=r: e.max(out=GATES[:, r * 8:(r + 1) * 8], in_=WORK[:, :]), [WORK], [GATES])
                P.op("dve", lambda e, r=r: e.max_index(out=IDX[:, r * 8:(r + 1) * 8], in_max=GATES[:, r * 8:(r + 1) * 8], in_values=WORK[:, :]), [WORK, GATES], [IDX])
                P.op("dve", lambda e, r=r: e.match_replace(out=WORK[:, :], in_to_replace=GATES[:, r * 8:(r + 1) * 8], in_values=WORK[:, :], imm_value=-1.0), [WORK, GATES], [WORK])
            P.op("dve", lambda e: e.tensor_copy(out=IDXF[:, :], in_=IDX[:, :]), [IDX], [IDXF])
            for st in range(4):
                pa, pb = kb.psum[st % 2], kb.psum[2 + st % 2]
                P.op("pe", lambda e, pa=pa, st=st: e.transpose(out=pa[:, 0:16], in_=IDXF[:, st * 128:(st + 1) * 128], identity=ident[0:16, 0:16]), [IDXF, ident], [pa])
                P.op("pe", lambda e, pb=pb, st=st: e.transpose(out=pb[:, 0:16], in_=GATES[:, st * 128:(st + 1) * 128], identity=ident[0:16, 0:16]), [GATES, ident], [pb])
                P.op("dve", lambda e, pa=pa, st=st: e.tensor_copy(out=IDXT[:, st, :], in_=pa[:, 0:16]), [pa], [IDXT])
                P.op("act", lambda e, pb=pb, st=st: e.copy(out=GATET[:, st, :], in_=pb[:, 0:16]), [pb], [GATET])
            P.barrier()
        if C.get("dbgname") == "IDXT":
            with ExitStack() as pz:
                Tf = kb.sb(pz, "idxf", [128, 64])
                P.op("dve", lambda e: e.tensor_copy(out=Tf[:, :], in_=IDXT[:, :, :].rearrange("p a b -> p (a b)")), [IDXT], [Tf])
                dbg_dump(C, "IDXT", Tf, Tf[:, :], 128, 64)
        dbg_dump(C, "GATET", GATET, GATET[:, :, :].rearrange("p a b -> p (a b)"), 128, 64)
        if stop_after == f"E{l}":
            return
        with ExitStack() as pf:
            XS = [kb.sb(pf, f"fXS{i}", [128, D]) for i in range(8)]
            XT = kb.sb(pf, "fXT", [128, 8, 512], BF16)
            HID = kb.sb(pf, "fHID", [128, 8, 512], BF16)
            SG = [kb.sb(pf, f"fSG{i}", [128, 512]) for i in range(2)]
            YY = [kb.sb(pf, f"fY{i}", [128, D]) for i in range(3)]

            def gather(e_):
                for st in range(4):
                    xs = XS[(e_ % 2) * 4 + st]
                    P.dma("pool", lambda e, xs=xs, st=st: e.indirect_dma_start(out=xs[:, :], out_offset=None, in_=h1_tok.ap[:, :], in_offset=bass.IndirectOffsetOnAxis(ap=IDXT[:, st, e_:e_ + 1], axis=0)), [h1_tok, IDXT], [xs])

            gather(0)
            n = 0
            for e_ in range(16):
                i2 = e_ % 2
                xs4 = [XS[i2 * 4 + st] for st in range(4)]
                if e_ + 1 < 16:
                    gather(e_ + 1)
                for k in range(8):
                    ps = kb.psum[k % 2]
                    for st in range(4):
                        P.op("pe", lambda e, ps=ps, st=st, k=k: e.transpose(out=ps[:, st * 128:(st + 1) * 128], in_=xs4[st][:, k * 128:(k + 1) * 128], identity=ident[:, :]), [xs4[st], ident], [ps])
                    copy_on(P, kb.alt(), XT, XT[:, k, :], ps, ps[:, :])
                for f in range(8):
                    if e_ + 1 < 16:
                        if f >= 1:
                            cast_part(e_ + 1, f - 1)
                        load_part(e_ + 1, f)
                    pg, pu = kb.psum[2 + 2 * (f % 2)], kb.psum[3 + 2 * (f % 2)]
                    for k in range(8):
                        P.op("pe", lambda e, pg=pg, f=f, k=k: e.matmul(pg[:, :], lhsT=Wg[i2][:, k, f * 128:(f + 1) * 128], rhs=XT[:, k, :], start=(k == 0), stop=(k == 7)), [Wg[i2], XT], [pg])
                    for k in range(8):
                        P.op("pe", lambda e, pu=pu, f=f, k=k: e.matmul(pu[:, :], lhsT=Wu[i2][:, k, f * 128:(f + 1) * 128], rhs=XT[:, k, :], start=(k == 0), stop=(k == 7)), [Wu[i2], XT], [pu])
                    sg = SG[f % 2]
                    P.op("act", lambda e, sg=sg, pg=pg: e.activation(out=sg[:, :], in_=pg[:, :], func=AF.Silu), [pg], [sg])
                    P.op("dve", lambda e, sg=sg, pu=pu, f=f: e.tensor_tensor(out=HID[:, f, :], in0=sg[:, :], in1=pu[:, :], op=ALU.mult), [sg, pu], [HID])
                if e_ + 1 < 16:
                    cast_part(e_ + 1, 7)
                for st in range(4):
                    yy = YY[(4 * e_ + st) % 3]
                    for half in range(2):
                        py = kb.psum[6 + half]
                        for f in range(8):
                            P.op("pe", lambda e, py=py, st=st, f=f, half=half: e.matmul(py[:, :], lhsT=HID[:, f, st * 128:(st + 1) * 128], rhs=Wd[i2][:, f, half * 512:(half + 1) * 512], start=(f == 0), stop=(f == 7)), [HID, Wd[i2]], [py])
                        P.op("dve" if half else "act", (lambda e, py=py, yy=yy, st=st, half=half: e.tensor_scalar(out=yy[:, half * 512:(half + 1) * 512], in0=py[:, :], scalar1=GATET[:, st, e_:e_ + 1], scalar2=None, op0=ALU.mult)) if half else
                             (lambda e, py=py, yy=yy, st=st, half=half: e.activation(out=yy[:, half * 512:(half + 1) * 512], in_=py[:, :], func=AF.Copy, scale=GATET[:, st, e_:e_ + 1])), [py, GATET], [yy])
                    P.dma("pool", lambda e, yy=yy, st=st: e.indirect_dma_start(out=moe_acc.ap[:, :], out_offset=bass.IndirectOffsetOnAxis(ap=IDXT[:, st, e_:e_ + 1], axis=0), in_=yy[:, :], in_offset=None, compute_op=ALU.add), [yy, IDXT], [moe_acc], waw=(st == 0))
            P.barrier()
        if stop_after == f"F{l}":
            return
        with ExitStack() as pg_:
            G2 = kb.sb(pg_, "gG2", [128, D]); B2 = kb.sb(pg_, "gB2", [128, D])
            ld(P, G2, G2[:, :], EXT["ln2_g"], EXT["ln2_g"].ap[l:l + 1, :].partition_broadcast(128))
            ld(P, B2, B2[:, :], EXT["ln2_b"], EXT["ln2_b"].ap[l:l + 1, :].partition_broadcast(128))
            Xs = [kb.sb(pg_, f"gX{i}", [128, D]) for i in range(5)]
            Ms = [kb.sb(pg_, f"gM{i}", [128, D]) for i in range(3)]
            stg = [kb.sb(pg_, f"gstg{i}", [128, 8, 128], BF16) for i in range(2)]
            scr = ln_scr(kb, pg_, 4)

            def g0(ti):
                X, M = Xs[ti % 5], Ms[ti % 3]
                ld(P, X, X[:, :], h1_tok, h1_tok.ap[ti * 128:(ti + 1) * 128, :])
                ld(P, M, M[:, :], moe_acc, moe_acc.ap[ti * 128:(ti + 1) * 128, :])

            def g1(ti):
                X, M = Xs[ti % 5], Ms[ti % 3]
                P.op("dve", lambda e: e.scalar_tensor_tensor(out=X[:, :], in0=X[:, :], scalar=float(DN_ALPHA), in1=M[:, :], op0=ALU.mult, op1=ALU.add), [X, M], [X])
                ln_a(P, X, scr[ti % 4])

            def g2(ti):
                ln_b(P, Xs[ti % 5], G2, B2, scr[ti % 4])

            def g3(ti):
                X = Xs[ti % 5]
                if l == DEPTH - 1:
                    P.dma("sp", lambda e: e.dma_start(out=C["out_d"].ap[ti * 128:(ti + 1) * 128, :], in_=X[:, :]), [X], [C["out_d"]], waw=False)
                else:
                    P.dma("sp", lambda e: e.dma_start(out=h_tok.ap[ti * 128:(ti + 1) * 128, :], in_=X[:, :]), [X], [h_tok], waw=False)
                    transpose_to_hT(P, kb, X, ident, stg[ti % 2], C["hT_d"], ti)

            pipeline([g0, g1, g2, g3], NT)
            P.barrier()


def phase_attn(C, l):
    P, kb, EXT, projT = C["P"], C["kb"], C["EXT"], C["projT"]
    with ExitStack() as ph:
        gain = load_cols(P, kb, ph, "mixg", EXT["mix_norm_g"], EXT["mix_norm_g"].ap[l, :], 8)
        COS = kb.sb(ph, "aCOS", [128, L]); SIN = kb.sb(ph, "aSIN", [128, L])
        ld(P, COS, COS[:, :], EXT["ropeC"], EXT["ropeC"].ap.ap())
        ld(P, SIN, SIN[:, :], EXT["ropeS"], EXT["ropeS"].ap.ap())
        Rm = kb.sb(ph, "aRm", [128, 128], BF16); blk = kb.sb(ph, "ablk", [128, 128])
        ld(P, Rm, Rm[:, :], EXT["Rm"], EXT["Rm"].ap.ap())
        ld(P, blk, blk[:, :], EXT["blk64"], EXT["blk64"].ap.ap())
        onesrow = kb.sb(ph, "aones", [128, 128])
        P.op("pool", lambda e: e.memset(onesrow[:, :], 1.0), [], [onesrow])
        gq = kb.sb(ph, "agq", [128, 1]); gk = kb.sb(ph, "agk", [128, 1])
        for hh in range(2):
            P.dma("sp", lambda e, hh=hh: e.dma_start(out=gq[hh * 64:(hh + 1) * 64, :], in_=EXT["q_norm_g"].ap[l, :].rearrange("(p o) -> p o", o=1)), [EXT["q_norm_g"]], [gq], waw=False)
            P.dma("sp", lambda e, hh=hh: e.dma_start(out=gk[hh * 64:(hh + 1) * 64, :], in_=EXT["k_norm_g"].ap[l, :].rearrange("(p o) -> p o", o=1)), [EXT["k_norm_g"]], [gk], waw=False)
        P.op("dve", lambda e: e.tensor_scalar(out=gq[:, :], in0=gq[:, :], scalar1=0.125, scalar2=None, op0=ALU.mult), [gq], [gq])
        KT = [kb.sb(ph, f"aKT{i}", [128, L], BF16) for i in range(2)]
        QT = [kb.sb(ph, f"aQT{i}", [128, L], BF16) for i in range(2)]
        with ExitStack() as ph2:
            RAW = kb.sb(ph2, "aRAW", [128, L])
            SQ = kb.sb(ph2, "aSQ", [128, 512]); R = kb.sb(ph2, "aR", [128, 512]); XN = kb.sb(ph2, "aXN", [128, 512])
            XNb = kb.sb(ph2, "aXNb", [128, 512], BF16); T1 = kb.sb(ph2, "aT1", [128, 512]); T2 = kb.sb(ph2, "aT2", [128, 512])
            for which in range(4):
                isq = which >= 2
                c = which % 2
                dst = QT[c] if isq else KT[c]
                g = gq if isq else gk
                if isq:
                    ld(P, RAW, RAW[:, :], projT, projT.ap[1536 + c * 128:1536 + (c + 1) * 128, :])
                else:
                    for hh in range(2):
                        ld(P, RAW, RAW[hh * 64:(hh + 1) * 64, :], projT, projT.ap[1792 + c * 64:1792 + (c + 1) * 64, :])
                for tc in range(8):
                    sl = slice(tc * 512, (tc + 1) * 512)
                    ps = kb.psum[tc % 2]; pr = kb.psum[2 + tc % 2]
                    P.op("act", lambda e: e.activation(out=SQ[:, :], in_=RAW[:, sl], func=AF.Square), [RAW], [SQ])
                    P.op("pe", lambda e: e.matmul(ps[:, :], lhsT=blk[:, :], rhs=SQ[:, :], start=True, stop=True), [blk, SQ], [ps])
                    P.op("dve", lambda e: e.tensor_scalar(out=R[:, :], in0=ps[:, :], scalar1=1e-6, scalar2=None, op0=ALU.add), [ps], [R])
                    P.op("act", lambda e: e.activation(out=R[:, :], in_=R[:, :], func=AF.Sqrt), [R], [R])
                    P.op("dve", lambda e: e.reciprocal(out=R[:, :], in_=R[:, :]), [R], [R])
                    P.op("dve", lambda e: e.scalar_tensor_tensor(out=XN[:, :], in0=RAW[:, sl], scalar=g[:, 0:1], in1=R[:, :], op0=ALU.mult, op1=ALU.mult), [RAW, g, R], [XN])
                    P.op("act", lambda e: e.copy(out=XNb[:, :], in_=XN[:, :]), [XN], [XNb])
                    P.op("pe", lambda e: e.matmul(pr[:, :], lhsT=Rm[:, :], rhs=XNb[:, :], start=True, stop=True), [Rm, XNb], [pr])
                    P.op("pool", lambda e: e.tensor_tensor(out=T1[:, :], in0=XN[:, :], in1=COS[:, sl], op=ALU.mult), [XN, COS], [T1])
                    P.op("dve", lambda e: e.tensor_tensor(out=T2[:, :], in0=pr[:, :], in1=SIN[:, sl], op=ALU.mult), [pr, SIN], [T2])
                    P.op("pool", lambda e: e.tensor_tensor(out=dst[:, sl], in0=T1[:, :], in1=T2[:, :], op=ALU.add), [T1, T2], [dst])
            P.barrier()
        KTz = [[kb.sb(ph, f"aKTz{c}{g}", [128, L], BF16) for g in range(2)] for c in range(2)]
        for c in range(2):
            for g in range(2):
                P.op("pool", lambda e, c=c, g=g: e.memset(KTz[c][g][:, :], 0.0), [], [KTz[c][g]])
                P.op("pool", lambda e, c=c, g=g: e.tensor_copy(out=KTz[c][g][g * 64:(g + 1) * 64, :], in_=KT[c][g * 64:(g + 1) * 64, :]), [KT[c]], [KTz[c][g]])
        dbg_dump(C, "KT0", KT[0], KT[0][:, :], 128, L)
        dbg_dump(C, "QT0", QT[0], QT[0][:, :], 128, L)
        VE = [kb.sb(ph, f"aVE{i}", [128, NT, 65], BF16) for i in range(2)]
        VO = [kb.sb(ph, f"aVO{i}", [128, NT, 128], BF16) for i in range(2)]
        ATT = kb.sb(ph, "aATT", [128, 2, L])
        with ExitStack() as ph2:
            VR = kb.sb(ph2, "aVR", [128, NT, 128])
            ld(P, VR, VR[:, :, :], C["v_tok"], C["v_tok"].ap.ap().rearrange("(t p) c -> p t c", p=128))
            for kv in range(2):
                P.op("pool", lambda e, kv=kv: e.memset(VE[kv][:, :, :], 1.0), [], [VE[kv]])
                P.op("pool", lambda e, kv=kv: e.memset(VO[kv][:, :, :], 0.0), [], [VO[kv]])
                P.op("pool", lambda e, kv=kv: e.memset(VO[kv][:, :, 0:1], 1.0), [], [VO[kv]])
                P.op("dve", lambda e, kv=kv: e.tensor_copy(out=VE[kv][:, :, 0:64], in_=VR[:, :, kv * 64:(kv + 1) * 64]), [VR], [VE[kv]])
                P.op("dve", lambda e, kv=kv: e.tensor_copy(out=VO[kv][:, :, 64:128], in_=VR[:, :, kv * 64:(kv + 1) * 64]), [VR], [VO[kv]])
            P.barrier()
        PT = [kb.sb(ph, f"aPT{i}", [128, 512], BF16) for i in range(4)]
        RS = [kb.sb(ph, f"aRS{i}", [128, 512]) for i in range(2)]; BC = [kb.sb(ph, f"aBC{i}", [128, 512]) for i in range(2)]
        steps = [(h, qc, s_) for h in range(4) for qc in range(8) for s_ in range(NT)]
        LOOK = 3

        def s_issue(i):
            h, qc, s_ = steps[i]
            kv, g = h // 2, h % 2
            lo, hi = g * 64, (g + 1) * 64
            qs = slice(qc * 512, (qc + 1) * 512)
            pss = kb.psum[i % 4]; pt = PT[i % 4]
            P.op("pe", lambda e: e.matmul(pss[:, :], lhsT=KTz[kv][g][:, s_ * 128:(s_ + 1) * 128], rhs=QT[kv][:, qs], start=True, stop=True), [KTz[kv][g], QT[kv]], [pss])
            P.op("act", lambda e: e.activation(out=pt[:, :], in_=pss[:, :], func=AF.Exp), [pss], [pt])

        def pv_issue(i):
            h, qc, s_ = steps[i]
            kv, g = h // 2, h % 2
            lo, hi = g * 64, (g + 1) * 64
            qs = slice(qc * 512, (qc + 1) * 512)
            n = h * 8 + qc
            acc = kb.psum[4 + (n % 2)]; pt = PT[i % 4]
            if g == 0:
                P.op("pe", lambda e: e.matmul(acc[0:65, :], lhsT=VE[kv][:, s_, :], rhs=pt[:, :], start=(s_ == 0), stop=(s_ == NT - 1)), [VE[kv], pt], [acc])
            else:
                P.op("pe", lambda e: e.matmul(acc[:, :], lhsT=VO[kv][:, s_, :], rhs=pt[:, :], start=(s_ == 0), stop=(s_ == NT - 1)), [VO[kv], pt], [acc])
            if s_ == NT - 1:
                p0 = 64 if g == 0 else 0
                rs, bc = RS[n % 2], BC[n % 2]
                P.op("dve", lambda e: e.reciprocal(out=rs[p0:p0 + 1, :], in_=acc[p0:p0 + 1, :]), [acc], [rs])
                pb = kb.psum[6 + (n % 2)]
                P.op("pe", lambda e: e.matmul(pb[:, :], lhsT=onesrow[p0:p0 + 1, :], rhs=rs[p0:p0 + 1, :], start=True, stop=True), [onesrow, rs], [pb])
                P.op("dve", lambda e: e.tensor_copy(out=bc[lo:hi, :], in_=pb[lo:hi, :]), [pb], [bc])
                P.op("dve", lambda e: e.tensor_tensor(out=ATT[lo:hi, kv, qs], in0=acc[lo:hi, :], in1=bc[lo:hi, :], op=ALU.mult), [acc, bc], [ATT])

        for i in range(len(steps) + LOOK):
            if i < len(steps):
                s_issue(i)
            if i - LOOK >= 0:
                pv_issue(i - LOOK)
        dbg_dump(C, "ATT0", ATT, ATT[:, 0, :], 128, L)
        scr = rms_scratch(kb, ph)
        for tc in range(8):
            sl = slice(tc * 512, (tc + 1) * 512)
            group_rms(C, scr, [(ATT, ATT[:, 0, sl]), (ATT, ATT[:, 1, sl])], gain, 6, 768, tc)
        P.barrier()


def build(stop_after=None, dbg=None):
    nc = bass.Bass("TRN2", target_bir_lowering=False)
    st = ExitStack()
    with st:
        P = Prog(nc, st)
        kb = KB(nc, st, P)
        x_d = kb.dram_in("x", [L, D])
        ln_in_g = kb.dram_in("ln_in_g", [1, D]); ln_in_b = kb.dram_in("ln_in_b", [1, D])
        w_in = kb.dram_in("w_in", [DEPTH, D, 2048])
        ident_d = kb.dram_in("ident", [128, 128])
        EXT = {}
        for nm, shp in [("conv_dw_w", [DEPTH, 31, 256]), ("conv_dw_b", [DEPTH, 256]), ("conv_ln_g", [DEPTH, 256]), ("conv_ln_b", [DEPTH, 256]),
                        ("mix_norm_g", [DEPTH, 1024]), ("q_norm_g", [DEPTH, 64]), ("k_norm_g", [DEPTH, 64]),
                        ("s5_lam_re", [DEPTH, 2, 1024]), ("s5_lam_im", [DEPTH, 2, 1024]), ("s5_log_step", [DEPTH, 2, 16]),
                        ("s5_b_re", [DEPTH, 2, 16, 64, 16]), ("s5_b_im", [DEPTH, 2, 16, 64, 16]),
                        ("s5_c_re", [DEPTH, 2, 16, 16, 64]), ("s5_c_im", [DEPTH, 2, 16, 16, 64]),
                        ("s5_d", [DEPTH, 256]), ("s5_w_glu", [DEPTH, 256, 512]), ("s5_b_glu", [DEPTH, 512]), ("iota512", [128, 512]),
                        ("hy_short_w", [DEPTH, 3, 768]), ("hy_short_b", [DEPTH, 768]), ("hy_f_w1", [DEPTH, 33, 64]), ("hy_f_b1", [DEPTH, 64]),
                        ("hy_f_freq", [DEPTH, 64]), ("hy_f_w2", [DEPTH, 64, 64]), ("hy_f_b2", [DEPTH, 64]), ("hy_f_w3", [DEPTH, 64, 1024]),
                        ("hy_bias", [DEPTH, 2, 256]), ("hy_zT", [33, L]), ("hy_win", [L, 256]), ("hy_ch", [128, 32]), ("hy_sh", [128, 32]), ("hy_sgn", [128, 1]), ("hy_J", [128, 128]),
                        ("w_out", [DEPTH, D, D]), ("ln1_g", [DEPTH, D]), ("ln1_b", [DEPTH, D]), ("ln2_g", [DEPTH, D]), ("ln2_b", [DEPTH, D]),
                        ("router_w", [DEPTH, D, 16]), ("router_b", [DEPTH, 16]),
                        ("exp_w_gate", [DEPTH, 16, D, D]), ("exp_w_up", [DEPTH, 16, D, D]), ("exp_w_down", [DEPTH, 16, D, D]),
                        ("ropeC", [128, L]), ("ropeS", [128, L]), ("blk64", [128, 128])]:
            EXT[nm] = kb.dram_in(nm, shp)
        EXT["Rm"] = kb.dram_in("Rm", [128, 128], BF16)
        EXT["dftC"] = kb.dram_in("dftC", [32, 128, 32 * 128], BF16)
        EXT["dftS"] = kb.dram_in("dftS", [32, 128, 32 * 128], BF16)
        EXT["dftC0"] = kb.dram_in("dftC0", [32, 128, 32 * 128], BF16)
        EXT["dftS0"] = kb.dram_in("dftS0", [32, 128, 32 * 128], BF16)
        h1_tok = kb.dram("h1_tok", [L, D]); moe_acc = kb.dram("moe_acc", [L, D])
        C_hy_tok = kb.dram("hy_tok", [L, 768]); C_z1_tok = kb.dram("z1_tok", [L, 256]); C_k_d = kb.dram("hy_kd", [32, 128, 2, 512])
        mergedT = kb.dram("mergedT", [D, L], BF16)
        out_d = kb.dram("out", [L, D], F32, kind="ExternalOutput")
        h_tok = kb.dram("h_tok", [L, D])
        hT_d = kb.dram("hT_d", [D, L], BF16)
        projT = kb.dram("projT", [2048, L])
        v_tok = kb.dram("v_tok", [L, 128])
        dbg_d = None
        if dbg is not None:
            dbg_d = kb.dram("dbg", dbg[1], F32, kind="ExternalOutput")

        glob = ExitStack()
        st.enter_context(glob)
        ident = kb.sb(glob, "ident", [128, 128])
        ld(P, ident, ident[:, :], ident_d, ident_d.ap.ap())
        ones256 = kb.sb(glob, "ones256", [128, 128])
        P.op("pool", lambda e: e.memset(ones256[:, :], 1.0 / 256.0), [], [ones256])
        C = dict(dbgname=(dbg[0] if dbg else None), dbg_d=dbg_d, nc=nc, P=P, kb=kb, EXT=EXT, projT=projT, mergedT=mergedT, ident=ident, ones256=ones256, v_tok=v_tok, h_tok=h_tok, hT_d=hT_d, h1_tok=h1_tok, moe_acc=moe_acc, out_d=out_d, hy_tok=C_hy_tok, z1_tok=C_z1_tok, k_d=C_k_d)

        with ExitStack() as ph:
            G = kb.sb(ph, "G", [128, D]); Bt = kb.sb(ph, "Bt", [128, D])
            ld(P, G, G[:, :], ln_in_g, ln_in_g.ap.ap().partition_broadcast(128))
            ld(P, Bt, Bt[:, :], ln_in_b, ln_in_b.ap.ap().partition_broadcast(128))
            Xs = [kb.sb(ph, f"X{i}", [128, D]) for i in range(5)]
            stg = [kb.sb(ph, f"stg{i}", [128, 8, 128], BF16) for i in range(2)]
            scr = ln_scr(kb, ph, 4)

            def a0(ti):
                X = Xs[ti % 5]
                ld(P, X, X[:, :], x_d, x_d.ap[ti * 128:(ti + 1) * 128, :])

            def a1(ti):
                ln_a(P, Xs[ti % 5], scr[ti % 4])

            def a2(ti):
                ln_b(P, Xs[ti % 5], G, Bt, scr[ti % 4])

            def a3(ti):
                X = Xs[ti % 5]
                P.dma("sp", lambda e: e.dma_start(out=h_tok.ap[ti * 128:(ti + 1) * 128, :], in_=X[:, :]), [X], [h_tok], waw=False)
                transpose_to_hT(P, kb, X, ident, stg[ti % 2], hT_d, ti)

            pipeline([a0, a1, a2, a3], NT)
            P.barrier()
        if stop_after == "A":
            return finish(nc, P, kb, dbg, dbg_d, locals())

        for l in range(DEPTH):
            with ExitStack() as ph:
                W = kb.sb(ph, "W", [128, 8, 2048], BF16)
                for k in range(8):
                    P.dma("pool", lambda e, k=k: e.dma_start(out=W[:, k, :], in_=w_in.ap[l, k * 128:(k + 1) * 128, :]), [w_in], [W], waw=False)
                HT = [kb.sb(ph, f"HT{i}", [128, 8, 512], BF16) for i in range(2)]
                OS = [kb.sb(ph, f"OS{i}", [128, 512]) for i in range(4)]
                VS = [kb.sb(ph, f"VS{i}", [128, 128]) for i in range(2)]
                n = 0
                for tc in range(8):
                    H = HT[tc % 2]
                    ld(P, H, H[:, :, :], hT_d, hT_d.ap.ap().rearrange("(k p) t -> p k t", p=128)[:, :, tc * 512:(tc + 1) * 512])
                    for fc in range(16):
                        if fc == 15:
                            continue
                        ps = kb.psum[n % 4]; O = OS[n % 4]; n += 1
                        for k in range(8):
                            P.op("pe", lambda e, k=k, fc=fc, ps=ps, H=H: e.matmul(ps[:, :], lhsT=W[:, k, fc * 128:(fc + 1) * 128], rhs=H[:, k, :], start=(k == 0), stop=(k == 7)), [W, H], [ps])
                        copy_on(P, kb.alt(), O, O[:, :], ps, ps[:, :])
                        P.dma("sp", lambda e, O=O, fc=fc, tc=tc: e.dma_start(out=projT.ap[fc * 128:(fc + 1) * 128, tc * 512:(tc + 1) * 512], in_=O[:, :]), [O], [projT], waw=False)
                    for tt in range(4):
                        ps = kb.psum[4 + (tt % 2)]; V = VS[tt % 2]
                        for k in range(8):
                            P.op("pe", lambda e, k=k, tt=tt, ps=ps, H=H: e.matmul(ps[:, 0:128], lhsT=H[:, k, tt * 128:(tt + 1) * 128], rhs=W[:, k, 1920:2048], start=(k == 0), stop=(k == 7)), [W, H], [ps])
                        copy_on(P, kb.alt(), V, V[:, :], ps, ps[:, 0:128])
                        ti = tc * 4 + tt
                        P.dma("sp", lambda e, V=V, ti=ti: e.dma_start(out=v_tok.ap[ti * 128:(ti + 1) * 128, :], in_=V[:, :]), [V], [v_tok], waw=False)
                P.barrier()
            if stop_after == f"B{l}":
                return finish(nc, P, kb, dbg, dbg_d, locals())
            phase_conv(C, l)
            if stop_after == f"C1{l}":
                return finish(nc, P, kb, dbg, dbg_d, locals())
            if not os.environ.get("SKIP_ATTN"):
                phase_attn(C, l)
            if not os.environ.get("SKIP_S5"):
                phase_s5(C, l)
            if not os.environ.get("SKIP_HY"):
                phase_hyena(C, l)
            if stop_after == f"C2{l}":
                return finish(nc, P, kb, dbg, dbg_d, locals())
            phase_moe(C, l, stop_after)
            if stop_after in (f"D{l}", f"E{l}", f"F{l}", f"G{l}"):
                return finish(nc, P, kb, dbg, dbg_d, locals())
            if stop_after == f"C3{l}":
                return finish(nc, P, kb, dbg, dbg_d, locals())
            if stop_after == f"C4{l}":
                return finish(nc, P, kb, dbg, dbg_d, locals())
        return finish(nc, P, kb, dbg, dbg_d, locals())


def finish(nc, P, kb, dbg, dbg_d, env):
    if dbg is not None and dbg[0] in env:
        src = env[dbg[0]]
        P.barrier()
        rows = dbg[1][0]
        with ExitStack() as ph:
            T = kb.sb(ph, "dbgT", [128, dbg[1][1]], src.ap.dtype if hasattr(src.ap, "dtype") else F32)
            T2 = kb.sb(ph, "dbgT2", [128, dbg[1][1]])
            for r in range(0, rows, 128):
                ld(P, T, T[:, :], src, src.ap[r:r + 128, :])
                P.op("dve", lambda e: e.tensor_copy(out=T2[:, :], in_=T[:, :]), [T], [T2])
                P.dma("sp", lambda e, r=r: e.dma_start(out=dbg_d.ap[r:r + 128, :], in_=T2[:, :]), [T2], [dbg_d], waw=False)
            P.barrier()
    P.barrier(engines=["sp"])
    print("ninstr", P.ninstr, "nsem", P.nsem)
    return nc


def host_consts():
    c = {}
    c["ident"] = np.eye(128, dtype=np.float32)
    t = np.arange(L)
    row = (t // 64).astype(np.float32); col = (t % 64).astype(np.float32)
    inv = (np.float32(10000.0) ** (-np.arange(0, 32, 2, dtype=np.float32) / np.float32(32))).astype(np.float32)
    ang = np.zeros((64, L), np.float32)
    for d in range(64):
        i = d % 16
        ang[d] = (row if d < 32 else col) * inv[i]
    c["ropeC"] = np.ascontiguousarray(np.tile(np.cos(ang), (2, 1)).astype(np.float32))
    c["ropeS"] = np.ascontiguousarray(np.tile(np.sin(ang), (2, 1)).astype(np.float32))
    Rm = np.zeros((128, 128), np.float32)
    for m in range(128):
        if m % 32 < 16:
            Rm[m + 16, m] = -1.0
        else:
            Rm[m - 16, m] = 1.0
    c["Rm"] = Rm.astype(ml_dtypes.bfloat16)
    blk = np.zeros((128, 128), np.float32); blk[:64, :64] = 1 / 64; blk[64:, 64:] = 1 / 64
    c["blk64"] = blk
    Nf = 2 * L
    n = np.arange(L, dtype=np.float64)
    ang = 2.0 * np.pi * np.outer(n + 0.5, n + 0.5) / Nf
    for nm, fn in (("dftC", np.cos), ("dftS", np.sin)):
        M = fn(ang).astype(np.float32).astype(ml_dtypes.bfloat16)
        c[nm] = np.ascontiguousarray(M.reshape(32, 128, 32, 128).transpose(2, 1, 0, 3)).reshape(32, 128, 32 * 128)
    ang0 = 2.0 * np.pi * np.outer(n, n + 0.5) / Nf
    for nm, fn in (("dftC0", np.cos), ("dftS0", np.sin)):
        M = fn(ang0).astype(np.float32).astype(ml_dtypes.bfloat16)
        c[nm] = np.ascontiguousarray(M.reshape(32, 128, 32, 128).transpose(2, 1, 0, 3)).reshape(32, 128, 32 * 128)
    del ang, ang0
    c["hy_sgn"] = np.ascontiguousarray(((-1.0) ** np.arange(128)).astype(np.float32).reshape(128, 1))
    c["hy_J"] = np.ascontiguousarray(np.eye(128, dtype=np.float32)[::-1])
    phi_half = np.pi * (n + 0.5) / Nf
    c["hy_ch"] = np.ascontiguousarray(np.cos(phi_half).reshape(32, 128).T.astype(np.float32))
    c["hy_sh"] = np.ascontiguousarray(np.sin(phi_half).reshape(32, 128).T.astype(np.float32))
    t = np.linspace(0.0, 1.0, L, dtype=np.float32)[:, None]
    w = (np.float32(2.0 * math.pi / L)) * np.arange(L, dtype=np.float32)
    f = np.linspace(1e-4, 15, 16, dtype=np.float32)
    angz = w[:, None] * f[None, :]
    z = np.concatenate([t, np.cos(angz), -np.sin(angz)], axis=-1).astype(np.float32)
    c["hy_zT"] = np.ascontiguousarray(z.T)
    max_decay = math.log(1e-2) / 0.3; min_decay = math.log(1e-2) / 1.5
    deltas = np.linspace(min_decay, max_decay, 256, dtype=np.float32)
    c["hy_win"] = np.ascontiguousarray((np.exp(-t * np.abs(deltas)[None, :]) + np.float32(0.05)).astype(np.float32))
    c["iota512"] = np.ascontiguousarray(np.tile(np.arange(512, dtype=np.float32), (128, 1)))
    return c


def make_in_maps(inputs, cores):
    c = host_consts()
    maps = []
    for b in cores:
        m = dict(c)
        m["x"] = np.ascontiguousarray(inputs["x"][b])
        m["ln_in_g"] = np.ascontiguousarray(inputs["ln_in_g"]).reshape(1, D)
        m["ln_in_b"] = np.ascontiguousarray(inputs["ln_in_b"]).reshape(1, D)
        m["w_in"] = np.ascontiguousarray(inputs["w_in"])
        for nm in ["conv_dw_w", "conv_dw_b", "conv_ln_g", "conv_ln_b", "mix_norm_g", "q_norm_g", "k_norm_g", "s5_log_step",
                   "s5_b_re", "s5_b_im", "s5_c_re", "s5_c_im", "s5_d", "s5_w_glu", "s5_b_glu",
                   "hy_short_w", "hy_short_b", "hy_f_w1", "hy_f_b1", "hy_f_freq", "hy_f_w2", "hy_f_b2", "hy_f_w3", "hy_bias",
                   "w_out", "ln1_g", "ln1_b", "ln2_g", "ln2_b", "router_w", "router_b", "exp_w_gate", "exp_w_up", "exp_w_down"]:
            m[nm] = np.ascontiguousarray(inputs[nm])
        m["s5_lam_re"] = np.ascontiguousarray(inputs["s5_lam_re"]).reshape(DEPTH, 2, 1024)
        m["s5_lam_im"] = np.ascontiguousarray(inputs["s5_lam_im"]).reshape(DEPTH, 2, 1024)
        maps.append(m)
    return maps


def kernel(**inputs):
    nc = build()
    in_maps = make_in_maps(inputs, list(range(8)))
    res = run_bass_kernel_spmd(nc, in_maps, core_ids=list(range(8)))
    return np.stack([r["out"] for r in res.results], axis=0)
```

```python
import math
import os
from contextlib import ExitStack

import numpy as np
import ml_dtypes

import concourse.bass as bass
import concourse.mybir as mybir
from concourse.bass_utils import run_bass_kernel_spmd

F32 = mybir.dt.float32
BF16 = mybir.dt.bfloat16
I32 = mybir.dt.int32
U32 = mybir.dt.uint32
AF = mybir.ActivationFunctionType
ALU = mybir.AluOpType
AX = mybir.AxisListType

L = 4096
D = 1024
NT = L // 128
DEPTH = 2
DN_ALPHA = (2 * DEPTH) ** 0.25
SEM_EPOCH = 12000
DMA_EPOCH = 1500


class Buf:
    def __init__(self, name, ap=None):
        self.name = name
        self.ap = ap
        self.last_w = None
        self.readers = {}
        self.dsem = None
        self.dcnt = 0
        self.ddone = 0
        self.psum = False

    def __getitem__(self, idx):
        return self.ap[idx]


class Prog:
    def __init__(self, nc, stack):
        self.nc = nc
        self.stack = stack
        self.eng = {"pe": nc.tensor, "dve": nc.vector, "act": nc.scalar,
                    "pool": nc.gpsimd, "sp": nc.sync}
        self.sem = {}
        self.cnt = {}
        self.nsem = 0
        self.sem_owner = {}
        for e in self.eng:
            self._new_eng_sem(e)
        self.seen = {e: {} for e in self.eng}
        self.all_dma_bufs = []
        self.final = {}
        self.old_dma = []
        self.ninstr = {e: 0 for e in self.eng}
        self.free_dsems = []

    def _alloc_sem(self, name):
        self.nsem += 1
        return self.stack.enter_context(self.nc.semaphore(f"{name}_{self.nsem}"))

    def _new_eng_sem(self, e):
        self.sem[e] = self._alloc_sem("e" + e)
        self.cnt[e] = 0
        self.sem_owner[id(self.sem[e])] = e

    def _wait(self, e, ev):
        if ev is None:
            return
        s, v = ev
        k = id(s)
        if self.seen[e].get(k, 0) >= v:
            return
        self.seen[e][k] = v
        self.eng[e].wait_ge(s, v)

    def _cur(self, ev):
        if ev is None:
            return None
        s, v, owner = ev
        if owner is not None:
            if owner.dsem is s:
                return (s, owner.dcnt)
            return (s, self.final.get(id(s), v))
        return (s, v)

    def _deps(self, e, reads, writes, waw=True):
        for b in reads:
            self._wait(e, self._cur(b.last_w))
            if b.psum:
                for r in list(b.readers.values()):
                    if self.sem_owner.get(id(r[0])) != e:
                        self._wait(e, self._cur(r))
        for b in writes:
            lst = list(b.readers.values())
            if waw:
                lst.append(b.last_w)
            for r in lst:
                if r is not None and self.sem_owner.get(id(r[0])) == e:
                    continue
                self._wait(e, self._cur(r))

    def _record(self, ev, reads, writes):
        for b in writes:
            b.last_w = ev
            b.readers = {}
        for b in reads:
            if b not in writes:
                b.readers[id(ev[0])] = ev

    def op(self, e, fn, reads=(), writes=()):
        if self.cnt[e] >= SEM_EPOCH:
            self._new_eng_sem(e)
        self._deps(e, reads, writes)
        ins = fn(self.eng[e])
        self.cnt[e] += 1
        ins.then_inc(self.sem[e], 1)
        self._record((self.sem[e], self.cnt[e], None), reads, writes)
        self.ninstr[e] += 1
        return ins

    def dma(self, q, fn, reads, writes, waw=True):
        self._deps(q, reads, writes, waw=waw)
        owner = writes[0]
        if owner.dsem is None or owner.ddone >= DMA_EPOCH:
            if owner.dsem is not None:
                self.final[id(owner.dsem)] = owner.dcnt
                self.old_dma.append((owner.dsem, owner.dcnt))
            if self.free_dsems:
                owner.dsem, owner.dcnt = self.free_dsems.pop()
            else:
                owner.dsem = self._alloc_sem("d")
                owner.dcnt = 0
            owner.ddone = 0
            self.all_dma_bufs.append(owner)
        ins = fn(self.eng[q])
        owner.dcnt += 16
        owner.ddone += 1
        ins.then_inc(owner.dsem, 16)
        self._record((owner.dsem, owner.dcnt, owner), reads, writes)
        self.ninstr[q] += 1
        return ins

    def release(self, buf):
        if buf.dsem is not None:
            self.final[id(buf.dsem)] = buf.dcnt
            if buf.dcnt < 30000:
                self.free_dsems.append((buf.dsem, buf.dcnt))
            else:
                self.old_dma.append((buf.dsem, buf.dcnt))
            if buf in self.all_dma_bufs:
                self.all_dma_bufs = [b for b in self.all_dma_bufs if b is not buf]
            buf.dsem = None

    def barrier(self, engines=None):
        evs = []
        for e in self.eng:
            if self.cnt[e] > 0:
                evs.append((self.sem[e], self.cnt[e]))
        for b in self.all_dma_bufs:
            if b.dsem is not None and b.dcnt > 0:
                evs.append((b.dsem, b.dcnt))
        evs.extend(self.old_dma)
        evs.extend(self.free_dsems)
        for e in (engines or self.eng):
            for ev in evs:
                self._wait(e, ev)


class KB:
    def __init__(self, nc, st, P):
        self.nc = nc
        self.st = st
        self.P = P
        self.uid = 0
        self.psum = [Buf(f"ps{i}", st.enter_context(nc.psum_tensor(f"ps{i}", [128, 512], F32))) for i in range(8)]
        for b in self.psum:
            b.psum = True
        self.rr = 0

    def sb(self, stack, name, shape, dt=F32):
        self.uid += 1
        b = Buf(name, stack.enter_context(self.nc.sbuf_tensor(f"{name}_{self.uid}", shape, dt)))
        stack.callback(lambda: self.P.release(b))
        return b

    def dram_in(self, name, shape, dt=F32):
        return Buf(name, self.nc.dram_tensor(name, shape, dt, kind="ExternalInput"))

    def dram(self, name, shape, dt=F32, kind="Internal"):
        return Buf(name, self.nc.dram_tensor(name, shape, dt, kind=kind))

    def alt(self):
        self.rr += 1
        return "dve" if self.rr % 2 else "act"


def ld(P, dst, dst_ap, src, src_ap, q="sp"):
    P.dma(q, lambda e: e.dma_start(out=dst_ap, in_=src_ap), [src], [dst], waw=False)


def copy_on(P, eng, out_b, out_ap, in_b, in_ap):
    if eng == "pool_via_dve":
        eng = "dve"
    if eng == "act":
        P.op("act", lambda e: e.copy(out=out_ap, in_=in_ap), [in_b], [out_b])
    else:
        P.op(eng, lambda e: e.tensor_copy(out=out_ap, in_=in_ap), [in_b], [out_b])


def rstd_from(P, kb, var_b, var_ap, eps, tmp_b, tmp_ap, out_b, out_ap):
    P.op("dve", lambda e: e.tensor_scalar(out=tmp_ap, in0=var_ap, scalar1=float(eps), scalar2=None, op0=ALU.add), [var_b], [tmp_b])
    P.op("act", lambda e: e.activation(out=tmp_ap, in_=tmp_ap, func=AF.Sqrt), [tmp_b], [tmp_b])
    P.op("dve", lambda e: e.reciprocal(out=out_ap, in_=tmp_ap), [tmp_b], [out_b])


def pipeline(stages, n):
    ns = len(stages)
    for step in range(n + ns - 1):
        for k, fn in enumerate(stages):
            it = step - k
            if 0 <= it < n:
                fn(it)


def ln_a(P, X, scr):
    st6, mv, tmp, rstd = scr
    for c in range(2):
        P.op("dve", lambda e, c=c: e.bn_stats(out=st6[:, c * 6:(c + 1) * 6], in_=X[:, c * 512:(c + 1) * 512]), [X], [st6])
    P.op("dve", lambda e: e.bn_aggr(out=mv[:, :], in_=st6[:, :]), [st6], [mv])
    P.op("dve", lambda e: e.tensor_scalar(out=tmp[:, :], in0=mv[:, 1:2], scalar1=1e-5, scalar2=None, op0=ALU.add), [mv], [tmp])
    P.op("act", lambda e: e.activation(out=tmp[:, :], in_=tmp[:, :], func=AF.Sqrt), [tmp], [tmp])


def ln_b(P, X, G, Bt, scr):
    st6, mv, tmp, rstd = scr
    P.op("dve", lambda e: e.reciprocal(out=rstd[:, :], in_=tmp[:, :]), [tmp], [rstd])
    P.op("dve", lambda e: e.tensor_scalar(out=X[:, :], in0=X[:, :], scalar1=mv[:, 0:1], scalar2=rstd[:, 0:1],
                                          op0=ALU.subtract, op1=ALU.mult), [X, mv, rstd], [X])
    P.op("pool", lambda e: e.tensor_tensor(out=X[:, :], in0=X[:, :], in1=G[:, :], op=ALU.mult), [X, G], [X])
    P.op("pool", lambda e: e.tensor_tensor(out=X[:, :], in0=X[:, :], in1=Bt[:, :], op=ALU.add), [X, Bt], [X])


def ln_scr(kb, ph, n):
    return [(kb.sb(ph, f"st6{i}", [128, 12]), kb.sb(ph, f"mv{i}", [128, 2]), kb.sb(ph, f"tmp{i}", [128, 1]), kb.sb(ph, f"rstd{i}", [128, 1])) for i in range(n)]


def ln_tok_tile(P, kb, X, G, Bt, scr):
    st6, mv, tmp, rstd = scr
    for c in range(2):
        P.op("dve", lambda e, c=c: e.bn_stats(out=st6[:, c * 6:(c + 1) * 6], in_=X[:, c * 512:(c + 1) * 512]), [X], [st6])
    P.op("dve", lambda e: e.bn_aggr(out=mv[:, :], in_=st6[:, :]), [st6], [mv])
    rstd_from(P, kb, mv, mv[:, 1:2], 1e-5, tmp, tmp[:, :], rstd, rstd[:, :])
    P.op("dve", lambda e: e.tensor_scalar(out=X[:, :], in0=X[:, :], scalar1=mv[:, 0:1], scalar2=rstd[:, 0:1],
                                          op0=ALU.subtract, op1=ALU.mult), [X, mv, rstd], [X])
    P.op("pool", lambda e: e.tensor_tensor(out=X[:, :], in0=X[:, :], in1=G[:, :], op=ALU.mult), [X, G], [X])
    P.op("dve", lambda e: e.tensor_tensor(out=X[:, :], in0=X[:, :], in1=Bt[:, :], op=ALU.add), [X, Bt], [X])


def transpose_to_hT(P, kb, X, ident, stage, hT_d, ti):
    for half in range(2):
        ps = kb.psum[(2 * ti + half) % 4]
        for j in range(4):
            k = half * 4 + j
            P.op("pe", lambda e, k=k, j=j, ps=ps: e.transpose(out=ps[:, j * 128:(j + 1) * 128], in_=X[:, k * 128:(k + 1) * 128], identity=ident[:, :]),
                 [X, ident], [ps])
        copy_on(P, kb.alt(), stage, stage[:, half * 4:(half + 1) * 4, :], ps, ps[:, :].rearrange("p (j t) -> p j t", j=4))
    P.dma("sp", lambda e: e.dma_start(out=hT_d.ap.ap().rearrange("(k p) t -> p k t", p=128)[:, :, ti * 128:(ti + 1) * 128], in_=stage[:, :, :]),
          [stage], [hT_d], waw=False)


def load_cols(P, kb, ph, name, src, src_ap_1d, n):
    T = kb.sb(ph, name, [128, n])
    P.dma("sp", lambda e: e.dma_start(out=T[:, :], in_=src_ap_1d.rearrange("(j p) -> p j", p=128), allow_slow_non_contiguous=True), [src], [T], waw=False)
    return T


def group_rms(C, ph_scr, Ys, gain, gcol0, row0, tc):
    P, kb = C["P"], C["kb"]
    SQ, R, OB = ph_scr
    ps = kb.psum[6]
    for c, (yb, yap) in enumerate(Ys):
        P.op("act", lambda e, c=c, yap=yap: e.activation(out=SQ[:, c, :], in_=yap, func=AF.Square), [yb], [SQ])
    for c in range(2):
        P.op("pe", lambda e, c=c: e.matmul(ps[:, :], lhsT=C["ones256"][:, :], rhs=SQ[:, c, :], start=(c == 0), stop=(c == 1)), [C["ones256"], SQ], [ps])
    P.op("dve", lambda e: e.tensor_scalar(out=R[:, :], in0=ps[:, :], scalar1=1e-6, scalar2=None, op0=ALU.add), [ps], [R])
    P.op("act", lambda e: e.activation(out=R[:, :], in_=R[:, :], func=AF.Sqrt), [R], [R])
    P.op("dve", lambda e: e.reciprocal(out=R[:, :], in_=R[:, :]), [R], [R])
    for c, (yb, yap) in enumerate(Ys):
        P.op("dve", lambda e, c=c, yap=yap: e.scalar_tensor_tensor(out=OB[:, c, :], in0=yap, scalar=gain[:, gcol0 + c:gcol0 + c + 1], in1=R[:, :], op0=ALU.mult, op1=ALU.mult), [yb, gain, R], [OB])
    P.dma("sp", lambda e: e.dma_start(out=C["mergedT"].ap.ap()[row0:row0 + 256, :].rearrange("(c p) t -> p c t", p=128)[:, :, tc * 512:(tc + 1) * 512], in_=OB[:, :, :]), [OB], [C["mergedT"]], waw=False)


def rms_scratch(kb, ph):
    return (kb.sb(ph, "rSQ", [128, 2, 512]), kb.sb(ph, "rR", [128, 512]), kb.sb(ph, "rOB", [128, 2, 512], BF16))


def phase_conv(C, l):
    P, kb, EXT, projT = C["P"], C["kb"], C["EXT"], C["projT"]
    with ExitStack() as ph:
        gain = load_cols(P, kb, ph, "mixg", EXT["mix_norm_g"], EXT["mix_norm_g"].ap[l, :], 8)
        lng = load_cols(P, kb, ph, "clng", EXT["conv_ln_g"], EXT["conv_ln_g"].ap[l, :], 2)
        lnb = load_cols(P, kb, ph, "clnb", EXT["conv_ln_b"], EXT["conv_ln_b"].ap[l, :], 2)
        dwb = load_cols(P, kb, ph, "cdwb", EXT["conv_dw_b"], EXT["conv_dw_b"].ap[l, :], 2)
        dww = kb.sb(ph, "cdww", [128, 2, 31])
        for c in range(2):
            P.dma("sp", lambda e, c=c: e.dma_start(out=dww[:, c, :], in_=EXT["conv_dw_w"].ap[l, :, c * 128:(c + 1) * 128].rearrange("k p -> p k"), allow_slow_non_contiguous=True), [EXT["conv_dw_w"]], [dww], waw=False)
        A = kb.sb(ph, "cA", [128, L]); Gt = kb.sb(ph, "cG", [128, L]); HP = kb.sb(ph, "cHP", [128, L + 30])
        CV = [kb.sb(ph, f"cCV{c}", [128, L]) for c in range(2)]
        P.op("pool", lambda e: e.memset(HP[:, :], 0.0), [], [HP])
        for c in range(2):
            ld(P, A, A[:, :], projT, projT.ap[c * 128:(c + 1) * 128, :])
            ld(P, Gt, Gt[:, :], projT, projT.ap[256 + c * 128:256 + (c + 1) * 128, :])
            P.op("act", lambda e: e.activation(out=Gt[:, :], in_=Gt[:, :], func=AF.Sigmoid), [Gt], [Gt])
            P.op("dve", lambda e: e.tensor_tensor(out=HP[:, 15:15 + L], in0=A[:, :], in1=Gt[:, :], op=ALU.mult), [A, Gt], [HP])
            cv = CV[c]
            P.op("dve", lambda e, c=c, cv=cv: e.tensor_scalar(out=cv[:, :], in0=HP[:, 0:L], scalar1=dww[:, c, 0:1], scalar2=dwb[:, c:c + 1], op0=ALU.mult, op1=ALU.add), [HP, dww, dwb], [cv])
            for k in range(1, 31):
                P.op("dve", lambda e, c=c, cv=cv, k=k: e.scalar_tensor_tensor(out=cv[:, :], in0=HP[:, k:k + L], scalar=dww[:, c, k:k + 1], in1=cv[:, :], op0=ALU.mult, op1=ALU.add), [HP, dww, cv], [cv])
        ones = C["ones256"]
        SQs = [kb.sb(ph, f"cSQ{i}", [128, 2, 512]) for i in range(2)]
        MEANs = [kb.sb(ph, f"cMEAN{i}", [128, 512]) for i in range(3)]; VARs = [kb.sb(ph, f"cVAR{i}", [128, 512]) for i in range(3)]
        Tts = [kb.sb(ph, f"cT{i}", [128, 2, 512]) for i in range(2)]; SLs = [kb.sb(ph, f"cSL{i}", [128, 2, 512]) for i in range(2)]
        scrs = [rms_scratch(kb, ph) for _ in range(2)]

        def c0(tc):
            sl = slice(tc * 512, (tc + 1) * 512)
            pm, pe_ = kb.psum[0 + 2 * (tc % 2)], kb.psum[1 + 2 * (tc % 2)]
            SQ = SQs[tc % 2]
            for c in range(2):
                P.op("act", lambda e, c=c: e.activation(out=SQ[:, c, :], in_=CV[c][:, sl], func=AF.Square), [CV[c]], [SQ])
            for c in range(2):
                P.op("pe", lambda e, c=c: e.matmul(pm[:, :], lhsT=ones[:, :], rhs=CV[c][:, sl], start=(c == 0), stop=(c == 1)), [ones, CV[c]], [pm])
            for c in range(2):
                P.op("pe", lambda e, c=c: e.matmul(pe_[:, :], lhsT=ones[:, :], rhs=SQ[:, c, :], start=(c == 0), stop=(c == 1)), [ones, SQ], [pe_])

        def c1(tc):
            pm, pe_ = kb.psum[0 + 2 * (tc % 2)], kb.psum[1 + 2 * (tc % 2)]
            MEAN, VAR = MEANs[tc % 3], VARs[tc % 3]
            P.op("act", lambda e: e.copy(out=MEAN[:, :], in_=pm[:, :]), [pm], [MEAN])
            P.op("act", lambda e: e.activation(out=VAR[:, :], in_=pm[:, :], func=AF.Square), [pm], [VAR])
            P.op("dve", lambda e: e.tensor_tensor(out=VAR[:, :], in0=pe_[:, :], in1=VAR[:, :], op=ALU.subtract), [pe_, VAR], [VAR])
            P.op("dve", lambda e: e.tensor_scalar(out=VAR[:, :], in0=VAR[:, :], scalar1=1e-5, scalar2=None, op0=ALU.add), [VAR], [VAR])
            P.op("act", lambda e: e.activation(out=VAR[:, :], in_=VAR[:, :], func=AF.Sqrt), [VAR], [VAR])

        def c2(tc):
            sl = slice(tc * 512, (tc + 1) * 512)
            MEAN, VAR, Tt, SL = MEANs[tc % 3], VARs[tc % 3], Tts[tc % 2], SLs[tc % 2]
            P.op("dve", lambda e: e.reciprocal(out=VAR[:, :], in_=VAR[:, :]), [VAR], [VAR])
            for c in range(2):
                P.op("pool", lambda e, c=c: e.tensor_tensor(out=Tt[:, c, :], in0=CV[c][:, sl], in1=MEAN[:, :], op=ALU.subtract), [CV[c], MEAN], [Tt])
                P.op("pool", lambda e, c=c: e.tensor_tensor(out=Tt[:, c, :], in0=Tt[:, c, :], in1=VAR[:, :], op=ALU.mult), [Tt, VAR], [Tt])
                P.op("pool", lambda e, c=c: e.tensor_scalar(out=Tt[:, c, :], in0=Tt[:, c, :], scalar1=lng[:, c:c + 1], scalar2=lnb[:, c:c + 1], op0=ALU.mult, op1=ALU.add), [Tt, lng, lnb], [Tt])
                P.op("act", lambda e, c=c: e.activation(out=SL[:, c, :], in_=Tt[:, c, :], func=AF.Silu), [Tt], [SL])

        def c3(tc):
            SL = SLs[tc % 2]
            group_rms(C, scrs[tc % 2], [(SL, SL[:, 0, :]), (SL, SL[:, 1, :])], gain, 0, 0, tc)

        pipeline([c0, c1, c2, c3], 8)
        P.barrier()


def dbg_dump(C, name, buf, ap, rows, cols, r0=0):
    if C.get("dbgname") != name:
        return
    P, kb = C["P"], C["kb"]
    P.barrier()
    with ExitStack() as ph:
        T = kb.sb(ph, "dbgtmp", [128, cols])
        P.op("dve", lambda e: e.tensor_copy(out=T[0:rows, :], in_=ap), [buf], [T])
        P.dma("sp", lambda e: e.dma_start(out=C["dbg_d"].ap[r0:r0 + rows, 0:cols], in_=T[0:rows, :]), [T], [C["dbg_d"]], waw=False)
        P.barrier()


TWO_PI_HI = 6.28125
TWO_PI_LO = 2.0 * math.pi - 6.28125


def sin_scratch(kb, ph, shape, tag):
    return (kb.sb(ph, "sk_i" + tag, shape, I32), kb.sb(ph, "sk_f" + tag, shape), kb.sb(ph, "sk_r" + tag, shape))


def sin_of(P, scr, Xb, Xap, Ob, Oap, phase=0.0, scale=None, scale_b=None):
    KI, KF, Rr = scr
    nd = len(KI.ap.shape)
    full = tuple(slice(None) for _ in range(nd))
    if scale is None:
        P.op("dve", lambda e: e.tensor_scalar(out=Rr[full], in0=Xap, scalar1=float(phase), scalar2=None, op0=ALU.add), [Xb], [Rr])
    else:
        P.op("dve", lambda e: e.tensor_scalar(out=Rr[full], in0=Xap, scalar1=scale, scalar2=float(phase), op0=ALU.mult, op1=ALU.add), [Xb, scale_b], [Rr])
    P.op("dve", lambda e: e.tensor_scalar(out=KI[full], in0=Rr[full], scalar1=float(1.0 / (2.0 * math.pi)), scalar2=None, op0=ALU.mult), [Rr], [KI])
    P.op("dve", lambda e: e.tensor_copy(out=KF[full], in_=KI[full]), [KI], [KF])
    P.op("dve", lambda e: e.scalar_tensor_tensor(out=Rr[full], in0=KF[full], scalar=-TWO_PI_HI, in1=Rr[full], op0=ALU.mult, op1=ALU.add), [KF, Rr], [Rr])
    P.op("dve", lambda e: e.scalar_tensor_tensor(out=Rr[full], in0=KF[full], scalar=-TWO_PI_LO, in1=Rr[full], op0=ALU.mult, op1=ALU.add), [KF, Rr], [Rr])
    P.op("act", lambda e: e.activation(out=Oap, in_=Rr[full], func=AF.Sin), [Rr], [Ob])


def tt(P, eng, ob, oap, ab, aap, bb, bap, op):
    P.op(eng, lambda e: e.tensor_tensor(out=oap, in0=aap, in1=bap, op=op), [ab, bb], [ob])


def phase_s5(C, l):
    P, kb, EXT, projT = C["P"], C["kb"], C["EXT"], C["projT"]
    ident = C["ident"]
    with ExitStack() as ph:
        gain = load_cols(P, kb, ph, "mixg", EXT["mix_norm_g"], EXT["mix_norm_g"].ap[l, :], 8)
        dvec = load_cols(P, kb, ph, "s5d", EXT["s5_d"], EXT["s5_d"].ap[l, :], 2)
        bglu = load_cols(P, kb, ph, "s5bg", EXT["s5_b_glu"], EXT["s5_b_glu"].ap[l, :], 4)
        IOTA = kb.sb(ph, "s5iota", [128, 512])
        ld(P, IOTA, IOTA[:, :], EXT["iota512"], EXT["iota512"].ap.ap())
        U = kb.sb(ph, "s5U", [128, 2, L], BF16)
        YACC = kb.sb(ph, "s5Y", [128, 2, L])
        for cc in range(2):
            P.dma("pool", lambda e, cc=cc: e.dma_start(out=U[:, cc, :], in_=projT.ap[1280 + cc * 128:1280 + (cc + 1) * 128, :]), [projT], [U], waw=False)
            ld(P, YACC, YACC[:, cc, :], projT, projT.ap[1280 + cc * 128:1280 + (cc + 1) * 128, :])
            P.op("dve", lambda e, cc=cc: e.tensor_scalar(out=YACC[:, cc, :], in0=YACC[:, cc, :], scalar1=dvec[:, cc:cc + 1], scalar2=None, op0=ALU.mult), [YACC, dvec], [YACC])
        COSt = kb.sb(ph, "s5cos", [128, 8, 512]); SINt = kb.sb(ph, "s5sin", [128, 8, 512])
        LB = [kb.sb(ph, f"s5LB{i}", [128, 8, 128], BF16) for i in range(2)]
        LC = [kb.sb(ph, f"s5LC{i}", [128, 8, 128], BF16) for i in range(2)]
        MAG = kb.sb(ph, "s5mag", [128, 8]); CT = kb.sb(ph, "s5cT", [128, 8]); ST = kb.sb(ph, "s5sT", [128, 8])
        for d in range(2):
            with ExitStack() as pp:
                LRE = load_cols(P, kb, pp, "lre", EXT["s5_lam_re"], EXT["s5_lam_re"].ap[l, d, :], 8)
                LIM = load_cols(P, kb, pp, "lim", EXT["s5_lam_im"], EXT["s5_lam_im"].ap[l, d, :], 8)
                STEP = kb.sb(pp, "step", [128, 8])
                for g in range(16):
                    j, gl = g // 2, g % 2
                    P.dma("sp", lambda e, g=g, j=j, gl=gl: e.dma_start(out=STEP[gl * 64:(gl + 1) * 64, j:j + 1], in_=EXT["s5_log_step"].ap[l, d, g:g + 1].rearrange("(a b) -> a b", b=1).partition_broadcast(64)), [EXT["s5_log_step"]], [STEP], waw=False)
                P.op("act", lambda e: e.activation(out=STEP[:, :], in_=STEP[:, :], func=AF.Exp), [STEP], [STEP])
                P.op("dve", lambda e: e.tensor_scalar(out=LRE[:, :], in0=LRE[:, :], scalar1=-1e-4, scalar2=None, op0=ALU.min), [LRE], [LRE])
                TH = kb.sb(pp, "th", [128, 8]); X8 = kb.sb(pp, "x8", [128, 8]); CO = kb.sb(pp, "co", [128, 8]); SI = kb.sb(pp, "si", [128, 8])
                ARE = kb.sb(pp, "are", [128, 8]); AIM = kb.sb(pp, "aim", [128, 8]); DEN = kb.sb(pp, "den", [128, 8]); T8 = kb.sb(pp, "t8", [128, 8])
                ZRE = kb.sb(pp, "zre", [128, 8]); ZIM = kb.sb(pp, "zim", [128, 8]); NZIM = kb.sb(pp, "nzim", [128, 8])
                tt(P, "dve", X8, X8[:, :], LRE, LRE[:, :], STEP, STEP[:, :], ALU.mult)
                P.op("act", lambda e: e.activation(out=MAG[:, :], in_=X8[:, :], func=AF.Exp), [X8], [MAG])
                tt(P, "dve", TH, TH[:, :], LIM, LIM[:, :], STEP, STEP[:, :], ALU.mult)
                ss8 = sin_scratch(kb, pp, [128, 8], "a")
                ssb = sin_scratch(kb, pp, [128, 512], "b")
                sin_of(P, ss8, TH, TH[:, :], SI, SI[:, :])
                sin_of(P, ss8, TH, TH[:, :], CO, CO[:, :], phase=math.pi / 2)
                tt(P, "dve", ARE, ARE[:, :], MAG, MAG[:, :], CO, CO[:, :], ALU.mult)
                tt(P, "dve", AIM, AIM[:, :], MAG, MAG[:, :], SI, SI[:, :], ALU.mult)
                tt(P, "dve", DEN, DEN[:, :], LRE, LRE[:, :], LRE, LRE[:, :], ALU.mult)
                tt(P, "dve", T8, T8[:, :], LIM, LIM[:, :], LIM, LIM[:, :], ALU.mult)
                tt(P, "dve", DEN, DEN[:, :], DEN, DEN[:, :], T8, T8[:, :], ALU.add)
                P.op("dve", lambda e: e.reciprocal(out=DEN[:, :], in_=DEN[:, :]), [DEN], [DEN])
                NRE = kb.sb(pp, "nre", [128, 8])
                P.op("dve", lambda e: e.tensor_scalar(out=NRE[:, :], in0=ARE[:, :], scalar1=-1.0, scalar2=None, op0=ALU.add), [ARE], [NRE])
                tt(P, "dve", ZRE, ZRE[:, :], NRE, NRE[:, :], LRE, LRE[:, :], ALU.mult)
                tt(P, "dve", T8, T8[:, :], AIM, AIM[:, :], LIM, LIM[:, :], ALU.mult)
                tt(P, "dve", ZRE, ZRE[:, :], ZRE, ZRE[:, :], T8, T8[:, :], ALU.add)
                tt(P, "dve", ZRE, ZRE[:, :], ZRE, ZRE[:, :], DEN, DEN[:, :], ALU.mult)
                tt(P, "dve", ZIM, ZIM[:, :], AIM, AIM[:, :], LRE, LRE[:, :], ALU.mult)
                tt(P, "dve", T8, T8[:, :], NRE, NRE[:, :], LIM, LIM[:, :], ALU.mult)
                tt(P, "dve", ZIM, ZIM[:, :], ZIM, ZIM[:, :], T8, T8[:, :], ALU.subtract)
                tt(P, "dve", ZIM, ZIM[:, :], ZIM, ZIM[:, :], DEN, DEN[:, :], ALU.mult)
                P.op("dve", lambda e: e.tensor_scalar(out=NZIM[:, :], in0=ZIM[:, :], scalar1=-1.0, scalar2=None, op0=ALU.mult), [ZIM], [NZIM])
                P.op("dve", lambda e: e.tensor_scalar(out=X8[:, :], in0=TH[:, :], scalar1=512.0, scalar2=None, op0=ALU.mult), [TH], [X8])
                sin_of(P, ss8, X8, X8[:, :], ST, ST[:, :])
                sin_of(P, ss8, X8, X8[:, :], CT, CT[:, :], phase=math.pi / 2)
                for j in range(8):
                    sin_of(P, ssb, IOTA, IOTA[:, :], SINt, SINt[:, j, :], scale=TH[:, j:j + 1], scale_b=TH)
                    sin_of(P, ssb, IOTA, IOTA[:, :], COSt, COSt[:, j, :], phase=math.pi / 2, scale=TH[:, j:j + 1], scale_b=TH)
                BRE = kb.sb(pp, "bre", [128, 8, 128]); BIM = kb.sb(pp, "bim", [128, 8, 128])
                CRE = kb.sb(pp, "cre", [128, 8, 128]); CIM = kb.sb(pp, "cim", [128, 8, 128])
                BB = [kb.sb(pp, f"bb{i}", [128, 8, 128]) for i in range(2)]
                TB = kb.sb(pp, "tb", [128, 128])
                for t_ in (BRE, BIM, CRE, CIM):
                    P.op("pool", lambda e, t_=t_: e.memset(t_[:, :, :], 0.0), [], [t_])
                for g in range(16):
                    j, gl = g // 2, g % 2
                    c0 = 32 * (j % 4) + 16 * gl
                    for (dst, nm) in ((BRE, "s5_b_re"), (BIM, "s5_b_im")):
                        P.dma("sp", lambda e, dst=dst, nm=nm, g=g, j=j, gl=gl, c0=c0: e.dma_start(out=dst[gl * 64:(gl + 1) * 64, j, c0:c0 + 16], in_=EXT[nm].ap[l, d, g, :, :]), [EXT[nm]], [dst])
                    for (dst, nm) in ((CRE, "s5_c_re"), (CIM, "s5_c_im")):
                        P.dma("sp", lambda e, dst=dst, nm=nm, g=g, j=j, gl=gl, c0=c0: e.dma_start(out=dst[c0:c0 + 16, j, gl * 64:(gl + 1) * 64], in_=EXT[nm].ap[l, d, g, :, :]), [EXT[nm]], [dst])
                for j in range(8):
                    P.op("dve", lambda e, j=j: e.tensor_scalar(out=TB[:, :], in0=BRE[:, j, :], scalar1=ZRE[:, j:j + 1], scalar2=None, op0=ALU.mult), [BRE, ZRE], [TB])
                    P.op("dve", lambda e, j=j: e.scalar_tensor_tensor(out=BB[0][:, j, :], in0=BIM[:, j, :], scalar=NZIM[:, j:j + 1], in1=TB[:, :], op0=ALU.mult, op1=ALU.add), [BIM, NZIM, TB], [BB[0]])
                    P.op("dve", lambda e, j=j: e.tensor_scalar(out=TB[:, :], in0=BIM[:, j, :], scalar1=ZRE[:, j:j + 1], scalar2=None, op0=ALU.mult), [BIM, ZRE], [TB])
                    P.op("dve", lambda e, j=j: e.scalar_tensor_tensor(out=BB[1][:, j, :], in0=BRE[:, j, :], scalar=ZIM[:, j:j + 1], in1=TB[:, :], op0=ALU.mult, op1=ALU.add), [BRE, ZIM, TB], [BB[1]])
                n = 0
                for (src, dst, scale) in ((BB[0], LB[0], 1.0), (BB[1], LB[1], 1.0), (CRE, LC[0], 1.0), (CIM, LC[1], -1.0)):
                    for j in range(8):
                        ps = kb.psum[n % 4]; n += 1
                        P.op("pe", lambda e, src=src, j=j, ps=ps: e.transpose(out=ps[:, 0:128], in_=src[:, j, :], identity=ident[:, :]), [src, ident], [ps])
                        P.op("act", lambda e, dst=dst, j=j, ps=ps, scale=scale: e.mul(out=dst[:, j, :], in_=ps[:, 0:128], mul=scale), [ps], [dst])
                P.barrier()
            with ExitStack() as mp:
                R3 = 3

                def w(nm, n_=R3, dt=F32):
                    return [kb.sb(mp, f"{nm}{i}", [128, 512], dt) for i in range(n_)]
                BR, BI, WR, WI, SR, SI_ = w("BR"), w("BI"), w("WR"), w("WI"), w("SR"), w("SI")
                SRb, SIb = w("SRb", R3, BF16), w("SIb", R3, BF16)
                TA = w("TA", 4); TB_ = w("TB", 4)
                INI = [kb.sb(mp, f"ini{i}", [128, 4]) for i in range(8)]
                rev = (d == 1)
                order = list(range(8)) if not rev else list(range(7, -1, -1))
                its = [(ci, c, j) for ci, c in enumerate(order) for j in range(8)]
                fl = (lambda t_: t_[:, ::-1]) if rev else (lambda t_: t_[:, :])

                def tabs(j):
                    return (COSt[:, j, ::-1] if rev else COSt[:, j, :]), (SINt[:, j, ::-1] if rev else SINt[:, j, :])

                def st0(it):
                    ci, c, j = its[it]
                    sl = slice(c * 512, (c + 1) * 512)
                    r = it % R3
                    pa, pb = kb.psum[2 * (it % 2)], kb.psum[2 * (it % 2) + 1]
                    P.op("pe", lambda e: e.matmul(pa[:, :], lhsT=LB[0][:, j, :], rhs=U[:, j // 4, sl], start=True, stop=True), [LB[0], U], [pa])
                    P.op("pe", lambda e: e.matmul(pb[:, :], lhsT=LB[1][:, j, :], rhs=U[:, j // 4, sl], start=True, stop=True), [LB[1], U], [pb])
                    copy_on(P, "act", BR[r], BR[r][:, :], pa, pa[:, :])
                    copy_on(P, "act", BI[r], BI[r][:, :], pb, pb[:, :])

                def st1(it):
                    ci, c, j = its[it]
                    r = it % R3
                    cosap, sinap = tabs(j)
                    br, bi, wr, wi = BR[r], BI[r], WR[r], WI[r]
                    t1, t2, t3, t4 = TA[0], TA[1], TA[2], TA[3]
                    tt(P, "pool", t1, t1[:, :], COSt, cosap, br, br[:, :], ALU.mult)
                    tt(P, "pool", t2, t2[:, :], SINt, sinap, bi, bi[:, :], ALU.mult)
                    tt(P, "pool", wr, wr[:, :], t1, t1[:, :], t2, t2[:, :], ALU.add)
                    tt(P, "dve", t3, t3[:, :], COSt, cosap, bi, bi[:, :], ALU.mult)
                    tt(P, "dve", t4, t4[:, :], SINt, sinap, br, br[:, :], ALU.mult)
                    tt(P, "dve", wi, wi[:, :], t3, t3[:, :], t4, t4[:, :], ALU.subtract)

                def st2(it):
                    ci, c, j = its[it]
                    r = it % R3
                    wr, wi, sr, si = WR[r], WI[r], SR[r], SI_[r]
                    ini = INI[j]
                    if ci == 0:
                        ire, iim = 0.0, 0.0
                    else:
                        P.op("dve", lambda e: e.tensor_scalar(out=ini[:, 2:3], in0=ini[:, 0:1], scalar1=CT[:, j:j + 1], scalar2=None, op0=ALU.mult), [ini, CT], [ini])
                        P.op("dve", lambda e: e.scalar_tensor_tensor(out=ini[:, 2:3], in0=ini[:, 1:2], scalar=ST[:, j:j + 1], in1=ini[:, 2:3], op0=ALU.mult, op1=ALU.subtract), [ini, ST], [ini])
                        P.op("dve", lambda e: e.tensor_scalar(out=ini[:, 2:3], in0=ini[:, 2:3], scalar1=-1.0, scalar2=None, op0=ALU.mult), [ini], [ini])
                        P.op("dve", lambda e: e.tensor_scalar(out=ini[:, 3:4], in0=ini[:, 0:1], scalar1=ST[:, j:j + 1], scalar2=None, op0=ALU.mult), [ini, ST], [ini])
                        P.op("dve", lambda e: e.scalar_tensor_tensor(out=ini[:, 3:4], in0=ini[:, 1:2], scalar=CT[:, j:j + 1], in1=ini[:, 3:4], op0=ALU.mult, op1=ALU.add), [ini, CT], [ini])
                        ire, iim = ini[:, 2:3], ini[:, 3:4]
                    magb = MAG[:, j:j + 1].to_broadcast([128, 512])
                    P.op("dve", lambda e: e.tensor_tensor_scan(out=fl(sr), data0=magb, data1=fl(wr), initial=ire, op0=ALU.mult, op1=ALU.add), [MAG, wr, ini], [sr])
                    P.op("dve", lambda e: e.tensor_tensor_scan(out=fl(si), data0=magb, data1=fl(wi), initial=iim, op0=ALU.mult, op1=ALU.add), [MAG, wi, ini], [si])
                    last = slice(0, 1) if rev else slice(511, 512)
                    P.op("pool", lambda e: e.tensor_copy(out=ini[:, 0:1], in_=sr[:, last]), [sr], [ini])
                    P.op("pool", lambda e: e.tensor_copy(out=ini[:, 1:2], in_=si[:, last]), [si], [ini])

                def st3(it):
                    ci, c, j = its[it]
                    r = it % R3
                    cosap, sinap = tabs(j)
                    sr, si = SR[r], SI_[r]
                    t1, t2, t3, t4 = TB_[0], TB_[1], TB_[2], TB_[3]
                    tt(P, "pool", t1, t1[:, :], COSt, cosap, sr, sr[:, :], ALU.mult)
                    tt(P, "pool", t2, t2[:, :], SINt, sinap, si, si[:, :], ALU.mult)
                    tt(P, "pool", SRb[r], SRb[r][:, :], t1, t1[:, :], t2, t2[:, :], ALU.subtract)
                    tt(P, "dve", t3, t3[:, :], SINt, sinap, sr, sr[:, :], ALU.mult)
                    tt(P, "dve", t4, t4[:, :], COSt, cosap, si, si[:, :], ALU.mult)
                    tt(P, "dve", SIb[r], SIb[r][:, :], t3, t3[:, :], t4, t4[:, :], ALU.add)

                def st4(it):
                    ci, c, j = its[it]
                    sl = slice(c * 512, (c + 1) * 512)
                    r = it % R3
                    cc, jj = j // 4, j % 4
                    py = kb.psum[4 + (ci * 2 + cc) % 4]
                    P.op("pe", lambda e: e.matmul(py[:, :], lhsT=LC[0][:, j, :], rhs=SRb[r][:, :], start=(jj == 0), stop=False), [LC[0], SRb[r]], [py])
                    P.op("pe", lambda e: e.matmul(py[:, :], lhsT=LC[1][:, j, :], rhs=SIb[r][:, :], start=False, stop=(jj == 3)), [LC[1], SIb[r]], [py])
                    if jj == 3:
                        P.op("dve", lambda e: e.tensor_tensor(out=YACC[:, cc, sl], in0=YACC[:, cc, sl], in1=py[:, :], op=ALU.add), [YACC, py], [YACC])

                stages = [st0, st1, st2, st3, st4]
                nit = len(its)
                for step in range(nit + len(stages) - 1):
                    for k, fn in enumerate(stages):
                        it = step - k
                        if 0 <= it < nit:
                            fn(it)
                P.barrier()
        dbg_dump(C, "S5Y0", YACC, YACC[:, 0, :], 128, L)
        with ExitStack() as gp:
            Wg = kb.sb(gp, "s5Wg", [128, 2, 512], BF16)
            for k in range(2):
                P.dma("pool", lambda e, k=k: e.dma_start(out=Wg[:, k, :], in_=EXT["s5_w_glu"].ap[l, k * 128:(k + 1) * 128, :]), [EXT["s5_w_glu"]], [Wg], waw=False)
            SQ = kb.sb(gp, "gSQ", [128, 2, 512]); GY = kb.sb(gp, "gGY", [128, 2, 512], BF16)
            AO = kb.sb(gp, "gAO", [128, 2, 512]); GO = kb.sb(gp, "gGO", [128, 2, 512])
            scr = rms_scratch(kb, gp)
            c0 = math.sqrt(2.0 / math.pi)
            for tc in range(8):
                sl = slice(tc * 512, (tc + 1) * 512)
                P.op("act", lambda e: e.activation(out=SQ[:, :, :], in_=YACC[:, :, sl], func=AF.Square), [YACC], [SQ])
                P.op("dve", lambda e: e.tensor_scalar(out=SQ[:, :, :], in0=SQ[:, :, :], scalar1=2.0 * c0 * 0.044715, scalar2=2.0 * c0, op0=ALU.mult, op1=ALU.add), [SQ], [SQ])
                P.op("pool", lambda e: e.tensor_tensor(out=SQ[:, :, :], in0=SQ[:, :, :], in1=YACC[:, :, sl], op=ALU.mult), [SQ, YACC], [SQ])
                P.op("act", lambda e: e.activation(out=SQ[:, :, :], in_=SQ[:, :, :], func=AF.Sigmoid), [SQ], [SQ])
                P.op("dve", lambda e: e.tensor_tensor(out=GY[:, :, :], in0=SQ[:, :, :], in1=YACC[:, :, sl], op=ALU.mult), [SQ, YACC], [GY])
                for oc in range(4):
                    ps = kb.psum[oc]
                    for k in range(2):
                        P.op("pe", lambda e, oc=oc, k=k, ps=ps: e.matmul(ps[:, :], lhsT=Wg[:, k, oc * 128:(oc + 1) * 128], rhs=GY[:, k, :], start=(k == 0), stop=(k == 1)), [Wg, GY], [ps])
                    if oc < 2:
                        P.op("act", lambda e, oc=oc, ps=ps: e.activation(out=AO[:, oc, :], in_=ps[:, :], func=AF.Identity, bias=bglu[:, oc:oc + 1]), [ps, bglu], [AO])
                    else:
                        P.op("act", lambda e, oc=oc, ps=ps: e.activation(out=GO[:, oc - 2, :], in_=ps[:, :], func=AF.Sigmoid, bias=bglu[:, oc:oc + 1]), [ps, bglu], [GO])
                P.op("dve", lambda e: e.tensor_tensor(out=AO[:, :, :], in0=AO[:, :, :], in1=GO[:, :, :], op=ALU.mult), [AO, GO], [AO])
                group_rms(C, scr, [(AO, AO[:, 0, :]), (AO, AO[:, 1, :])], gain, 4, 512, tc)
            P.barrier()


def sin_reduce(P, scr, Ob, Oap):
    KI, KF, Rr = scr
    full = tuple(slice(None) for _ in range(len(KI.ap.shape)))
    P.op("dve", lambda e: e.tensor_scalar(out=KI[full], in0=Rr[full], scalar1=float(1.0 / (2.0 * math.pi)), scalar2=None, op0=ALU.mult), [Rr], [KI])
    P.op("dve", lambda e: e.tensor_copy(out=KF[full], in_=KI[full]), [KI], [KF])
    P.op("dve", lambda e: e.scalar_tensor_tensor(out=Rr[full], in0=KF[full], scalar=-TWO_PI_HI, in1=Rr[full], op0=ALU.mult, op1=ALU.add), [KF, Rr], [Rr])
    P.op("dve", lambda e: e.scalar_tensor_tensor(out=Rr[full], in0=KF[full], scalar=-TWO_PI_LO, in1=Rr[full], op0=ALU.mult, op1=ALU.add), [KF, Rr], [Rr])
    P.op("act", lambda e: e.activation(out=Oap, in_=Rr[full], func=AF.Sin), [Rr], [Ob])


def phase_hyena(C, l):
    P, kb, EXT, projT = C["P"], C["kb"], C["EXT"], C["projT"]
    ident, hy_tok, z1_tok, k_d = C["ident"], C["hy_tok"], C["z1_tok"], C["k_d"]
    with ExitStack() as ph:
        gain = load_cols(P, kb, ph, "mixg", EXT["mix_norm_g"], EXT["mix_norm_g"].ap[l, :], 8)
        CH = kb.sb(ph, "hCH", [128, 32]); SH = kb.sb(ph, "hSH", [128, 32])
        ld(P, CH, CH[:, :], EXT["hy_ch"], EXT["hy_ch"].ap.ap()); ld(P, SH, SH[:, :], EXT["hy_sh"], EXT["hy_sh"].ap.ap())
        BIAS = kb.sb(ph, "hBIAS", [128, 2, 256])
        for o in range(2):
            ld(P, BIAS, BIAS[:, o, :], EXT["hy_bias"], EXT["hy_bias"].ap[l, o:o + 1, :].partition_broadcast(128))
        Zop = kb.sb(ph, "hZop", [128, NT, 512], BF16)
        SGN = kb.sb(ph, "hSGN", [128, 1]); Jm = kb.sb(ph, "hJ", [128, 128])
        ld(P, SGN, SGN[:, :], EXT["hy_sgn"], EXT["hy_sgn"].ap.ap()); ld(P, Jm, Jm[:, :], EXT["hy_J"], EXT["hy_J"].ap.ap())
        csn = [0]

        with ExitStack() as p1:
            sw = kb.sb(p1, "hsw", [128, 6, 3]); sbias = load_cols(P, kb, p1, "hsb", EXT["hy_short_b"], EXT["hy_short_b"].ap[l, :], 6)
            for ch in range(6):
                P.dma("sp", lambda e, ch=ch: e.dma_start(out=sw[:, ch, :], in_=EXT["hy_short_w"].ap[l, :, ch * 128:(ch + 1) * 128].rearrange("k p -> p k"), allow_slow_non_contiguous=True), [EXT["hy_short_w"]], [sw], waw=False)
            XP = [kb.sb(p1, f"hXP{i}", [128, L + 2]) for i in range(2)]
            Y = [kb.sb(p1, f"hY{i}", [128, L]) for i in range(2)]
            STG = [kb.sb(p1, f"hSTG{i}", [128, 4, 128]) for i in range(2)]
            for xp in XP:
                P.op("pool", lambda e, xp=xp: e.memset(xp[:, 0:1], 0.0), [], [xp])
                P.op("pool", lambda e, xp=xp: e.memset(xp[:, L + 1:L + 2], 0.0), [], [xp])
            n = 0
            for ch in range(6):
                xp, y = XP[ch % 2], Y[ch % 2]
                ld(P, xp, xp[:, 1:L + 1], projT, projT.ap[512 + ch * 128:512 + (ch + 1) * 128, :])
                P.op("dve", lambda e, ch=ch, xp=xp, y=y: e.tensor_scalar(out=y[:, :], in0=xp[:, 0:L], scalar1=sw[:, ch, 0:1], scalar2=sbias[:, ch:ch + 1], op0=ALU.mult, op1=ALU.add), [xp, sw, sbias], [y])
                for k in (1, 2):
                    P.op("dve", lambda e, ch=ch, xp=xp, y=y, k=k: e.scalar_tensor_tensor(out=y[:, :], in0=xp[:, k:k + L], scalar=sw[:, ch, k:k + 1], in1=y[:, :], op0=ALU.mult, op1=ALU.add), [xp, sw, y], [y])
                for g4 in range(8):
                    ps = kb.psum[n % 4]; stg = STG[n % 2]; n += 1
                    for j in range(4):
                        nt = g4 * 4 + j
                        P.op("pe", lambda e, ps=ps, j=j, nt=nt, y=y: e.transpose(out=ps[:, j * 128:(j + 1) * 128], in_=y[:, nt * 128:(nt + 1) * 128], identity=ident[:, :]), [y, ident], [ps])
                    copy_on(P, "act", stg, stg[:, :, :], ps, ps[:, :].rearrange("p (j c) -> p j c", j=4))
                    if ch < 2:
                        P.op("pool", lambda e, stg=stg, g4=g4, ch=ch: e.tensor_copy(out=Zop[:, g4 * 4:(g4 + 1) * 4, ch * 128:(ch + 1) * 128], in_=stg[:, :, :]), [stg], [Zop])
                        P.op("pool", lambda e, stg=stg, g4=g4, ch=ch: e.tensor_scalar(out=Zop[:, g4 * 4:(g4 + 1) * 4, 256 + ch * 128:256 + (ch + 1) * 128], in0=stg[:, :, :], scalar1=SGN[:, 0:1], scalar2=None, op0=ALU.mult), [stg, SGN], [Zop])
                    P.dma("sp", lambda e, stg=stg, g4=g4, ch=ch: e.dma_start(out=hy_tok.ap[g4 * 512:(g4 + 1) * 512, ch * 128:(ch + 1) * 128].rearrange("(j p) c -> p j c", p=128), in_=stg[:, :, :]), [stg], [hy_tok], waw=False)
            P.barrier()
        with ExitStack() as p2:
            Fop = kb.sb(p2, "hF", [128, NT, 1024], BF16)
            with ExitStack() as p2a:
                ZT = kb.sb(p2a, "hZT", [33, L]); W1 = kb.sb(p2a, "hW1", [33, 64]); W2 = kb.sb(p2a, "hW2", [64, 64]); W3 = kb.sb(p2a, "hW3", [64, 1024])
                ld(P, ZT, ZT[:, :], EXT["hy_zT"], EXT["hy_zT"].ap.ap())
                ld(P, W1, W1[:, :], EXT["hy_f_w1"], EXT["hy_f_w1"].ap[l, :, :]); ld(P, W2, W2[:, :], EXT["hy_f_w2"], EXT["hy_f_w2"].ap[l, :, :])
                ld(P, W3, W3[:, :], EXT["hy_f_w3"], EXT["hy_f_w3"].ap[l, :, :])
                prm = kb.sb(p2a, "hprm", [64, 3])
                for i, nm in enumerate(("hy_f_b1", "hy_f_b2", "hy_f_freq")):
                    P.dma("sp", lambda e, i=i, nm=nm: e.dma_start(out=prm[:, i:i + 1], in_=EXT[nm].ap[l, :].rearrange("(p o) -> p o", o=1)), [EXT[nm]], [prm], waw=False)
                H1T = kb.sb(p2a, "hH1T", [64, L]); H2T = kb.sb(p2a, "hH2T", [64, L])
                scr = sin_scratch(kb, p2a, [64, 512], "h")
                for (Wm, kdim, src, dst, bcol) in ((W1, 33, ZT, H1T, 0), (W2, 64, H1T, H2T, 1)):
                    for tc in range(8):
                        sl = slice(tc * 512, (tc + 1) * 512)
                        ps = kb.psum[tc % 2]
                        P.op("pe", lambda e, Wm=Wm, kdim=kdim, src=src, ps=ps, sl=sl: e.matmul(ps[0:64, :], lhsT=Wm[0:kdim, 0:64], rhs=src[0:kdim, sl], start=True, stop=True), [Wm, src], [ps])
                        P.op("dve", lambda e, ps=ps, bcol=bcol: e.tensor_scalar(out=scr[2][:, :], in0=ps[0:64, :], scalar1=prm[:, bcol:bcol + 1], scalar2=prm[:, 2:3], op0=ALU.add, op1=ALU.mult), [ps, prm], [scr[2]])
                        sin_reduce(P, scr, dst, dst[:, sl])
                WIN = [kb.sb(p2a, f"hWIN{i}", [128, 256]) for i in range(2)]
                BW = [kb.sb(p2a, f"hBW{i}", [128, 256]) for i in range(2)]
                SD = [kb.sb(p2a, f"hSD{i}", [128, 512]) for i in range(2)]
                for mt in range(NT):
                    win = WIN[mt % 2]
                    ld(P, win, win[:, :], EXT["hy_win"], EXT["hy_win"].ap[mt * 128:(mt + 1) * 128, :])
                    for o in range(2):
                        ps = kb.psum[2 + (2 * mt + o) % 4]; bw = BW[o]; sd = SD[o]
                        P.op("pe", lambda e, ps=ps, mt=mt, o=o: e.matmul(ps[:, :], lhsT=H2T[:, mt * 128:(mt + 1) * 128], rhs=W3[:, o * 512:(o + 1) * 512], start=True, stop=True), [H2T, W3], [ps])
                        copy_on(P, "act", bw, bw[:, :], ps, ps[:, 256:512])
                        if mt == 0:
                            P.op("pool", lambda e, bw=bw: e.memset(bw[0:1, :], 0.0), [], [bw])
                        P.op("dve", lambda e, ps=ps, bw=bw, sd=sd: e.tensor_tensor(out=sd[:, 0:256], in0=ps[:, 0:256], in1=bw[:, :], op=ALU.add), [ps, bw], [sd])
                        P.op("dve", lambda e, ps=ps, bw=bw, sd=sd: e.tensor_tensor(out=sd[:, 256:512], in0=bw[:, :], in1=ps[:, 0:256], op=ALU.subtract), [ps, bw], [sd])
                        P.op("pool", lambda e, sd=sd, mt=mt, o=o, win=win: e.tensor_tensor(out=Fop[:, mt, o * 256:(o + 1) * 256], in0=sd[:, 0:256], in1=win[:, :], op=ALU.mult), [sd, win], [Fop])
                        P.op("pool", lambda e, sd=sd, mt=mt, o=o, win=win: e.tensor_tensor(out=Fop[:, mt, 512 + o * 256:512 + (o + 1) * 256], in0=sd[:, 256:512], in1=win[:, :], op=ALU.mult), [sd, win], [Fop])
                P.barrier()
            T1s = [kb.sb(p2, f"hfT1{i}", [128, 512]) for i in range(2)]; T2s = [kb.sb(p2, f"hfT2{i}", [128, 512]) for i in range(2)]; KO = [kb.sb(p2, f"hKO{i}", [128, 2, 512]) for i in range(2)]
            CTF = [kb.sb(p2, f"hCTF{i}", [128, 32, 128], BF16) for i in range(6)]
            CS = [(CTF[0], CTF[1]), (CTF[2], CTF[3]), (CTF[4], CTF[5])]
            for kc in range(32):
                Ct, St = CS[csn[0] % 3]; csn[0] += 1
                ld(P, Ct, Ct[:, :, :], EXT["dftC0"], EXT["dftC0"].ap[kc, :, :].rearrange("p (a b) -> p a b", b=128))
                ld(P, St, St[:, :, :], EXT["dftS0"], EXT["dftS0"].ap[kc, :, :].rearrange("p (a b) -> p a b", b=128))
                pc0, ps0 = kb.psum[2 * (kc % 4)], kb.psum[2 * (kc % 4) + 1]
                for nc_ in range(NT):
                    P.op("pe", lambda e, nc_=nc_: e.matmul(pc0[:, :], lhsT=Ct[:, nc_, :], rhs=Fop[:, nc_, 0:512], start=(nc_ == 0), stop=(nc_ == NT - 1)), [Ct, Fop], [pc0])
                    P.op("pe", lambda e, nc_=nc_: e.matmul(ps0[:, :], lhsT=St[:, nc_, :], rhs=Fop[:, nc_, 512:1024], start=(nc_ == 0), stop=(nc_ == NT - 1)), [St, Fop], [ps0])
                ko = KO[kc % 2]
                P.op("act", lambda e: e.copy(out=ko[:, 0, :], in_=pc0[:, :]), [pc0], [ko])
                P.op("dve", lambda e: e.tensor_copy(out=ko[:, 1, :], in_=ps0[:, :]), [ps0], [ko])
                P.dma("act", lambda e, ko=ko, kc=kc: e.dma_start(out=k_d.ap[kc, :, :, :], in_=ko[:, :, :]), [ko], [k_d], waw=False)
            P.barrier()
        CT6 = [kb.sb(ph, f"hCT{i}", [128, 32, 128], BF16) for i in range(6)]
        YW = kb.sb(ph, "hYW", [128, NT, 512], BF16)
        OUTF = kb.sb(ph, "hOUTF", [128, 2, L])
        ctn = [0]

        def stream_c(idx):
            Ct = CT6[ctn[0] % 6]; ctn[0] += 1
            ld(P, Ct, Ct[:, :, :], EXT["dftC"], EXT["dftC"].ap[idx, :, :].rearrange("p (a b) -> p a b", b=128))
            return Ct

        KS = [kb.sb(ph, f"hKS{i}", [128, 2, 256]) for i in range(6)]
        W = [[kb.sb(ph, f"hw{a}{i}", [128, 256]) for i in range(2)] for a in range(5)]
        XS = [kb.sb(ph, f"hXS{i}", [128, 256]) for i in range(4)]
        X1 = [kb.sb(ph, f"hX1{i}", [128, 256]) for i in range(4)]
        ZG = [kb.sb(ph, f"hZG{i}", [128, 2, 256]) for i in range(6)]
        OT = [kb.sb(ph, f"hOT{i}", [128, 256]) for i in range(2)]
        pairn = [0]
        HS = os.environ.get("HY_STOP", "")
        for o in range(2):
            if HS == "pre" or (HS == "fwd0" and o == 1):
                break
            base = pairn[0]; pairn[0] += 32
            st_f = {}

            def f0(b, o=o, base=base, st_f=st_f):
                pr_ = (base + b) % 2
                blocks = (b, 31 - b)
                outs = (kb.psum[2 * pr_], kb.psum[2 * pr_ + 1])
                kss = []
                for bi, kc in enumerate(blocks):
                    Ct = stream_c(kc)
                    ks = KS[(2 * b + bi) % 6]; kss.append(ks)
                    ld(P, ks, ks[:, :, :], k_d, k_d.ap[kc, :, :, o * 256:(o + 1) * 256])
                    po = outs[bi]
                    for nc_ in range(NT):
                        P.op("pe", lambda e, po=po, Ct=Ct, nc_=nc_: e.matmul(po[:, :], lhsT=Ct[:, nc_, :], rhs=Zop[:, nc_, :], start=(nc_ == 0), stop=(nc_ == NT - 1)), [Ct, Zop], [po])
                st_f[b] = (pr_, blocks, outs, kss)

            def f1(b, st_f=st_f):
                pr_, blocks, outs, kss = st_f.pop(b)
                jps = (kb.psum[4 + 2 * pr_], kb.psum[5 + 2 * pr_])
                xs = [XS[(2 * b + bi) % 4] for bi in range(2)]
                for bi in range(2):
                    copy_on(P, "act", xs[bi], xs[bi][:, :], outs[bi], outs[bi][:, 256:512])
                for bi in range(2):
                    P.op("pe", lambda e, bi=bi: e.matmul(jps[bi][:, 0:256], lhsT=Jm[:, :], rhs=xs[1 - bi][:, :], start=True, stop=True), [Jm, xs[1 - bi]], [jps[bi]])
                for bi, kc in enumerate(blocks):
                    pa, pb, ks = outs[bi], jps[bi], kss[bi]
                    t1, t2, t3, t4, t5 = W[0][bi], W[1][bi], W[2][bi], W[3][bi], W[4][bi]
                    P.op("dve", lambda e: e.tensor_tensor(out=t1[:, :], in0=pa[:, 0:256], in1=ks[:, 0, :], op=ALU.mult), [pa, ks], [t1])
                    P.op("dve", lambda e: e.tensor_tensor(out=t2[:, :], in0=pb[:, 0:256], in1=ks[:, 1, :], op=ALU.mult), [pb, ks], [t2])
                    P.op("pool", lambda e: e.tensor_tensor(out=YW[:, kc, 0:256], in0=t1[:, :], in1=t2[:, :], op=ALU.add), [t1, t2], [YW])
                    P.op("dve", lambda e: e.tensor_tensor(out=t3[:, :], in0=pa[:, 0:256], in1=ks[:, 1, :], op=ALU.mult), [pa, ks], [t3])
                    P.op("dve", lambda e: e.tensor_tensor(out=t4[:, :], in0=pb[:, 0:256], in1=ks[:, 0, :], op=ALU.mult), [pb, ks], [t4])
                    P.op("pool", lambda e: e.tensor_tensor(out=t5[:, :], in0=t4[:, :], in1=t3[:, :], op=ALU.subtract), [t3, t4], [t5])
                    P.op("pool", lambda e: e.tensor_scalar(out=YW[:, kc, 256:512], in0=t5[:, :], scalar1=SGN[:, 0:1], scalar2=None, op0=ALU.mult), [t5, SGN], [YW])

            pipeline([f0, f1], 16)
            if HS == "fwd0":
                break
            base = pairn[0]; pairn[0] += 32
            st_i = {}

            def i0(b, o=o, base=base, st_i=st_i):
                pr_ = (base + b) % 2
                blocks = (b, 31 - b)
                outs = (kb.psum[2 * pr_], kb.psum[2 * pr_ + 1])
                zgs = []
                for bi, nc_ in enumerate(blocks):
                    Ct = stream_c(nc_)
                    zg = ZG[(2 * b + bi) % 6]; zgs.append(zg)
                    if o == 0:
                        ld(P, zg, zg[:, 0, :], hy_tok, hy_tok.ap[nc_ * 128:(nc_ + 1) * 128, 0:256])
                        ld(P, zg, zg[:, 1, :], hy_tok, hy_tok.ap[nc_ * 128:(nc_ + 1) * 128, 256:512])
                    else:
                        ld(P, zg, zg[:, 0, :], z1_tok, z1_tok.ap[nc_ * 128:(nc_ + 1) * 128, :])
                        ld(P, zg, zg[:, 1, :], hy_tok, hy_tok.ap[nc_ * 128:(nc_ + 1) * 128, 512:768])
                    po = outs[bi]
                    for kc in range(32):
                        P.op("pe", lambda e, po=po, Ct=Ct, kc=kc: e.matmul(po[:, :], lhsT=Ct[:, kc, :], rhs=YW[:, kc, :], start=(kc == 0), stop=(kc == 31)), [Ct, YW], [po])
                st_i[b] = (pr_, blocks, outs, zgs)

            def i1(b, o=o, st_i=st_i):
                pr_, blocks, outs, zgs = st_i.pop(b)
                jps = (kb.psum[4 + 2 * pr_], kb.psum[5 + 2 * pr_])
                xs = [XS[(2 * b + bi) % 4] for bi in range(2)]
                x1s = [X1[(2 * b + bi) % 4] for bi in range(2)]
                for bi in range(2):
                    copy_on(P, "act", xs[bi], xs[bi][:, :], outs[bi], outs[bi][:, 256:512])
                    copy_on(P, "act", x1s[bi], x1s[bi][:, :], outs[bi], outs[bi][:, 0:256])
                for bi in range(2):
                    P.op("pe", lambda e, bi=bi: e.matmul(jps[bi][:, 0:256], lhsT=Jm[:, :], rhs=xs[1 - bi][:, :], start=True, stop=True), [Jm, xs[1 - bi]], [jps[bi]])
                for bi, nc_ in enumerate(blocks):
                    zg, pj, x1 = zgs[bi], jps[bi], x1s[bi]
                    t1, ot = W[0][bi], OT[bi]
                    P.op("pool", lambda e: e.tensor_tensor(out=t1[:, :], in0=zg[:, 0, :], in1=BIAS[:, o, :], op=ALU.mult), [zg, BIAS], [t1])
                    P.op("dve", lambda e: e.scalar_tensor_tensor(out=t1[:, :], in0=pj[:, 0:256], scalar=2.0 / (2 * L), in1=t1[:, :], op0=ALU.mult, op1=ALU.add), [pj, t1], [t1])
                    P.op("dve", lambda e: e.scalar_tensor_tensor(out=t1[:, :], in0=x1[:, :], scalar=2.0 / (2 * L), in1=t1[:, :], op0=ALU.mult, op1=ALU.add), [x1, t1], [t1])
                    P.op("dve", lambda e: e.tensor_tensor(out=ot[:, :], in0=t1[:, :], in1=zg[:, 1, :], op=ALU.mult), [t1, zg], [ot])
                    if o == 0:
                        P.op("pool", lambda e: e.tensor_copy(out=Zop[:, nc_, 0:256], in_=ot[:, :]), [ot], [Zop])
                        P.op("pool", lambda e: e.tensor_scalar(out=Zop[:, nc_, 256:512], in0=ot[:, :], scalar1=SGN[:, 0:1], scalar2=None, op0=ALU.mult), [ot, SGN], [Zop])
                        P.dma("act", lambda e: e.dma_start(out=z1_tok.ap[nc_ * 128:(nc_ + 1) * 128, :], in_=ot[:, :]), [ot], [z1_tok], waw=False)
                    else:
                        pt = jps[bi]
                        for c in range(2):
                            P.op("pe", lambda e, c=c: e.transpose(out=pt[:, 256 + c * 128:256 + (c + 1) * 128], in_=ot[:, c * 128:(c + 1) * 128], identity=ident[:, :]), [ot, ident], [pt])
                        copy_on(P, "act", OUTF, OUTF[:, :, nc_ * 128:(nc_ + 1) * 128], pt, pt[:, 256:512].rearrange("p (c t) -> p c t", c=2))

            pipeline([i0, i1], 16)
        dbg_dump(C, "HYO0", OUTF, OUTF[:, 0, :], 128, L)
        scr = rms_scratch(kb, ph)
        for tc in range(8):
            sl = slice(tc * 512, (tc + 1) * 512)
            group_rms(C, scr, [(OUTF, OUTF[:, 0, sl]), (OUTF, OUTF[:, 1, sl])], gain, 2, 256, tc)
        P.barrier()


def phase_moe(C, l, stop_after):
    P, kb, EXT = C["P"], C["kb"], C["EXT"]
    ident, h_tok, h1_tok, moe_acc, mergedT = C["ident"], C["h_tok"], C["h1_tok"], C["moe_acc"], C["mergedT"]
    with ExitStack() as ph:
        AFFT = kb.sb(ph, "mAFFT", [16, L])
        IDXT = kb.sb(ph, "mIDXT", [128, 4, 16], I32); GATET = kb.sb(ph, "mGATET", [128, 4, 16])
        with ExitStack() as pd:
            Wo = kb.sb(pd, "dWo", [128, 8, D], BF16)
            for k in range(8):
                P.dma("pool", lambda e, k=k: e.dma_start(out=Wo[:, k, :], in_=EXT["w_out"].ap[l, k * 128:(k + 1) * 128, :]), [EXT["w_out"]], [Wo], waw=False)
            G1 = kb.sb(pd, "dG1", [128, D]); B1 = kb.sb(pd, "dB1", [128, D])
            ld(P, G1, G1[:, :], EXT["ln1_g"], EXT["ln1_g"].ap[l:l + 1, :].partition_broadcast(128))
            ld(P, B1, B1[:, :], EXT["ln1_b"], EXT["ln1_b"].ap[l:l + 1, :].partition_broadcast(128))
            RW = kb.sb(pd, "dRW", [128, 8, 16]); RB = kb.sb(pd, "dRB", [128, 16])
            ld(P, RW, RW[:, :, :], EXT["router_w"], EXT["router_w"].ap[l, :, :].rearrange("(k p) e -> p k e", p=128))
            ld(P, RB, RB[:, :], EXT["router_b"], EXT["router_b"].ap[l:l + 1, :].partition_broadcast(128))
            MT = [kb.sb(pd, f"dMT{i}", [128, 8, 512], BF16) for i in range(3)]
            Hh = [kb.sb(pd, f"dH{i}", [128, D]) for i in range(3)]
            Xx = [kb.sb(pd, f"dX{i}", [128, D]) for i in range(6)]
            HT1 = [kb.sb(pd, f"dHT{i}", [128, 8, 128]) for i in range(3)]
            scr = ln_scr(kb, pd, 4)
            LG = [kb.sb(pd, f"dLG{i}", [128, 16]) for i in range(4)]
            SM = [kb.sb(pd, f"dSM{i}", [128, 4]) for i in range(4)]

            def d0(ti):
                tc, t4 = ti // 4, ti % 4
                mt = MT[tc % 3]
                if t4 == 0:
                    ld(P, mt, mt[:, :, :], mergedT, mergedT.ap.ap().rearrange("(k p) t -> p k t", p=128)[:, :, tc * 512:(tc + 1) * 512])
                H, X = Hh[ti % 3], Xx[ti % 6]
                ld(P, H, H[:, :], h_tok, h_tok.ap[ti * 128:(ti + 1) * 128, :])
                for half in range(2):
                    ps = kb.psum[(2 * ti + half) % 4]
                    for k in range(8):
                        P.op("pe", lambda e, ps=ps, k=k, half=half: e.matmul(ps[:, :], lhsT=mt[:, k, t4 * 128:(t4 + 1) * 128], rhs=Wo[:, k, half * 512:(half + 1) * 512], start=(k == 0), stop=(k == 7)), [mt, Wo], [ps])
                    P.op("dve", lambda e, ps=ps, half=half: e.scalar_tensor_tensor(out=X[:, half * 512:(half + 1) * 512], in0=H[:, half * 512:(half + 1) * 512], scalar=float(DN_ALPHA), in1=ps[:, :], op0=ALU.mult, op1=ALU.add), [H, ps], [X])

            def d1(ti):
                ln_a(P, Xx[ti % 6], scr[ti % 4])

            def d2(ti):
                ln_b(P, Xx[ti % 6], G1, B1, scr[ti % 4])

            def d3(ti):
                X, ht = Xx[ti % 6], HT1[ti % 3]
                P.dma("sp", lambda e: e.dma_start(out=h1_tok.ap[ti * 128:(ti + 1) * 128, :], in_=X[:, :]), [X], [h1_tok], waw=False)
                for half in range(2):
                    ps = kb.psum[4 + half]
                    for j in range(4):
                        k = half * 4 + j
                        P.op("pe", lambda e, ps=ps, k=k, j=j: e.transpose(out=ps[:, j * 128:(j + 1) * 128], in_=X[:, k * 128:(k + 1) * 128], identity=ident[:, :]), [X, ident], [ps])
                    copy_on(P, "act" if half else "pool_via_dve", ht, ht[:, half * 4:(half + 1) * 4, :], ps, ps[:, :].rearrange("p (j t) -> p j t", j=4))

            def d4(ti):
                ht, lg, sm = HT1[ti % 3], LG[ti % 4], SM[ti % 4]
                pr = kb.psum[6]
                for k in range(8):
                    P.op("pe", lambda e, k=k: e.matmul(pr[:, 0:16], lhsT=ht[:, k, :], rhs=RW[:, k, :], start=(k == 0), stop=(k == 7)), [ht, RW], [pr])
                P.op("dve", lambda e: e.tensor_tensor(out=lg[:, :], in0=pr[:, 0:16], in1=RB[:, :], op=ALU.add), [pr, RB], [lg])
                P.op("dve", lambda e: e.reduce_max(out=sm[:, 0:1], in_=lg[:, :], axis=AX.X), [lg], [sm])
                P.op("dve", lambda e: e.tensor_scalar(out=sm[:, 1:2], in0=sm[:, 0:1], scalar1=-1.0, scalar2=None, op0=ALU.mult), [sm], [sm])
                P.op("act", lambda e: e.activation(out=lg[:, :], in_=lg[:, :], func=AF.Exp, bias=sm[:, 1:2]), [lg, sm], [lg])

            def d5(ti):
                lg, sm = LG[ti % 4], SM[ti % 4]
                P.op("dve", lambda e: e.reduce_sum(out=sm[:, 2:3], in_=lg[:, :], axis=AX.X), [lg], [sm])
                P.op("dve", lambda e: e.reciprocal(out=sm[:, 3:4], in_=sm[:, 2:3]), [sm], [sm])
                P.op("dve", lambda e: e.tensor_scalar(out=lg[:, :], in0=lg[:, :], scalar1=sm[:, 3:4], scalar2=None, op0=ALU.mult), [lg, sm], [lg])
                pt = kb.psum[7]
                P.op("pe", lambda e: e.transpose(out=pt[0:16, 0:128], in_=lg[:, :], identity=ident[:, :]), [lg, ident], [pt])
                copy_on(P, "act", AFFT, AFFT[:, ti * 128:(ti + 1) * 128], pt, pt[0:16, 0:128])

            pipeline([d0, d1, d2, d3, d4, d5], NT)
            P.barrier()
        dbg_dump(C, "AFFT", AFFT, AFFT[:, :], 16, L)
        if stop_after == f"D{l}":
            return
        Wg = [kb.sb(ph, f"fWg{i}", [128, 8, D], BF16) for i in range(2)]
        Wu = [kb.sb(ph, f"fWu{i}", [128, 8, D], BF16) for i in range(2)]
        Wd = [kb.sb(ph, f"fWd{i}", [128, 8, D], BF16) for i in range(2)]
        ZERO = kb.sb(ph, "fZ", [128, D])
        WSTG = [kb.sb(ph, f"fWS{i}", [128, D]) for i in range(6)]
        wsn = [0]
        stg_of = {}
        WNAMES = ("exp_w_gate", "exp_w_up", "exp_w_down")

        def load_part(e_, k):
            for m in range(3):
                stg = WSTG[wsn[0] % 6]; wsn[0] += 1
                stg_of[(e_, k, m)] = stg
                P.dma("sp", lambda e, stg=stg, m=m: e.dma_start(out=stg[:, :], in_=EXT[WNAMES[m]].ap[l, e_, k * 128:(k + 1) * 128, :]), [EXT[WNAMES[m]]], [stg], waw=False)

        def cast_part(e_, k, engs=("act", "dve", "pool")):
            i2 = e_ % 2
            for m, Wt in enumerate((Wg[i2], Wu[i2], Wd[i2])):
                stg = stg_of.pop((e_, k, m))
                copy_on(P, engs[m], Wt, Wt[:, k, :], stg, stg[:, :])

        def load_w(e_, engs):
            for k in range(8):
                load_part(e_, k)
                if k >= 1:
                    cast_part(e_, k - 1, engs)
            cast_part(e_, 7, engs)

        P.op("pool", lambda e: e.memset(ZERO[:, :], 0.0), [], [ZERO])
        for ti in range(NT):
            P.dma("sp", lambda e, ti=ti: e.dma_start(out=moe_acc.ap[ti * 128:(ti + 1) * 128, :], in_=ZERO[:, :]), [ZERO], [moe_acc], waw=False)
        load_w(0, ("act", "pool", "pool"))
        with ExitStack() as pe_:
            WORK = kb.sb(pe_, "eWORK", [16, L]); MX = kb.sb(pe_, "eMX", [16, 8]); IDX = kb.sb(pe_, "eIDX", [16, 512], U32)
            GATES = kb.sb(pe_, "eGATES", [16, 512]); IDXF = kb.sb(pe_, "eIDXF", [16, 512])
            P.op("dve", lambda e: e.tensor_copy(out=WORK[:, :], in_=AFFT[:, :]), [AFFT], [WORK])
            for r in range(64):
                P.op("dve", lambda e, r=r: e.max(out=GATES[:, r * 8:(r + 1) * 8], in_=WORK[:, :]), [WORK], [GATES])
                P.op("dve", lambda e, r=r: e.max_index(out=IDX[:, r * 8:(r + 1) * 8], in_max=GATES[:, r * 8:(r + 1) * 8], in_values=WORK[:, :]), [WORK, GATES], [IDX])
                P.op("dve", lambda e, r=r: e.match_replace(out=WORK[:, :], in_to_replace=GATES[:, r * 8:(r + 1) * 8], in_values=WORK[:, :], imm_value=-1.0), [WORK, GATES], [WORK])
            P.op("dve", lambda e: e.tensor_copy(out=IDXF[:, :], in_=IDX[:, :]), [IDX], [IDXF])
            for st in range(4):
                pa, pb = kb.psum[st % 2], kb.psum[2 + st % 2]
                P.op("pe", lambda e, pa=pa, st=st: e.transpose(out=pa[:, 0:16], in_=IDXF[:, st * 128:(st + 1) * 128], identity=ident[0:16, 0:16]), [IDXF, ident], [pa])
                P.op("pe", lambda e, pb=pb, st=st: e.transpose(out=pb[:, 0:16], in_=GATES[:, st * 128:(st + 1) * 128], identity=ident[0:16, 0:16]), [GATES, ident], [pb])
                P.op("dve", lambda e, pa=pa, st=st: e.tensor_copy(out=IDXT[:, st, :], in_=pa[:, 0:16]), [pa], [IDXT])
                P.op("act", lambda e, pb=pb, st=st: e.copy(out=GATET[:, st, :], in_=pb[:, 0:16]), [pb], [GATET])
            P.barrier()
        if C.get("dbgname") == "IDXT":
            with ExitStack() as pz:
                Tf = kb.sb(pz, "idxf", [128, 64])
                P.op("dve", lambda e: e.tensor_copy(out=Tf[:, :], in_=IDXT[:, :, :].rearrange("p a b -> p (a b)")), [IDXT], [Tf])
                dbg_dump(C, "IDXT", Tf, Tf[:, :], 128, 64)
        dbg_dump(C, "GATET", GATET, GATET[:, :, :].rearrange("p a b -> p (a b)"), 128, 64)
        if stop_after == f"E{l}":
            return
        with ExitStack() as pf:
            XS = [kb.sb(pf, f"fXS{i}", [128, D]) for i in range(8)]
            XT = kb.sb(pf, "fXT", [128, 8, 512], BF16)
            HID = kb.sb(pf, "fHID", [128, 8, 512], BF16)
            SG = [kb.sb(pf, f"fSG{i}", [128, 512]) for i in range(2)]
            YY = [kb.sb(pf, f"fY{i}", [128, D]) for i in range(3)]

            def gather(e_):
                for st in range(4):
                    xs = XS[(e_ % 2) * 4 + st]
                    P.dma("pool", lambda e, xs=xs, st=st: e.indirect_dma_start(out=xs[:, :], out_offset=None, in_=h1_tok.ap[:, :], in_offset=bass.IndirectOffsetOnAxis(ap=IDXT[:, st, e_:e_ + 1], axis=0)), [h1_tok, IDXT], [xs])

            gather(0)
            n = 0
            for e_ in range(16):
                i2 = e_ % 2
                xs4 = [XS[i2 * 4 + st] for st in range(4)]
                if e_ + 1 < 16:
                    gather(e_ + 1)
                for k in range(8):
                    ps = kb.psum[k % 2]
                    for st in range(4):
                        P.op("pe", lambda e, ps=ps, st=st, k=k: e.transpose(out=ps[:, st * 128:(st + 1) * 128], in_=xs4[st][:, k * 128:(k + 1) * 128], identity=ident[:, :]), [xs4[st], ident], [ps])
                    copy_on(P, kb.alt(), XT, XT[:, k, :], ps, ps[:, :])
                for f in range(8):
                    if e_ + 1 < 16:
                        if f >= 1:
                            cast_part(e_ + 1, f - 1)
                        load_part(e_ + 1, f)
                    pg, pu = kb.psum[2 + 2 * (f % 2)], kb.psum[3 + 2 * (f % 2)]
                    for k in range(8):
                        P.op("pe", lambda e, pg=pg, f=f, k=k: e.matmul(pg[:, :], lhsT=Wg[i2][:, k, f * 128:(f + 1) * 128], rhs=XT[:, k, :], start=(k == 0), stop=(k == 7)), [Wg[i2], XT], [pg])
                    for k in range(8):
                        P.op("pe", lambda e, pu=pu, f=f, k=k: e.matmul(pu[:, :], lhsT=Wu[i2][:, k, f * 128:(f + 1) * 128], rhs=XT[:, k, :], start=(k == 0), stop=(k == 7)), [Wu[i2], XT], [pu])
                    sg = SG[f % 2]
                    P.op("act", lambda e, sg=sg, pg=pg: e.activation(out=sg[:, :], in_=pg[:, :], func=AF.Silu), [pg], [sg])
                    P.op("dve", lambda e, sg=sg, pu=pu, f=f: e.tensor_tensor(out=HID[:, f, :], in0=sg[:, :], in1=pu[:, :], op=ALU.mult), [sg, pu], [HID])
                if e_ + 1 < 16:
                    cast_part(e_ + 1, 7)
                for st in range(4):
                    yy = YY[(4 * e_ + st) % 3]
                    for half in range(2):
                        py = kb.psum[6 + half]
                        for f in range(8):
                            P.op("pe", lambda e, py=py, st=st, f=f, half=half: e.matmul(py[:, :], lhsT=HID[:, f, st * 128:(st + 1) * 128], rhs=Wd[i2][:, f, half * 512:(half + 1) * 512], start=(f == 0), stop=(f == 7)), [HID, Wd[i2]], [py])
                        P.op("dve" if half else "act", (lambda e, py=py, yy=yy, st=st, half=half: e.tensor_scalar(out=yy[:, half * 512:(half + 1) * 512], in0=py[:, :], scalar1=GATET[:, st, e_:e_ + 1], scalar2=None, op0=ALU.mult)) if half else
                             (lambda e, py=py, yy=yy, st=st, half=half: e.activation(out=yy[:, half * 512:(half + 1) * 512], in_=py[:, :], func=AF.Copy, scale=GATET[:, st, e_:e_ + 1])), [py, GATET], [yy])
                    P.dma("pool", lambda e, yy=yy, st=st: e.indirect_dma_start(out=moe_acc.ap[:, :], out_offset=bass.IndirectOffsetOnAxis(ap=IDXT[:, st, e_:e_ + 1], axis=0), in_=yy[:, :], in_offset=None, compute_op=ALU.add), [yy, IDXT], [moe_acc], waw=(st == 0))
            P.barrier()
        if stop_after == f"F{l}":
            return
        with ExitStack() as pg_:
            G2 = kb.sb(pg_, "gG2", [128, D]); B2 = kb.sb(pg_, "gB2", [128, D])
            ld(P, G2, G2[:, :], EXT["ln2_g"], EXT["ln2_g"].ap[l:l + 1, :].partition_broadcast(128))
            ld(P, B2, B2[:, :], EXT["ln2_b"], EXT["ln2_b"].ap[l:l + 1, :].partition_broadcast(128))
            Xs = [kb.sb(pg_, f"gX{i}", [128, D]) for i in range(5)]
            Ms = [kb.sb(pg_, f"gM{i}", [128, D]) for i in range(3)]
            stg = [kb.sb(pg_, f"gstg{i}", [128, 8, 128], BF16) for i in range(2)]
            scr = ln_scr(kb, pg_, 4)

            def g0(ti):
                X, M = Xs[ti % 5], Ms[ti % 3]
                ld(P, X, X[:, :], h1_tok, h1_tok.ap[ti * 128:(ti + 1) * 128, :])
                ld(P, M, M[:, :], moe_acc, moe_acc.ap[ti * 128:(ti + 1) * 128, :])

            def g1(ti):
                X, M = Xs[ti % 5], Ms[ti % 3]
                P.op("dve", lambda e: e.scalar_tensor_tensor(out=X[:, :], in0=X[:, :], scalar=float(DN_ALPHA), in1=M[:, :], op0=ALU.mult, op1=ALU.add), [X, M], [X])
                ln_a(P, X, scr[ti % 4])

            def g2(ti):
                ln_b(P, Xs[ti % 5], G2, B2, scr[ti % 4])

            def g3(ti):
                X = Xs[ti % 5]
                if l == DEPTH - 1:
                    P.dma("sp", lambda e: e.dma_start(out=C["out_d"].ap[ti * 128:(ti + 1) * 128, :], in_=X[:, :]), [X], [C["out_d"]], waw=False)
                else:
                    P.dma("sp", lambda e: e.dma_start(out=h_tok.ap[ti * 128:(ti + 1) * 128, :], in_=X[:, :]), [X], [h_tok], waw=False)
                    transpose_to_hT(P, kb, X, ident, stg[ti % 2], C["hT_d"], ti)

            pipeline([g0, g1, g2, g3], NT)
            P.barrier()


def phase_attn(C, l):
    P, kb, EXT, projT = C["P"], C["kb"], C["EXT"], C["projT"]
    with ExitStack() as ph:
        gain = load_cols(P, kb, ph, "mixg", EXT["mix_norm_g"], EXT["mix_norm_g"].ap[l, :], 8)
        COS = kb.sb(ph, "aCOS", [128, L]); SIN = kb.sb(ph, "aSIN", [128, L])
        ld(P, COS, COS[:, :], EXT["ropeC"], EXT["ropeC"].ap.ap())
        ld(P, SIN, SIN[:, :], EXT["ropeS"], EXT["ropeS"].ap.ap())
        Rm = kb.sb(ph, "aRm", [128, 128], BF16); blk = kb.sb(ph, "ablk", [128, 128])
        ld(P, Rm, Rm[:, :], EXT["Rm"], EXT["Rm"].ap.ap())
        ld(P, blk, blk[:, :], EXT["blk64"], EXT["blk64"].ap.ap())
        onesrow = kb.sb(ph, "aones", [128, 128])
        P.op("pool", lambda e: e.memset(onesrow[:, :], 1.0), [], [onesrow])
        gq = kb.sb(ph, "agq", [128, 1]); gk = kb.sb(ph, "agk", [128, 1])
        for hh in range(2):
            P.dma("sp", lambda e, hh=hh: e.dma_start(out=gq[hh * 64:(hh + 1) * 64, :], in_=EXT["q_norm_g"].ap[l, :].rearrange("(p o) -> p o", o=1)), [EXT["q_norm_g"]], [gq], waw=False)
            P.dma("sp", lambda e, hh=hh: e.dma_start(out=gk[hh * 64:(hh + 1) * 64, :], in_=EXT["k_norm_g"].ap[l, :].rearrange("(p o) -> p o", o=1)), [EXT["k_norm_g"]], [gk], waw=False)
        P.op("dve", lambda e: e.tensor_scalar(out=gq[:, :], in0=gq[:, :], scalar1=0.125, scalar2=None, op0=ALU.mult), [gq], [gq])
        KT = [kb.sb(ph, f"aKT{i}", [128, L], BF16) for i in range(2)]
        QT = [kb.sb(ph, f"aQT{i}", [128, L], BF16) for i in range(2)]
        with ExitStack() as ph2:
            RAW = kb.sb(ph2, "aRAW", [128, L])
            SQ = kb.sb(ph2, "aSQ", [128, 512]); R = kb.sb(ph2, "aR", [128, 512]); XN = kb.sb(ph2, "aXN", [128, 512])
            XNb = kb.sb(ph2, "aXNb", [128, 512], BF16); T1 = kb.sb(ph2, "aT1", [128, 512]); T2 = kb.sb(ph2, "aT2", [128, 512])
            for which in range(4):
                isq = which >= 2
                c = which % 2
                dst = QT[c] if isq else KT[c]
                g = gq if isq else gk
                if isq:
                    ld(P, RAW, RAW[:, :], projT, projT.ap[1536 + c * 128:1536 + (c + 1) * 128, :])
                else:
                    for hh in range(2):
                        ld(P, RAW, RAW[hh * 64:(hh + 1) * 64, :], projT, projT.ap[1792 + c * 64:1792 + (c + 1) * 64, :])
                for tc in range(8):
                    sl = slice(tc * 512, (tc + 1) * 512)
                    ps = kb.psum[tc % 2]; pr = kb.psum[2 + tc % 2]
                    P.op("act", lambda e: e.activation(out=SQ[:, :], in_=RAW[:, sl], func=AF.Square), [RAW], [SQ])
                    P.op("pe", lambda e: e.matmul(ps[:, :], lhsT=blk[:, :], rhs=SQ[:, :], start=True, stop=True), [blk, SQ], [ps])
                    P.op("dve", lambda e: e.tensor_scalar(out=R[:, :], in0=ps[:, :], scalar1=1e-6, scalar2=None, op0=ALU.add), [ps], [R])
                    P.op("act", lambda e: e.activation(out=R[:, :], in_=R[:, :], func=AF.Sqrt), [R], [R])
                    P.op("dve", lambda e: e.reciprocal(out=R[:, :], in_=R[:, :]), [R], [R])
                    P.op("dve", lambda e: e.scalar_tensor_tensor(out=XN[:, :], in0=RAW[:, sl], scalar=g[:, 0:1], in1=R[:, :], op0=ALU.mult, op1=ALU.mult), [RAW, g, R], [XN])
                    P.op("act", lambda e: e.copy(out=XNb[:, :], in_=XN[:, :]), [XN], [XNb])
                    P.op("pe", lambda e: e.matmul(pr[:, :], lhsT=Rm[:, :], rhs=XNb[:, :], start=True, stop=True), [Rm, XNb], [pr])
                    P.op("pool", lambda e: e.tensor_tensor(out=T1[:, :], in0=XN[:, :], in1=COS[:, sl], op=ALU.mult), [XN, COS], [T1])
                    P.op("dve", lambda e: e.tensor_tensor(out=T2[:, :], in0=pr[:, :], in1=SIN[:, sl], op=ALU.mult), [pr, SIN], [T2])
                    P.op("pool", lambda e: e.tensor_tensor(out=dst[:, sl], in0=T1[:, :], in1=T2[:, :], op=ALU.add), [T1, T2], [dst])
            P.barrier()
        KTz = [[kb.sb(ph, f"aKTz{c}{g}", [128, L], BF16) for g in range(2)] for c in range(2)]
        for c in range(2):
            for g in range(2):
                P.op("pool", lambda e, c=c, g=g: e.memset(KTz[c][g][:, :], 0.0), [], [KTz[c][g]])
                P.op("pool", lambda e, c=c, g=g: e.tensor_copy(out=KTz[c][g][g * 64:(g + 1) * 64, :], in_=KT[c][g * 64:(g + 1) * 64, :]), [KT[c]], [KTz[c][g]])
        dbg_dump(C, "KT0", KT[0], KT[0][:, :], 128, L)
        dbg_dump(C, "QT0", QT[0], QT[0][:, :], 128, L)
        VE = [kb.sb(ph, f"aVE{i}", [128, NT, 65], BF16) for i in range(2)]
        VO = [kb.sb(ph, f"aVO{i}", [128, NT, 128], BF16) for i in range(2)]
        ATT = kb.sb(ph, "aATT", [128, 2, L])
        with ExitStack() as ph2:
            VR = kb.sb(ph2, "aVR", [128, NT, 128])
            ld(P, VR, VR[:, :, :], C["v_tok"], C["v_tok"].ap.ap().rearrange("(t p) c -> p t c", p=128))
            for kv in range(2):
                P.op("pool", lambda e, kv=kv: e.memset(VE[kv][:, :, :], 1.0), [], [VE[kv]])
                P.op("pool", lambda e, kv=kv: e.memset(VO[kv][:, :, :], 0.0), [], [VO[kv]])
                P.op("pool", lambda e, kv=kv: e.memset(VO[kv][:, :, 0:1], 1.0), [], [VO[kv]])
                P.op("dve", lambda e, kv=kv: e.tensor_copy(out=VE[kv][:, :, 0:64], in_=VR[:, :, kv * 64:(kv + 1) * 64]), [VR], [VE[kv]])
                P.op("dve", lambda e, kv=kv: e.tensor_copy(out=VO[kv][:, :, 64:128], in_=VR[:, :, kv * 64:(kv + 1) * 64]), [VR], [VO[kv]])
            P.barrier()
        PT = [kb.sb(ph, f"aPT{i}", [128, 512], BF16) for i in range(4)]
        RS = [kb.sb(ph, f"aRS{i}", [128, 512]) for i in range(2)]; BC = [kb.sb(ph, f"aBC{i}", [128, 512]) for i in range(2)]
        steps = [(h, qc, s_) for h in range(4) for qc in range(8) for s_ in range(NT)]
        LOOK = 3

        def s_issue(i):
            h, qc, s_ = steps[i]
            kv, g = h // 2, h % 2
            lo, hi = g * 64, (g + 1) * 64
            qs = slice(qc * 512, (qc + 1) * 512)
            pss = kb.psum[i % 4]; pt = PT[i % 4]
            P.op("pe", lambda e: e.matmul(pss[:, :], lhsT=KTz[kv][g][:, s_ * 128:(s_ + 1) * 128], rhs=QT[kv][:, qs], start=True, stop=True), [KTz[kv][g], QT[kv]], [pss])
            P.op("act", lambda e: e.activation(out=pt[:, :], in_=pss[:, :], func=AF.Exp), [pss], [pt])

        def pv_issue(i):
            h, qc, s_ = steps[i]
            kv, g = h // 2, h % 2
            lo, hi = g * 64, (g + 1) * 64
            qs = slice(qc * 512, (qc + 1) * 512)
            n = h * 8 + qc
            acc = kb.psum[4 + (n % 2)]; pt = PT[i % 4]
            if g == 0:
                P.op("pe", lambda e: e.matmul(acc[0:65, :], lhsT=VE[kv][:, s_, :], rhs=pt[:, :], start=(s_ == 0), stop=(s_ == NT - 1)), [VE[kv], pt], [acc])
            else:
                P.op("pe", lambda e: e.matmul(acc[:, :], lhsT=VO[kv][:, s_, :], rhs=pt[:, :], start=(s_ == 0), stop=(s_ == NT - 1)), [VO[kv], pt], [acc])
            if s_ == NT - 1:
                p0 = 64 if g == 0 else 0
                rs, bc = RS[n % 2], BC[n % 2]
                P.op("dve", lambda e: e.reciprocal(out=rs[p0:p0 + 1, :], in_=acc[p0:p0 + 1, :]), [acc], [rs])
                pb = kb.psum[6 + (n % 2)]
                P.op("pe", lambda e: e.matmul(pb[:, :], lhsT=onesrow[p0:p0 + 1, :], rhs=rs[p0:p0 + 1, :], start=True, stop=True), [onesrow, rs], [pb])
                P.op("dve", lambda e: e.tensor_copy(out=bc[lo:hi, :], in_=pb[lo:hi, :]), [pb], [bc])
                P.op("dve", lambda e: e.tensor_tensor(out=ATT[lo:hi, kv, qs], in0=acc[lo:hi, :], in1=bc[lo:hi, :], op=ALU.mult), [acc, bc], [ATT])

        for i in range(len(steps) + LOOK):
            if i < len(steps):
                s_issue(i)
            if i - LOOK >= 0:
                pv_issue(i - LOOK)
        dbg_dump(C, "ATT0", ATT, ATT[:, 0, :], 128, L)
        scr = rms_scratch(kb, ph)
        for tc in range(8):
            sl = slice(tc * 512, (tc + 1) * 512)
            group_rms(C, scr, [(ATT, ATT[:, 0, sl]), (ATT, ATT[:, 1, sl])], gain, 6, 768, tc)
        P.barrier()


def build(stop_after=None, dbg=None):
    nc = bass.Bass("TRN2", target_bir_lowering=False)
    st = ExitStack()
    with st:
        P = Prog(nc, st)
        kb = KB(nc, st, P)
        x_d = kb.dram_in("x", [L, D])
        ln_in_g = kb.dram_in("ln_in_g", [1, D]); ln_in_b = kb.dram_in("ln_in_b", [1, D])
        w_in = kb.dram_in("w_in", [DEPTH, D, 2048])
        ident_d = kb.dram_in("ident", [128, 128])
        EXT = {}
        for nm, shp in [("conv_dw_w", [DEPTH, 31, 256]), ("conv_dw_b", [DEPTH, 256]), ("conv_ln_g", [DEPTH, 256]), ("conv_ln_b", [DEPTH, 256]),
                        ("mix_norm_g", [DEPTH, 1024]), ("q_norm_g", [DEPTH, 64]), ("k_norm_g", [DEPTH, 64]),
                        ("s5_lam_re", [DEPTH, 2, 1024]), ("s5_lam_im", [DEPTH, 2, 1024]), ("s5_log_step", [DEPTH, 2, 16]),
                        ("s5_b_re", [DEPTH, 2, 16, 64, 16]), ("s5_b_im", [DEPTH, 2, 16, 64, 16]),
                        ("s5_c_re", [DEPTH, 2, 16, 16, 64]), ("s5_c_im", [DEPTH, 2, 16, 16, 64]),
                        ("s5_d", [DEPTH, 256]), ("s5_w_glu", [DEPTH, 256, 512]), ("s5_b_glu", [DEPTH, 512]), ("iota512", [128, 512]),
                        ("hy_short_w", [DEPTH, 3, 768]), ("hy_short_b", [DEPTH, 768]), ("hy_f_w1", [DEPTH, 33, 64]), ("hy_f_b1", [DEPTH, 64]),
                        ("hy_f_freq", [DEPTH, 64]), ("hy_f_w2", [DEPTH, 64, 64]), ("hy_f_b2", [DEPTH, 64]), ("hy_f_w3", [DEPTH, 64, 1024]),
                        ("hy_bias", [DEPTH, 2, 256]), ("hy_zT", [33, L]), ("hy_win", [L, 256]), ("hy_ch", [128, 32]), ("hy_sh", [128, 32]), ("hy_sgn", [128, 1]), ("hy_J", [128, 128]),
                        ("w_out", [DEPTH, D, D]), ("ln1_g", [DEPTH, D]), ("ln1_b", [DEPTH, D]), ("ln2_g", [DEPTH, D]), ("ln2_b", [DEPTH, D]),
                        ("router_w", [DEPTH, D, 16]), ("router_b", [DEPTH, 16]),
                        ("exp_w_gate", [DEPTH, 16, D, D]), ("exp_w_up", [DEPTH, 16, D, D]), ("exp_w_down", [DEPTH, 16, D, D]),
                        ("ropeC", [128, L]), ("ropeS", [128, L]), ("blk64", [128, 128])]:
            EXT[nm] = kb.dram_in(nm, shp)
        EXT["Rm"] = kb.dram_in("Rm", [128, 128], BF16)
        EXT["dftC"] = kb.dram_in("dftC", [32, 128, 32 * 128], BF16)
        EXT["dftS"] = kb.dram_in("dftS", [32, 128, 32 * 128], BF16)
        EXT["dftC0"] = kb.dram_in("dftC0", [32, 128, 32 * 128], BF16)
        EXT["dftS0"] = kb.dram_in("dftS0", [32, 128, 32 * 128], BF16)
        h1_tok = kb.dram("h1_tok", [L, D]); moe_acc = kb.dram("moe_acc", [L, D])
        C_hy_tok = kb.dram("hy_tok", [L, 768]); C_z1_tok = kb.dram("z1_tok", [L, 256]); C_k_d = kb.dram("hy_kd", [32, 128, 2, 512])
        mergedT = kb.dram("mergedT", [D, L], BF16)
        out_d = kb.dram("out", [L, D], F32, kind="ExternalOutput")
        h_tok = kb.dram("h_tok", [L, D])
        hT_d = kb.dram("hT_d", [D, L], BF16)
        projT = kb.dram("projT", [2048, L])
        v_tok = kb.dram("v_tok", [L, 128])
        dbg_d = None
        if dbg is not None:
            dbg_d = kb.dram("dbg", dbg[1], F32, kind="ExternalOutput")

        glob = ExitStack()
        st.enter_context(glob)
        ident = kb.sb(glob, "ident", [128, 128])
        ld(P, ident, ident[:, :], ident_d, ident_d.ap.ap())
        ones256 = kb.sb(glob, "ones256", [128, 128])
        P.op("pool", lambda e: e.memset(ones256[:, :], 1.0 / 256.0), [], [ones256])
        C = dict(dbgname=(dbg[0] if dbg else None), dbg_d=dbg_d, nc=nc, P=P, kb=kb, EXT=EXT, projT=projT, mergedT=mergedT, ident=ident, ones256=ones256, v_tok=v_tok, h_tok=h_tok, hT_d=hT_d, h1_tok=h1_tok, moe_acc=moe_acc, out_d=out_d, hy_tok=C_hy_tok, z1_tok=C_z1_tok, k_d=C_k_d)

        with ExitStack() as ph:
            G = kb.sb(ph, "G", [128, D]); Bt = kb.sb(ph, "Bt", [128, D])
            ld(P, G, G[:, :], ln_in_g, ln_in_g.ap.ap().partition_broadcast(128))
            ld(P, Bt, Bt[:, :], ln_in_b, ln_in_b.ap.ap().partition_broadcast(128))
            Xs = [kb.sb(ph, f"X{i}", [128, D]) for i in range(5)]
            stg = [kb.sb(ph, f"stg{i}", [128, 8, 128], BF16) for i in range(2)]
            scr = ln_scr(kb, ph, 4)

            def a0(ti):
                X = Xs[ti % 5]
                ld(P, X, X[:, :], x_d, x_d.ap[ti * 128:(ti + 1) * 128, :])

            def a1(ti):
                ln_a(P, Xs[ti % 5], scr[ti % 4])

            def a2(ti):
                ln_b(P, Xs[ti % 5], G, Bt, scr[ti % 4])

            def a3(ti):
                X = Xs[ti % 5]
                P.dma("sp", lambda e: e.dma_start(out=h_tok.ap[ti * 128:(ti + 1) * 128, :], in_=X[:, :]), [X], [h_tok], waw=False)
                transpose_to_hT(P, kb, X, ident, stg[ti % 2], hT_d, ti)

            pipeline([a0, a1, a2, a3], NT)
            P.barrier()
        if stop_after == "A":
            return finish(nc, P, kb, dbg, dbg_d, locals())

        for l in range(DEPTH):
            with ExitStack() as ph:
                W = kb.sb(ph, "W", [128, 8, 2048], BF16)
                for k in range(8):
                    P.dma("pool", lambda e, k=k: e.dma_start(out=W[:, k, :], in_=w_in.ap[l, k * 128:(k + 1) * 128, :]), [w_in], [W], waw=False)
                HT = [kb.sb(ph, f"HT{i}", [128, 8, 512], BF16) for i in range(2)]
                OS = [kb.sb(ph, f"OS{i}", [128, 512]) for i in range(4)]
                VS = [kb.sb(ph, f"VS{i}", [128, 128]) for i in range(2)]
                n = 0
                for tc in range(8):
                    H = HT[tc % 2]
                    ld(P, H, H[:, :, :], hT_d, hT_d.ap.ap().rearrange("(k p) t -> p k t", p=128)[:, :, tc * 512:(tc + 1) * 512])
                    for fc in range(16):
                        if fc == 15:
                            continue
                        ps = kb.psum[n % 4]; O = OS[n % 4]; n += 1
                        for k in range(8):
                            P.op("pe", lambda e, k=k, fc=fc, ps=ps, H=H: e.matmul(ps[:, :], lhsT=W[:, k, fc * 128:(fc + 1) * 128], rhs=H[:, k, :], start=(k == 0), stop=(k == 7)), [W, H], [ps])
                        copy_on(P, kb.alt(), O, O[:, :], ps, ps[:, :])
                        P.dma("sp", lambda e, O=O, fc=fc, tc=tc: e.dma_start(out=projT.ap[fc * 128:(fc + 1) * 128, tc * 512:(tc + 1) * 512], in_=O[:, :]), [O], [projT], waw=False)
                    for tt in range(4):
                        ps = kb.psum[4 + (tt % 2)]; V = VS[tt % 2]
                        for k in range(8):
                            P.op("pe", lambda e, k=k, tt=tt, ps=ps, H=H: e.matmul(ps[:, 0:128], lhsT=H[:, k, tt * 128:(tt + 1) * 128], rhs=W[:, k, 1920:2048], start=(k == 0), stop=(k == 7)), [W, H], [ps])
                        copy_on(P, kb.alt(), V, V[:, :], ps, ps[:, 0:128])
                        ti = tc * 4 + tt
                        P.dma("sp", lambda e, V=V, ti=ti: e.dma_start(out=v_tok.ap[ti * 128:(ti + 1) * 128, :], in_=V[:, :]), [V], [v_tok], waw=False)
                P.barrier()
            if stop_after == f"B{l}":
                return finish(nc, P, kb, dbg, dbg_d, locals())
            phase_conv(C, l)
            if stop_after == f"C1{l}":
                return finish(nc, P, kb, dbg, dbg_d, locals())
            if not os.environ.get("SKIP_ATTN"):
                phase_attn(C, l)
            if not os.environ.get("SKIP_S5"):
                phase_s5(C, l)
            if not os.environ.get("SKIP_HY"):
                phase_hyena(C, l)
            if stop_after == f"C2{l}":
                return finish(nc, P, kb, dbg, dbg_d, locals())
            phase_moe(C, l, stop_after)
            if stop_after in (f"D{l}", f"E{l}", f"F{l}", f"G{l}"):
                return finish(nc, P, kb, dbg, dbg_d, locals())
            if stop_after == f"C3{l}":
                return finish(nc, P, kb, dbg, dbg_d, locals())
            if stop_after == f"C4{l}":
                return finish(nc, P, kb, dbg, dbg_d, locals())
        return finish(nc, P, kb, dbg, dbg_d, locals())


def finish(nc, P, kb, dbg, dbg_d, env):
    if dbg is not None and dbg[0] in env:
        src = env[dbg[0]]
        P.barrier()
        rows = dbg[1][0]
        with ExitStack() as ph:
            T = kb.sb(ph, "dbgT", [128, dbg[1][1]], src.ap.dtype if hasattr(src.ap, "dtype") else F32)
            T2 = kb.sb(ph, "dbgT2", [128, dbg[1][1]])
            for r in range(0, rows, 128):
                ld(P, T, T[:, :], src, src.ap[r:r + 128, :])
                P.op("dve", lambda e: e.tensor_copy(out=T2[:, :], in_=T[:, :]), [T], [T2])
                P.dma("sp", lambda e, r=r: e.dma_start(out=dbg_d.ap[r:r + 128, :], in_=T2[:, :]), [T2], [dbg_d], waw=False)
            P.barrier()
    P.barrier(engines=["sp"])
    print("ninstr", P.ninstr, "nsem", P.nsem)
    return nc


def host_consts():
    c = {}
    c["ident"] = np.eye(128, dtype=np.float32)
    t = np.arange(L)
    row = (t // 64).astype(np.float32); col = (t % 64).astype(np.float32)
    inv = (np.float32(10000.0) ** (-np.arange(0, 32, 2, dtype=np.float32) / np.float32(32))).astype(np.float32)
    ang = np.zeros((64, L), np.float32)
    for d in range(64):
        i = d % 16
        ang[d] = (row if d < 32 else col) * inv[i]
    c["ropeC"] = np.ascontiguousarray(np.tile(np.cos(ang), (2, 1)).astype(np.float32))
    c["ropeS"] = np.ascontiguousarray(np.tile(np.sin(ang), (2, 1)).astype(np.float32))
    Rm = np.zeros((128, 128), np.float32)
    for m in range(128):
        if m % 32 < 16:
            Rm[m + 16, m] = -1.0
        else:
            Rm[m - 16, m] = 1.0
    c["Rm"] = Rm.astype(ml_dtypes.bfloat16)
    blk = np.zeros((128, 128), np.float32); blk[:64, :64] = 1 / 64; blk[64:, 64:] = 1 / 64
    c["blk64"] = blk
    Nf = 2 * L
    n = np.arange(L, dtype=np.float64)
    ang = 2.0 * np.pi * np.outer(n + 0.5, n + 0.5) / Nf
    for nm, fn in (("dftC", np.cos), ("dftS", np.sin)):
        M = fn(ang).astype(np.float32).astype(ml_dtypes.bfloat16)
        c[nm] = np.ascontiguousarray(M.reshape(32, 128, 32, 128).transpose(2, 1, 0, 3)).reshape(32, 128, 32 * 128)
    ang0 = 2.0 * np.pi * np.outer(n, n + 0.5) / Nf
    for nm, fn in (("dftC0", np.cos), ("dftS0", np.sin)):
        M = fn(ang0).astype(np.float32).astype(ml_dtypes.bfloat16)
        c[nm] = np.ascontiguousarray(M.reshape(32, 128, 32, 128).transpose(2, 1, 0, 3)).reshape(32, 128, 32 * 128)
    del ang, ang0
    c["hy_sgn"] = np.ascontiguousarray(((-1.0) ** np.arange(128)).astype(np.float32).reshape(128, 1))
    c["hy_J"] = np.ascontiguousarray(np.eye(128, dtype=np.float32)[::-1])
    phi_half = np.pi * (n + 0.5) / Nf
    c["hy_ch"] = np.ascontiguousarray(np.cos(phi_half).reshape(32, 128).T.astype(np.float32))
    c["hy_sh"] = np.ascontiguousarray(np.sin(phi_half).reshape(32, 128).T.astype(np.float32))
    t = np.linspace(0.0, 1.0, L, dtype=np.float32)[:, None]
    w = (np.float32(2.0 * math.pi / L)) * np.arange(L, dtype=np.float32)
    f = np.linspace(1e-4, 15, 16, dtype=np.float32)
    angz = w[:, None] * f[None, :]
    z = np.concatenate([t, np.cos(angz), -np.sin(angz)], axis=-1).astype(np.float32)
    c["hy_zT"] = np.ascontiguousarray(z.T)
    max_decay = math.log(1e-2) / 0.3; min_decay = math.log(1e-2) / 1.5
    deltas = np.linspace(min_decay, max_decay, 256, dtype=np.float32)
    c["hy_win"] = np.ascontiguousarray((np.exp(-t * np.abs(deltas)[None, :]) + np.float32(0.05)).astype(np.float32))
    c["iota512"] = np.ascontiguousarray(np.tile(np.arange(512, dtype=np.float32), (128, 1)))
    return c


def make_in_maps(inputs, cores):
    c = host_consts()
    maps = []
    for b in cores:
        m = dict(c)
        m["x"] = np.ascontiguousarray(inputs["x"][b])
        m["ln_in_g"] = np.ascontiguousarray(inputs["ln_in_g"]).reshape(1, D)
        m["ln_in_b"] = np.ascontiguousarray(inputs["ln_in_b"]).reshape(1, D)
        m["w_in"] = np.ascontiguousarray(inputs["w_in"])
        for nm in ["conv_dw_w", "conv_dw_b", "conv_ln_g", "conv_ln_b", "mix_norm_g", "q_norm_g", "k_norm_g", "s5_log_step",
                   "s5_b_re", "s5_b_im", "s5_c_re", "s5_c_im", "s5_d", "s5_w_glu", "s5_b_glu",
                   "hy_short_w", "hy_short_b", "hy_f_w1", "hy_f_b1", "hy_f_freq", "hy_f_w2", "hy_f_b2", "hy_f_w3", "hy_bias",
                   "w_out", "ln1_g", "ln1_b", "ln2_g", "ln2_b", "router_w", "router_b", "exp_w_gate", "exp_w_up", "exp_w_down"]:
            m[nm] = np.ascontiguousarray(inputs[nm])
        m["s5_lam_re"] = np.ascontiguousarray(inputs["s5_lam_re"]).reshape(DEPTH, 2, 1024)
        m["s5_lam_im"] = np.ascontiguousarray(inputs["s5_lam_im"]).reshape(DEPTH, 2, 1024)
        maps.append(m)
    return maps


def kernel(**inputs):
    nc = build()
    in_maps = make_in_maps(inputs, list(range(8)))
    res = run_bass_kernel_spmd(nc, in_maps, core_ids=list(range(8)))
    return np.stack([r["out"] for r in res.results], axis=0)
```
